# Optimizing a Trainium2 kernel written in Bass

```python
import math
import jax, jax.numpy as jnp
from jax import lax
import numpy as np

D_MODEL = 2048
BATCH = 2
SEQ = 16384
DEPTH = 2

CHUNK = 64
Q_BLOCK = 128
N_A_LAYERS = DEPTH // 2
N_B_LAYERS = DEPTH - N_A_LAYERS
HEAD_DIM = 128
DIFF_HEADS = D_MODEL // (2 * HEAD_DIM)
DIFF_VDIM = 2 * HEAD_DIM
FOX_HEADS = D_MODEL // HEAD_DIM
D_FF = 4 * D_MODEL
REL_BUCKETS = 32
REL_MAX_DIST = 128
NORM_EPS = 1e-6
SUBLN_EPS = 1e-5
NEG = -1e30

kernel_name = "yoco_diffattn_fox_hybrid"


def rmsnorm(x, g, eps=NORM_EPS):
    xf = x.astype(jnp.float32)
    y = xf * lax.rsqrt(jnp.mean(xf * xf, axis=-1, keepdims=True) + eps)
    return (y * g.astype(jnp.float32)).astype(x.dtype)


def t5_bucket(rel):
    half = REL_BUCKETS // 2
    max_exact = half // 2
    ret = jnp.where(rel > 0, half, 0)
    n = jnp.abs(rel)
    nf = jnp.maximum(n, 1).astype(jnp.float32)
    large = max_exact + (jnp.log(nf / max_exact) / math.log(REL_MAX_DIST / max_exact)
                         * (half - max_exact)).astype(jnp.int32)
    large = jnp.minimum(large, half - 1)
    return ret + jnp.where(n < max_exact, n, large)


def diff_attention(h, w_qkv, lq1, lk1, lq2, lk2, subln_g, rel_table, lambda_init):
    B, S, _ = h.shape
    nb = S // Q_BLOCK
    q, k, v = jnp.split(h @ w_qkv, 3, axis=-1)
    q = q.reshape(B, nb, Q_BLOCK, DIFF_HEADS, 2, HEAD_DIM).transpose(1, 0, 3, 4, 2, 5)
    k = k.reshape(B, S, DIFF_HEADS, 2, HEAD_DIM).transpose(0, 2, 3, 1, 4)
    v = v.reshape(B, S, DIFF_HEADS, DIFF_VDIM).transpose(0, 2, 1, 3)
    f32 = jnp.float32
    lam = (jnp.exp(jnp.sum(lq1.astype(f32) * lk1.astype(f32)))
           - jnp.exp(jnp.sum(lq2.astype(f32) * lk2.astype(f32))) + lambda_init)
    key_pos = jnp.arange(S, dtype=jnp.int32)
    scale = HEAD_DIM ** -0.5

    def block(args):
        qi, bi = args
        q_pos = bi * Q_BLOCK + jnp.arange(Q_BLOCK, dtype=jnp.int32)
        s = jnp.einsum('bhcqd,bhckd->bhcqk', qi, k).astype(f32) * scale
        rel = key_pos[None, :] - q_pos[:, None]
        bias = jnp.transpose(rel_table[t5_bucket(rel)], (2, 0, 1)).astype(f32)
        mask = (key_pos[None, :] // CHUNK) <= (q_pos[:, None] // CHUNK)
        s = jnp.where(mask, s + bias[None, :, None], NEG)
        p = jax.nn.softmax(s, axis=-1)
        a = p[:, :, 0] - lam * p[:, :, 1]
        return jnp.einsum('bhqk,bhkv->bhqv', a.astype(v.dtype), v)

    o = lax.map(block, (q, jnp.arange(nb, dtype=jnp.int32)))
    o = rmsnorm(o, subln_g, SUBLN_EPS) * (1.0 - lambda_init)
    return o.transpose(1, 0, 3, 2, 4).reshape(B, S, D_MODEL)


def shared_kv(h, g, w_k, w_v, w_f, b_f):
    B, S, _ = h.shape
    hn = rmsnorm(h, g)
    k = (hn @ w_k).reshape(B, S, FOX_HEADS, HEAD_DIM).transpose(0, 2, 1, 3)
    v = (hn @ w_v).reshape(B, S, FOX_HEADS, HEAD_DIM).transpose(0, 2, 1, 3)
    log_f = jax.nn.log_sigmoid((hn @ w_f + b_f).astype(jnp.float32))
    c = jnp.cumsum(log_f, axis=1).transpose(0, 2, 1)
    return k, v, c


def forgetting_attention(h, w_q, k, v, c):
    B, S, _ = h.shape
    nb = S // Q_BLOCK
    q = (h @ w_q).reshape(B, nb, Q_BLOCK, FOX_HEADS, HEAD_DIM).transpose(1, 0, 3, 2, 4)
    cq = c.reshape(B, FOX_HEADS, nb, Q_BLOCK).transpose(2, 0, 1, 3)
    key_pos = jnp.arange(S, dtype=jnp.int32)
    scale = HEAD_DIM ** -0.5

    def block(args):
        qi, ci, bi = args
        q_pos = bi * Q_BLOCK + jnp.arange(Q_BLOCK, dtype=jnp.int32)
        s = jnp.einsum('bhqd,bhkd->bhqk', qi, k).astype(jnp.float32) * scale
        mask = key_pos[None, :] <= q_pos[:, None]
        s = jnp.where(mask, s + ci[..., None] - c[:, :, None, :], NEG)
        p = jax.nn.softmax(s, axis=-1)
        return jnp.einsum('bhqk,bhkd->bhqd', p.astype(v.dtype), v)

    o = lax.map(block, (q, cq, jnp.arange(nb, dtype=jnp.int32)))
    return o.transpose(1, 0, 3, 2, 4).reshape(B, S, D_MODEL)


def sq_relu_mlp(h, w_in, w_out):
    return jnp.square(jax.nn.relu(h @ w_in)) @ w_out


def setup_inputs(seed: int = 0) -> dict:
    key = jax.random.key(seed)
    ks = jax.random.split(key, 24)
    D = D_MODEL
    nrm = jax.random.normal
    f32 = jnp.float32
    return {
        "x": nrm(ks[0], (BATCH, SEQ, D), f32),
        "rel_bias_table": 0.5 * nrm(ks[1], (REL_BUCKETS, DIFF_HEADS), f32),
        "attn_norm_g": 1.0 + 0.02 * nrm(ks[2], (DEPTH, D), f32),
        "mlp_norm_g": 1.0 + 0.02 * nrm(ks[3], (DEPTH, D), f32),
        "w_qkv_a": nrm(ks[4], (N_A_LAYERS, D, 3 * D), f32) * D ** -0.5,
        "lam_q1": 0.1 * nrm(ks[5], (N_A_LAYERS, HEAD_DIM), f32),
        "lam_k1": 0.1 * nrm(ks[6], (N_A_LAYERS, HEAD_DIM), f32),
        "lam_q2": 0.1 * nrm(ks[7], (N_A_LAYERS, HEAD_DIM), f32),
        "lam_k2": 0.1 * nrm(ks[8], (N_A_LAYERS, HEAD_DIM), f32),
        "subln_g": 1.0 + 0.02 * nrm(ks[9], (N_A_LAYERS, DIFF_VDIM), f32),
        "w_o_a": nrm(ks[10], (N_A_LAYERS, D, D), f32) * D ** -0.5,
        "kv_norm_g": 1.0 + 0.02 * nrm(ks[11], (D,), f32),
        "w_k_b": nrm(ks[12], (D, D), f32) * D ** -0.5,
        "w_v_b": nrm(ks[13], (D, D), f32) * D ** -0.5,
        "w_f_b": nrm(ks[14], (D, FOX_HEADS), f32) * D ** -0.5,
        "b_f_b": 2.0 + 0.1 * nrm(ks[15], (FOX_HEADS,), f32),
        "w_q_b": nrm(ks[16], (N_B_LAYERS, D, D), f32) * D ** -0.5,
        "w_o_b": nrm(ks[17], (N_B_LAYERS, D, D), f32) * D ** -0.5,
        "w_mlp_in": nrm(ks[18], (DEPTH, D, D_FF), f32) * D ** -0.5,
        "w_mlp_out": nrm(ks[19], (DEPTH, D_FF, D), f32) * D_FF ** -0.5,
        "final_norm_g": 1.0 + 0.02 * nrm(ks[20], (D,), f32),
    }


def reference(x, rel_bias_table, attn_norm_g, mlp_norm_g, w_qkv_a, lam_q1, lam_k1,
              lam_q2, lam_k2, subln_g, w_o_a, kv_norm_g, w_k_b, w_v_b, w_f_b, b_f_b,
              w_q_b, w_o_b, w_mlp_in, w_mlp_out, final_norm_g):
    h = x
    k_sh = v_sh = c_sh = None
    for layer in range(DEPTH):
        hn = rmsnorm(h, attn_norm_g[layer])
        if layer < N_A_LAYERS:
            i = layer
            lambda_init = 0.8 - 0.6 * math.exp(-0.3 * layer)
            o = diff_attention(hn, w_qkv_a[i], lam_q1[i], lam_k1[i], lam_q2[i], lam_k2[i],
                               subln_g[i], rel_bias_table, lambda_init)
            h = h + o @ w_o_a[i]
        else:
            j = layer - N_A_LAYERS
            if j == 0:
                k_sh, v_sh, c_sh = shared_kv(h, kv_norm_g, w_k_b, w_v_b, w_f_b, b_f_b)
            o = forgetting_attention(hn, w_q_b[j], k_sh, v_sh, c_sh)
            h = h + o @ w_o_b[j]
        hn = rmsnorm(h, mlp_norm_g[layer])
        h = h + sq_relu_mlp(hn, w_mlp_in[layer], w_mlp_out[layer])
    return rmsnorm(h, final_norm_g)
```

```python
import math
from contextlib import ExitStack

import numpy as np
import ml_dtypes
import concourse.bass as bass
import concourse.mybir as mybir
from concourse.bass_utils import run_bass_kernel_spmd

F32 = mybir.dt.float32
BF16 = mybir.dt.bfloat16
AF = mybir.ActivationFunctionType
ALU = mybir.AluOpType
AX = mybir.AxisListType

D = 2048
S = 16384
NB = 2
DFF = 8192
NCORE = 8
TOK = 4096
CH = 2048
NEG = -30000.0
SCALE = 128 ** -0.5
EPS = 1e-6
SUBEPS = 1e-5
LAMBDA_INIT = 0.8 - 0.6 * math.exp(-0.3 * 0)


class Tk:
    __slots__ = ("name", "w", "r", "semcnt")

    def __init__(self, name):
        self.name = name
        self.w = None
        self.r = []
        self.semcnt = 0


class Op:
    __slots__ = ("eng", "fn", "deps", "needed", "is_dma", "tk", "val", "key", "inc", "nosig")


ENGS = ("pe", "act", "dve", "pool", "sp")
BLK = {"pe": "tensor", "act": "scalar", "dve": "vector", "pool": "gpsimd", "sp": "sync"}


class Sched:
    def __init__(self, nc, gsem=None, tag=""):
        self.nc = nc
        self.gsem = gsem
        self.tag = tag
        self.ops = {e: [] for e in ENGS}
        self.all = []

    def op(self, eng, fn, reads=(), writes=(), dma=None, inc=16):
        o = Op()
        o.inc = inc
        o.nosig = False
        o.eng = eng
        o.fn = fn
        o.needed = False
        o.is_dma = dma is not None
        o.tk = dma
        o.val = 0
        o.key = None
        deps = []
        for t in reads:
            if t.w is not None:
                deps.append(t.w)
        for t in writes:
            if t.w is not None:
                deps.append(t.w)
            deps.extend(t.r)
        o.deps = deps
        for d in deps:
            d.needed = True
        for t in reads:
            t.r.append(o)
        for t in writes:
            t.w = o
            t.r = []
        self.ops[eng].append(o)
        self.all.append(o)
        return o

    def emit(self):
        nc = self.nc
        engcnt = {e: 0 for e in ENGS}
        keys = {}
        for o in self.all:
            if o.is_dma:
                o.tk.semcnt += o.inc
                o.val = o.tk.semcnt
                o.key = ("t", id(o.tk))
                keys[o.key] = "d_" + o.tk.name
            elif o.needed:
                engcnt[o.eng] += 1
                o.val = engcnt[o.eng]
                o.key = ("e", o.eng)
                keys[o.key] = "e_" + o.eng
        if self.gsem is not None:
            for e in ENGS:
                for o in reversed(self.ops[e]):
                    if not o.is_dma and not o.nosig:
                        if not o.needed:
                            o.needed = True
                            engcnt[e] += 1
                            o.val = engcnt[e]
                            o.key = ("e", e)
                            keys[o.key] = "e_" + e
                        break
        final = {}
        for o in self.all:
            if o.key is not None:
                final[o.key] = max(final.get(o.key, 0), o.val)
        with ExitStack() as es:
            sems, base = {}, {}
            for k, n in keys.items():
                if self.gsem is None:
                    sems[k], base[k] = es.enter_context(nc.semaphore(self.tag + n)), 0
                elif self.gsem.free:
                    sems[k], base[k] = self.gsem.free.pop()
                else:
                    sems[k], base[k] = self.gsem.stack.enter_context(nc.semaphore(self.tag + n)), 0
            block = es.enter_context(nc.Block())
            for en in ENGS:
                def body(eng, en=en):
                    waited = {}
                    for o in self.ops[en]:
                        for d in o.deps:
                            if en == "pe" and d.eng == "pe" and not d.is_dma:
                                continue
                            if waited.get(d.key, 0) >= d.val:
                                continue
                            eng.wait_ge(sems[d.key], base[d.key] + d.val)
                            waited[d.key] = d.val
                        ins = o.fn(eng)
                        if o.is_dma:
                            ins.then_inc(sems[o.key], o.inc)
                        elif o.needed:
                            ins.then_inc(sems[o.key], 1)
                    if en == "sp" or self.gsem is not None:
                        for k, v in final.items():
                            if waited.get(k, 0) < v:
                                eng.wait_ge(sems[k], base[k] + v)
                getattr(block, BLK[en])(body)
            if self.gsem is not None:
                for k in keys:
                    self.gsem.free.append((sems[k], base[k] + final[k]))


class Ctx:
    def __init__(self, nc, es, gsem=None, tag=""):
        self.nc = nc
        self.es = es
        self.tag = tag
        self.s = Sched(nc, gsem, tag)
        self.rr = 0

    def sb(self, name, shape, dt):
        t = self.es.enter_context(self.nc.sbuf_tensor("sb_" + self.tag + name, list(shape), dt))
        return t, Tk(name)

    def ps(self, name, shape, dt=F32):
        t = self.es.enter_context(self.nc.psum_tensor("ps_" + self.tag + name, list(shape), dt))
        return t, Tk(name)

    def dma(self, q, out, in_, tk, reads=(), writes=()):
        return self.s.op(q, lambda e: e.dma_start(out=out, in_=in_), reads, writes, dma=tk)

    def mm(self, out, lhsT, rhs, start, stop, reads, writes):
        return self.s.op("pe", lambda e: e.matmul(out, lhsT, rhs, start=start, stop=stop),
                         reads, writes)

    def tr(self, out, in_, ident, reads, writes):
        return self.s.op("pe", lambda e: e.transpose(out, in_, ident), reads, writes)

    def act(self, out, in_, func, reads, writes, bias=0.0, scale=1.0, accum=None):
        if accum is None:
            return self.s.op("act", lambda e: e.activation(out, in_, func, bias=bias, scale=scale),
                             reads, writes)
        return self.s.op("act", lambda e: e.activation(out, in_, func, bias=bias, scale=scale,
                                                       accum_out=accum), reads, writes)

    def copy(self, out, in_, reads, writes, eng=None):
        if eng is None:
            self.rr ^= 1
            eng = "act" if self.rr else "dve"
        if eng == "act":
            return self.s.op("act", lambda e: e.copy(out, in_), reads, writes)
        return self.s.op(eng, lambda e: e.tensor_copy(out, in_), reads, writes)

    def ts(self, out, in0, s1, s2, op0, op1, reads, writes, eng="dve"):
        if op1 is None:
            return self.s.op(eng, lambda e: e.tensor_scalar(out, in0, s1, s2, op0), reads, writes)
        return self.s.op(eng, lambda e: e.tensor_scalar(out, in0, s1, s2, op0, op1), reads, writes)

    def stt(self, out, in0, scalar, in1, op0, op1, reads, writes, eng="dve"):
        return self.s.op(eng, lambda e: e.scalar_tensor_tensor(out, in0, scalar, in1, op0, op1),
                         reads, writes)

    def tt(self, out, in0, in1, op, reads, writes, eng="dve"):
        return self.s.op(eng, lambda e: e.tensor_tensor(out, in0, in1, op), reads, writes)

    def memset(self, ap, val, writes, eng="dve"):
        return self.s.op(eng, lambda e: e.memset(ap, val), (), writes)


class Slots:
    def __init__(self, cx, name, n, shape, dt, psum=False):
        self.items = []
        for i in range(n):
            self.items.append(cx.ps(f"{name}{i}", shape, dt) if psum else cx.sb(f"{name}{i}", shape, dt))
        self.i = 0

    def next(self):
        it = self.items[self.i % len(self.items)]
        self.i += 1
        return it


class RowKit:
    def __init__(self, cx, ident_d, nss=64, wslots=3, hnslots=2):
        self.cx = cx
        self.ident, self.ident_k = cx.sb("ident", [128, 128], BF16)
        cx.dma("sp", self.ident[:], ident_d, self.ident_k, (), (self.ident_k,))
        self.w = Slots(cx, "w", wslots, [128, 8192], BF16)
        self.pacc = Slots(cx, "pacc", 4, [128, 512], F32, psum=True)
        self.ptr = Slots(cx, "ptr", 2, [128, 1024], BF16, psum=True)
        self.hn = Slots(cx, "hn", hnslots, [128, 2048], BF16)
        self.ss, self.ss_k = cx.sb("ss", [128, nss], F32)
        self.rs, self.rs_k = cx.sb("rs", [128, nss], F32)
        cx.memset(self.ss[:], 0.0, (self.ss_k,))
        self.epsb, self.eps_k = cx.sb("epsb", [128, 2], F32)
        cx.memset(self.epsb[:, 0:1], EPS, (self.eps_k,))
        cx.memset(self.epsb[:, 1:2], SUBEPS, (self.eps_k,))
        self.eps_main = self.epsb[:, 0:1]
        self.eps_sub = self.epsb[:, 1:2]
        self.nss = 0
        self.pending = []
        self.per_load = 1

    def drain(self, n=None):
        while self.pending and (n is None or n > 0):
            self.pending.pop(0)()
            if n is not None:
                n -= 1

    def norm_stats(self, x_ap, x_k, junk=None, junk_k=None):
        cx = self.cx
        j = self.nss
        self.nss += 1
        ss = self.ss[:, j:j + 1]
        rs = self.rs[:, j:j + 1]
        if junk is None:
            junk, junk_k = self.hn.next()
        cx.act(junk[:], x_ap, AF.Square, (x_k,), (junk_k, self.ss_k), accum=ss)
        cx.act(rs, ss, AF.Ln, (self.ss_k, self.eps_k), (self.rs_k,), bias=self.eps_main, scale=1.0 / D)
        cx.act(rs, rs, AF.Exp, (self.rs_k,), (self.rs_k,), scale=-0.5)
        return rs

    def norm_T(self, x_ap, x_k, g, g_k, dstT, dstT_k, tcol):
        cx = self.cx
        hn, hn_k = self.hn.next()
        rs = self.norm_stats(x_ap, x_k, hn, hn_k)
        cx.stt(hn[:], x_ap, rs, g[:], ALU.mult, ALU.mult, (x_k, self.rs_k, g_k), (hn_k,))
        self.transpose_in(hn, hn_k, dstT, dstT_k, tcol)

    def transpose_in(self, src, src_k, dstT, dstT_k, tcol, nchunk=16, c0=0):
        cx = self.cx
        for half in range(0, nchunk, 8):
            n = min(8, nchunk - half)
            pt, pt_k = self.ptr.next()
            for i in range(n):
                cx.tr(pt[:, i * 128:(i + 1) * 128], src[:, (half + i) * 128:(half + i + 1) * 128],
                      self.ident[:], (src_k, self.ident_k), (pt_k,))
            cx.copy(dstT[:, c0 + half:c0 + half + n, tcol:tcol + 128],
                    pt[:, 0:n * 128].rearrange("p (c t) -> p c t", t=128), (pt_k,), (dstT_k,))

    def load_w(self, w_d, c0, ncols, nk=16, k0=0):
        cx = self.cx
        wt, wk = self.w.next()
        wv_s = wt[:, 0:nk * ncols].rearrange("p (k n) -> p k n", n=ncols)
        wv = w_d.rearrange("(kc p) n -> p kc n", p=128)
        st = max(1, nk // 4)
        for q in range(0, nk, st):
            cx.dma("pool", wv_s[:, q:q + st, :], wv[:, k0 + q:k0 + q + st, c0:c0 + ncols], wk, (), (wk,))
        self.drain(self.per_load)
        return wv_s, wk

    def lin_fm(self, srcT, srcT_k, ntok, wt, wk, ncols, cb, nk=16):
        cx = self.cx
        for oc in range(ncols // 128):
            for tt in range(ntok // 512):
                pa, pa_k = self.pacc.next()
                for kc in range(nk):
                    cx.mm(pa[:], wt[:, kc, oc * 128:(oc + 1) * 128], srcT[:, kc, tt * 512:(tt + 1) * 512],
                          kc == 0, kc == nk - 1, (wk, srcT_k), (pa_k,))
                cb(oc, tt, pa, pa_k)

    def lin_tm(self, srcT, srcT_k, ntok, wt, wk, ncols, cb, nk=16):
        cx = self.cx
        for tb in range(ntok // 128):
            pa, pa_k = self.pacc.next()
            for kc in range(nk):
                cx.mm(pa[:, 0:ncols], srcT[:, kc, tb * 128:(tb + 1) * 128], wt[:, kc, 0:ncols],
                      kc == 0, kc == nk - 1, (wk, srcT_k), (pa_k,))
            cb(tb, pa, pa_k)


class XItem:
    def __init__(self, src, gall, dst):
        self.src, self.gall, self.dst = src, gall, dst
        self.issued = set()
        self.tks = {}

    def reset(self):
        self.issued = set()
        self.tks = {}

    def tk(self, g, rt):
        if (g, rt) not in self.tks:
            self.tks[(g, rt)] = Tk(f"x{g}{rt}")
        return self.tks[(g, rt)]

    def issue(self, cx, cck, rts, gs=(0, 1, 2, 3), defer=None):
        for rt in rts:
            for g in gs:
                ci = 4 * g + rt
                if ci in self.issued:
                    continue
                self.issued.add(ci)
                src, gall = self.src, self.gall

                def rec(src=src, gall=gall, ci=ci, tk=self.tk(g, rt)):
                    cx.s.op("pool", lambda e: e.collective_compute(
                        "AllGather", ALU.bypass, replica_groups=GROUPS,
                        ins=[bass.AP(src, ci * CHK, [[8192, CHK // 8192], [1, 8192]])],
                        outs=[bass.AP(gall, ci * 4 * CHK, [[8192, 4 * CHK // 8192], [1, 8192]])]),
                        (), (tk,), dma=cck, inc=1)
                if defer is None:
                    rec()
                else:
                    defer.pending.append(rec)


def proj_qkv(cx, rk, hT, hT_k, ntok, t0, ost, specs, after_load=None):
    for si, (kind, w_d, col0, dst, xi) in enumerate(specs):
        for wb in range(4):
            wt, wk = rk.load_w(w_d, col0 + wb * 512, 512)
            if after_load is not None:
                after_load(si, wb)
            if kind == "fm":
                def cb(oc, tt, pa, pa_k, dst=dst, wb=wb, xi=xi):
                    o, ok = ost.next()
                    cx.copy(o[:], pa[:], (pa_k,), (ok,))
                    tok = t0 + tt * 512
                    rds = (ok,) if xi is None else (ok, xi.tk(wb, tok // 1024))
                    cx.dma("sp", dst[wb, tok // 1024, oc, :, tok % 1024:tok % 1024 + 512], o[:], ok, rds, ())
                rk.lin_fm(hT, hT_k, ntok, wt, wk, 512, cb)
            else:
                def cb(tb, pa, pa_k, dst=dst, wb=wb, xi=xi):
                    o, ok = ost.next()
                    cx.copy(o[:], pa[:], (pa_k,), (ok,))
                    tok = t0 + tb * 128
                    rds = (ok,) if xi is None else (ok, xi.tk(wb, tok // 1024))
                    cx.dma("sp", dst[wb, tok:tok + 128, :], o[:], ok, rds, ())
                rk.lin_tm(hT, hT_k, ntok, wt, wk, 512, cb)


def phase_a(nc, es, x_d, g_d, wqkv_d, ident_d, qT_d, kT_d, v_d, gsem=None, tag="", xis=(None, None, None)):
    cx = Ctx(nc, es, gsem, tag)
    rk = RowKit(cx, ident_d)
    rk.per_load = 2
    g, g_k = cx.sb("g", [128, 2048], F32)
    cx.dma("sp", g[:], g_d, g_k, (), (g_k,))
    xs = Slots(cx, "x", 2, [128, 2048], F32)
    hT, hT_k = cx.sb("hT", [128, 16, CH], BF16)
    ost = Slots(cx, "ost", 4, [128, 512], BF16)
    cck = Tk("cc")
    for half in range(2):
        t0 = half * CH
        if half == 1 and xis[0] is not None:
            for xi in xis:
                xi.issue(cx, cck, (0, 1), defer=rk)
        for tb in range(CH // 128):
            xt, xk = xs.next()
            cx.dma("sp", xt[:], x_d[t0 + tb * 128:t0 + (tb + 1) * 128, :], xk, (), (xk,))
            rk.norm_T(xt[:], xk, g, g_k, hT, hT_k, tb * 128)
        def early(si, wb, half=half):
            if half == 1 and xis[0] is not None and wb == 1 and si in (1, 2):
                xis[si - 1].issue(cx, cck, (2, 3), defer=rk)
        proj_qkv(cx, rk, hT, hT_k, CH, t0, ost,
                 [("fm", wqkv_d, 0, qT_d, xis[0]), ("fm", wqkv_d, 2048, kT_d, xis[1]),
                  ("tm", wqkv_d, 4096, v_d, xis[2])], early)
    rk.drain()
    cx.s.emit()


def attention(nc, es, mode, qT_d, kT_d, v_d, o_d, aux, gsem=None, tag="", xo=None):
    cx = Ctx(nc, es, gsem, tag)
    cck = Tk("cc")
    diff = mode == "diff"
    VD = 256 if diff else 128
    nheads = 2 if diff else 4
    nmaps = 2 if diff else 1
    NKB = S // 128
    KT = Slots(cx, "KT", 2, [128, S], BF16)
    Vt = Slots(cx, "Vt", 1 if diff else 2, [128, NKB, VD + 1], BF16)
    for vt, vk in Vt.items:
        cx.memset(vt[:, :, VD:VD + 1], 1.0, (vk,), eng="pool")
    QT = Slots(cx, "QT", 2, [128, nmaps, 512], BF16)
    if diff:
        PT = Slots(cx, "PT", 3, [128, 1024], BF16)
        psS = Slots(cx, "psS", 2, [128, 1024], F32, psum=True)
    else:
        PT = Slots(cx, "PT", 5, [128, 512], BF16)
        psS = Slots(cx, "psS", 4, [128, 512], F32, psum=True)
    psO = Slots(cx, "psO", 4, [128, 512], F32, psum=True)
    ostg = Slots(cx, "ostg", 2, [128, 4, VD], BF16)
    sm, sm_k = cx.sb("sm", [128, 16], F32)
    rec = Slots(cx, "rec", 4, [128, 2], F32)
    nssq = 0
    if diff:
        bw, bw_k = cx.sb("bw", [128, 2, 640], F32)
        cx.dma("sp", bw[:], aux["bias"], bw_k, (), (bw_k,))
        cx.memset(bw[64:128, :, 0:64], NEG, (bw_k,))
        tmpS = Slots(cx, "tmpS", 2, [128, 512], F32)
        A0, A0_k = cx.sb("A0", [128, 4, 256], F32)
        comb = Slots(cx, "comb", 2, [128, 256], F32)
        junk, junk_k = cx.sb("junk", [128, 256], BF16)
        gs, gs_k = cx.sb("gs", [128, 256], F32)
        cx.dma("sp", gs[:], aux["subg"], gs_k, (), (gs_k,))
        cx.ts(gs[:], gs[:], 1.0 - LAMBDA_INIT, None, ALU.mult, None, (gs_k,), (gs_k,))
        lamb, lamb_k = cx.sb("lamb", [128, 4, 128], F32)
        cx.dma("sp", lamb[:], aux["lam"], lamb_k, (), (lamb_k,))
        lp, lp_k = cx.sb("lp", [128, 2, 128], F32)
        cx.tt(lp[:, 0, :], lamb[:, 0, :], lamb[:, 1, :], ALU.mult, (lamb_k,), (lp_k,))
        cx.tt(lp[:, 1, :], lamb[:, 2, :], lamb[:, 3, :], ALU.mult, (lamb_k,), (lp_k,))
        cx.s.op("dve", lambda e: e.tensor_reduce(sm[:, 0:2], lp[:], AX.X, ALU.add), (lp_k,), (sm_k,))
        cx.act(sm[:, 2:4], sm[:, 0:2], AF.Exp, (sm_k,), (sm_k,))
        cx.tt(sm[:, 4:5], sm[:, 3:4], sm[:, 2:3], ALU.subtract, (sm_k,), (sm_k,))
        cx.ts(sm[:, 5:6], sm[:, 4:5], -LAMBDA_INIT, None, ALU.add, None, (sm_k,), (sm_k,))
        nlam = sm[:, 5:6]
        cx.memset(sm[:, 6:7], SUBEPS, (sm_k,))
        epsb = sm[:, 6:7]
        ssq, ssq_k = cx.sb("ssq", [128, 2 * 32 * 4], F32)
        cx.memset(ssq[:], 0.0, (ssq_k,))
    else:
        tri, tri_k = cx.sb("tri", [128, 128], BF16)
        cx.dma("sp", tri[:], aux["trimask"], tri_k, (), (tri_k,))
        cm, cm_k = cx.sb("cm", [128, 3, 128], F32)
        cx.dma("sp", cm[:], aux["cmats"], cm_k, (), (cm_k,))
        X, X_k = cx.sb("X", [128, NKB, 4], F32)
        for r in range(4):
            cx.dma("sp", X[:, r * 32:(r + 1) * 32, :],
                   aux["logf"][r].rearrange("(kb p) h -> p kb h", p=128), X_k, (), (X_k,))
        pre, pre_k = cx.sb("pre", [128, NKB * 4], F32)
        sA, sA_k = cx.sb("sA", [128, NKB * 4], F32)
        sB, sB_k = cx.sb("sB", [128, NKB * 4], F32)
        cT, cT_k = cx.sb("cT", [128, NKB, 4], F32)
        crefb, crefb_k = cx.sb("crefb", [128, NKB, 4], F32)
        Xf = X[:].rearrange("p k h -> p (k h)")
        pc, pc_k = psS.next()
        cx.mm(pc[:], cm[:, 0, :], Xf, True, True, (cm_k, X_k), (pc_k,))
        cx.copy(pre[:], pc[:], (pc_k,), (pre_k,), eng="dve")
        pc, pc_k = psS.next()
        cx.mm(pc[:], cm[:, 1, :], pre[:], True, True, (cm_k, pre_k), (pc_k,))
        cx.copy(sA[:], pc[:], (pc_k,), (sA_k,), eng="dve")
        cx.tt(pre[:], pre[:], sA[:], ALU.subtract, (pre_k, sA_k), (pre_k,))
        a, ak, b, bk = sA, sA_k, sB, sB_k
        sft = 4
        while sft < NKB * 4:
            cx.copy(b[:, 0:sft], a[:, 0:sft], (ak,), (bk,), eng="dve")
            cx.tt(b[:, sft:], a[:, sft:], a[:, 0:NKB * 4 - sft], ALU.add, (ak,), (bk,))
            a, ak, b, bk = b, bk, a, ak
            sft *= 2
        cx.tt(cT[:].rearrange("p k h -> p (k h)"), pre[:], a[:], ALU.add, (pre_k, ak), (cT_k,))
        pc, pc_k = psS.next()
        cx.mm(pc[:], cm[:, 2, :], cT[:].rearrange("p k h -> p (k h)"), True, True, (cm_k, cT_k), (pc_k,))
        cx.copy(crefb[:].rearrange("p k h -> p (k h)"), pc[:], (pc_k,), (crefb_k,), eng="dve")
        bcol = Slots(cx, "bcol", 3, [128, 2, NKB], F32)

    for hl in range(nheads):
        vt, vk = Vt.next()
        for r in range(4):
            for cl in range(4):
                src = v_d[r, cl].rearrange("(kb p) v -> p kb v", p=128)
                cx.dma("sp", vt[:, r * 32 + cl * 8:r * 32 + cl * 8 + 8, 0:VD],
                       src[:, :, hl * VD:(hl + 1) * VD], vk, (), (vk,))
        kts = []
        for c in range(nmaps):
            u = hl * nmaps + c
            kt, kk = KT.next()
            for r in range(4):
                for rt in range(4):
                    cx.dma("pool", kt[:, r * 4096 + rt * 1024:r * 4096 + (rt + 1) * 1024], kT_d[rt, r, u],
                           kk, (), (kk,))
            kts.append((kt, kk))
        tasks = []
        for qt in range(S // 512):
            for c in range(nmaps):
                kb, nfar = 0, (max(0, 4 * qt - 1) if diff else 0)
                while kb < 4 * qt + 4:
                    if kb + 1 < nfar:
                        tasks.append((qt, c, (kb, kb + 1)))
                        kb += 2
                    else:
                        tasks.append((qt, c, (kb,)))
                        kb += 1
        qtiles, ptiles, groups, ostage, bcols = {}, {}, {}, {}, {}

        def stage_s(i):
            nonlocal nssq
            qt, c, kbs = tasks[i]
            if qt not in qtiles:
                r, tq = qt // 8, (qt % 8) * 512
                q, qk = QT.next()
                for cc in range(nmaps):
                    cx.dma("sp", q[:, cc, :], qT_d[tq // 1024, r, hl * nmaps + cc, :, tq % 1024:tq % 1024 + 512],
                           qk, (), (qk,))
                qtiles[qt] = (q, qk)
                if not diff:
                    bc, bc_k = bcol.next()
                    nkb = 4 * qt + 4
                    for qh in range(FOXH):
                        refblk = 4 * qt + (2 if FOXH == 1 else 1 + 2 * qh)
                        cx.ts(bc[:, qh, 0:nkb], cT[:, 0:nkb, hl], -1.0, crefb[:, refblk, hl:hl + 1],
                              ALU.mult, ALU.add, (cT_k, crefb_k), (bc_k,))
                    bcols[qt] = (bc, bc_k)
            q, qk = qtiles[qt]
            kt, kk = kts[c]
            ps, ps_k = psS.next()
            p, pk = PT.next()
            for j, kb in enumerate(kbs):
                t = kb - 4 * qt
                ql = max(0, 128 * t)
                cx.mm(ps[:, j * 512 + ql:(j + 1) * 512], kt[:, kb * 128:(kb + 1) * 128], q[:, c, ql:512],
                      True, True, (kk, qk), (ps_k,))
            if diff:
                if t <= -2:
                    w_ = 512 * len(kbs)
                    cx.act(p[:, 0:w_], ps[:, 0:w_], AF.Exp, (ps_k, bw_k), (pk,), bias=bw[:, hl, 639:640], scale=SCALE)
                else:
                    tm, tmk = tmpS.next()
                    cx.stt(tm[:, ql:512], ps[:, ql:512], SCALE, bw[:, hl, ql - 128 * t:512 - 128 * t],
                           ALU.mult, ALU.add, (ps_k, bw_k), (tmk,))
                    cx.act(p[:, ql:512], tm[:, ql:512], AF.Exp, (tmk,), (pk,))
            else:
                bc, bc_k = bcols[qt]
                for qh in range(FOXH):
                    w_ = 512 // FOXH
                    lo, hi = max(ql, w_ * qh), w_ * (qh + 1)
                    if lo < hi:
                        cx.act(p[:, lo:hi], ps[:, lo:hi], AF.Exp, (ps_k, bc_k), (pk,),
                               bias=bc[:, qh, kb:kb + 1], scale=SCALE)
                if t >= 0:
                    cx.tt(p[:, ql:ql + 128], p[:, ql:ql + 128], tri[:], ALU.mult, (pk, tri_k), (pk,))
            ptiles[i] = (p, pk)

        def stage_av(i):
            nonlocal nssq
            qt, c, kbs = tasks[i]
            p, pk = ptiles.pop(i)
            if (qt, c) not in groups:
                groups[(qt, c)] = [psO.next() for _ in range(4)]
            Os = groups[(qt, c)]
            if qt not in ostage:
                ostage[qt] = ostg.next()
            os_, os_k = ostage[qt]
            for j, kb in enumerate(kbs):
                t = kb - 4 * qt
                for qs in range(max(t, 0), 4):
                    O, Ok = Os[qs]
                    cx.mm(O[:, 0:VD + 1], p[:, j * 512 + qs * 128:j * 512 + (qs + 1) * 128], vt[:, kb, :],
                          kb == 0, kb == 4 * qt + qs, (pk, vk), (Ok,))
            if kb != 4 * qt + 3:
                return
            for qs in range(4):
                O, Ok = Os[qs]
                rc, rck = rec.next()
                cx.s.op("dve", lambda e, rc=rc, O=O: e.reciprocal(rc[:, 0:1], O[:, VD:VD + 1]), (Ok,), (rck,))
                if not diff:
                    cx.ts(os_[:, qs, :], O[:, 0:VD], rc[:, 0:1], None, ALU.mult, None, (Ok, rck), (os_k,))
                elif c == 0:
                    cx.ts(A0[:, qs, :], O[:, 0:VD], rc[:, 0:1], None, ALU.mult, None, (Ok, rck), (A0_k,))
                else:
                    cx.tt(rc[:, 1:2], rc[:, 0:1], nlam, ALU.mult, (rck, sm_k), (rck,))
                    cb_, cbk = comb.next()
                    cx.stt(cb_[:], O[:, 0:VD], rc[:, 1:2], A0[:, qs, :], ALU.mult, ALU.add,
                           (Ok, rck, A0_k), (cbk,))
                    j = nssq
                    nssq += 1
                    cx.act(junk[:], cb_[:], AF.Square, (cbk,), (junk_k, ssq_k), accum=ssq[:, j:j + 1])
                    cx.act(ssq[:, j:j + 1], ssq[:, j:j + 1], AF.Ln, (ssq_k, sm_k), (ssq_k,),
                           bias=epsb, scale=1.0 / 256)
                    cx.act(ssq[:, j:j + 1], ssq[:, j:j + 1], AF.Exp, (ssq_k,), (ssq_k,), scale=-0.5)
                    cx.stt(os_[:, qs, :], cb_[:], ssq[:, j:j + 1], gs[:], ALU.mult, ALU.mult,
                           (cbk, ssq_k, gs_k), (os_k,))
            if c == nmaps - 1:
                ci = qt // 2
                rds = (os_k,) if xo is None else (os_k, xo.tk(ci // 4, ci % 4))
                cx.dma("sp", o_d[qt * 512:(qt + 1) * 512, hl * VD:(hl + 1) * VD].rearrange("(qs p) v -> p qs v", p=128),
                       os_[:], os_k, rds, ())
                if xo is not None and hl == nheads - 1 and qt % 2 == 1:
                    xo.issue(cx, cck, (ci % 4,), gs=(ci // 4,))
                del qtiles[qt]

        LOOK = 1 if diff else 3
        for i in range(min(LOOK, len(tasks))):
            stage_s(i)
        for i in range(len(tasks)):
            if i + LOOK < len(tasks):
                stage_s(i + LOOK)
            stage_av(i)
    cx.s.emit()


TT = 1024


def row_phase(nc, es, layer, x_d, o_d, ident_d, wo_d, gm_d, win_d, wout_d, ex, gsem=None, tag=""):
    cx = Ctx(nc, es, gsem, tag)
    rk = RowKit(cx, ident_d, nss=128, wslots=2, hnslots=1)
    h, _ = cx.sb("h", [128, 8, 2048], F32)
    hk = [Tk(f"h{i}") for i in range(8)]
    aT, aT_k = cx.sb("aT", [128, 16, TT], BF16)
    hid = Slots(cx, "hid", 2, [128, 4, TT], BF16)
    g, g_k = cx.sb("g", [128, 2048], F32)
    ob, ob_k = cx.sb("ob", [128, 2048], BF16)
    rr = Slots(cx, "rr", 2, [128, 512], BF16)
    ost = Slots(cx, "ost", 4, [128, 512], BF16)
    if layer == 0:
        wf, wf_k = cx.sb("wf", [128, 16, 16], BF16)
        cx.dma("pool", wf[:], ex["wf"].rearrange("(kc p) n -> p kc n", p=128), wf_k, (), (wf_k,))
        bfb, bfb_k = cx.sb("bfb", [128, 16], F32)
        cx.dma("sp", bfb[:], ex["bf"], bfb_k, (), (bfb_k,))
        zs = Slots(cx, "zs", 2, [128, 16], F32)
        one, one_k = cx.sb("one", [128, 1], F32)
        cx.memset(one[:], 1.0, (one_k,))
    else:
        fin = Slots(cx, "fin", 1, [128, 2048], F32)

    cck = Tk("cc")
    xis_ = ex.get("xis") or (None, None, None)
    for rt in range(TOK // TT):
        r0 = rt * TT
        for tb in range(8):
            cx.dma("sp", h[:, tb, :], x_d[r0 + tb * 128:r0 + (tb + 1) * 128, :], hk[tb], (), (hk[tb],))
        for tb in range(8):
            for gg in range(4):
                rw = r0 + tb * 128
                cx.dma("sp", ob[:, gg * 512:(gg + 1) * 512], o_d[gg, rw // 1024, rw % 1024:rw % 1024 + 128, :],
                       ob_k, (), (ob_k,))
            rk.transpose_in(ob, ob_k, aT, aT_k, tb * 128)
        for wb in range(4):
            wt, wk = rk.load_w(wo_d, wb * 512, 512)
            if layer == 0 and wb == 1 and rt > 0 and ex.get("xis") is not None:
                for xi in ex["xis"]:
                    xi.issue(cx, cck, (rt - 1,), defer=rk)

            def cb(tb, pa, pa_k, wb=wb):
                hs = h[:, tb, wb * 512:(wb + 1) * 512]
                cx.tt(hs, hs, pa[:], ALU.add, (pa_k, hk[tb]), (hk[tb],))
            rk.lin_tm(aT, aT_k, TT, wt, wk, 512, cb)
        cx.dma("sp", g[:], gm_d, g_k, (), (g_k,))
        for tb in range(8):
            rk.norm_T(h[:, tb, :], hk[tb], g, g_k, aT, aT_k, tb * 128)
        for hb in range(DFF // 512):
            wt, wk = rk.load_w(win_d, hb * 512, 512)
            hd, hd_k = hid.next()

            def cb(oc, tt, pa, pa_k, hd=hd, hd_k=hd_k):
                r_, rk_ = rr.next()
                cx.act(r_[:], pa[:], AF.Relu, (pa_k,), (rk_,))
                cx.tt(hd[:, oc, tt * 512:(tt + 1) * 512], r_[:], r_[:], ALU.mult, (rk_,), (hd_k,))
            rk.lin_fm(aT, aT_k, TT, wt, wk, 512, cb)
            wt, wk = rk.load_w(wout_d, 0, 2048, nk=4, k0=hb * 4)
            wv = wt
            for tb in range(8):
                for cg in range(4):
                    pa, pa_k = rk.pacc.next()
                    for kc in range(4):
                        cx.mm(pa[:], hd[:, kc, tb * 128:(tb + 1) * 128], wv[:, kc, cg * 512:(cg + 1) * 512],
                              kc == 0, kc == 3, (hd_k, wk), (pa_k,))
                    hs = h[:, tb, cg * 512:(cg + 1) * 512]
                    cx.tt(hs, hs, pa[:], ALU.add, (pa_k, hk[tb]), (hk[tb],))
        if layer == 0:
            for tb in range(8):
                cx.dma("sp", ex["h_out"][r0 + tb * 128:r0 + (tb + 1) * 128, :], h[:, tb, :], hk[tb], (hk[tb],), ())
            cx.dma("sp", g[:], ex["kvg"], g_k, (), (g_k,))
            for tb in range(8):
                rk.norm_T(h[:, tb, :], hk[tb], g, g_k, aT, aT_k, tb * 128)
            for tb in range(8):
                pa, pa_k = rk.pacc.next()
                for kc in range(16):
                    cx.mm(pa[:, 0:16], aT[:, kc, tb * 128:(tb + 1) * 128], wf[:, kc, :], kc == 0, kc == 15,
                          (aT_k, wf_k), (pa_k,))
                z, zk = zs.next()
                cx.tt(z[:], pa[:, 0:16], bfb[:], ALU.add, (pa_k, bfb_k), (zk,))
                cx.act(z[:], z[:], AF.Exp, (zk,), (zk,), scale=-1.0)
                cx.act(z[:], z[:], AF.Ln, (zk, one_k), (zk,), bias=one[:, 0:1])
                cx.ts(z[:], z[:], -1.0, None, ALU.mult, None, (zk,), (zk,))
                for gg in range(4):
                    cx.dma("sp", ex["logf"][gg, r0 + tb * 128:r0 + (tb + 1) * 128, :], z[:, gg * 4:(gg + 1) * 4],
                           zk, (zk,), ())
            def early_kv(si, wb, rt=rt):
                if rt == 3 and xis_[0] is not None and si == 1 and wb == 1:
                    xis_[1].issue(cx, cck, (3,), defer=rk)

            def early_q(si, wb, rt=rt):
                if rt == 3 and xis_[0] is not None and wb == 1:
                    xis_[2].issue(cx, cck, (3,), defer=rk)
            proj_qkv(cx, rk, aT, aT_k, TT, r0, ost,
                     [("fm", ex["wk"], 0, ex["k2T"], xis_[1]), ("tm", ex["wv"], 0, ex["v2"], xis_[2])], early_kv)
            cx.dma("sp", g[:], ex["qg"], g_k, (), (g_k,))
            for tb in range(8):
                rk.norm_T(h[:, tb, :], hk[tb], g, g_k, aT, aT_k, tb * 128)
            proj_qkv(cx, rk, aT, aT_k, TT, r0, ost, [("fm", ex["wq"], 0, ex["q2T"], xis_[0])], early_q)
        else:
            cx.dma("sp", g[:], ex["fg"], g_k, (), (g_k,))
            for tb in range(8):
                f_, fk = fin.next()
                rs = rk.norm_stats(h[:, tb, :], hk[tb])
                cx.stt(f_[:], h[:, tb, :], rs, g[:], ALU.mult, ALU.mult, (hk[tb], rk.rs_k, g_k), (fk,))
                cx.dma("sp", ex["out"][r0 + tb * 128:r0 + (tb + 1) * 128, :], f_[:], fk, (fk,), ())
    rk.drain()
    cx.s.emit()


def _t5_bucket_np(rel):
    half, max_exact = 16, 8
    ret = np.where(rel > 0, half, 0)
    n = np.abs(rel)
    nf = np.maximum(n, 1).astype(np.float32)
    large = max_exact + (np.log(nf / np.float32(max_exact)) / np.float32(math.log(128 / max_exact))
                         * np.float32(half - max_exact)).astype(np.int32)
    large = np.minimum(large, half - 1)
    return ret + np.where(n < max_exact, n, large)


def _bc(v, n=128):
    v = np.asarray(v, np.float32).reshape(1, -1)
    return np.ascontiguousarray(np.broadcast_to(v, (n, v.shape[1])))


def _a2a(outs, key, b):
    return [np.ascontiguousarray(np.stack([outs[b * 4 + r][key][g] for r in range(4)])) for g in range(4)]


def _launch(build, in_maps):
    nc = bass.Bass("TRN2", target_bir_lowering=False)
    with ExitStack() as es:
        build(nc, es)
    res = run_bass_kernel_spmd(nc, in_maps, core_ids=list(range(NCORE)))
    return res.results


def _din(nc, name, shape, dt=F32):
    return nc.dram_tensor(name, list(shape), dt, kind="ExternalInput").ap()


def _dout(nc, name, shape, dt=F32):
    return nc.dram_tensor(name, list(shape), dt, kind="ExternalOutput").ap()


IDENT = np.eye(128, dtype=np.float32).astype(ml_dtypes.bfloat16)


def build_a(nc, es):
    phase_a(nc, es, _din(nc, "x", [TOK, D]), _din(nc, "g", [128, D]), _din(nc, "wqkv", [D, 3 * D]),
            _din(nc, "ident", [128, 128], BF16), _dout(nc, "qT", [16, 128, TOK], BF16),
            _dout(nc, "kT", [16, 128, TOK], BF16), _dout(nc, "v", [4, TOK, 512], BF16))


def build_attn(mode):
    def build(nc, es):
        aux = {}
        if mode == "diff":
            aux["bias"] = _din(nc, "bias", [128, 2, 640])
            aux["subg"] = _din(nc, "subg", [128, 256])
            aux["lam"] = _din(nc, "lam", [128, 4, 128])
        else:
            aux["trimask"] = _din(nc, "trimask", [128, 128], BF16)
            aux["cmats"] = _din(nc, "cmats", [128, 3, 128])
            aux["logf"] = _din(nc, "logf", [4, TOK, 4])
        attention(nc, es, mode, _din(nc, "qT", [4, 4, 128, TOK], BF16), _din(nc, "kT", [4, 4, 128, TOK], BF16),
                  _din(nc, "v", [4, TOK, 512], BF16), _dout(nc, "o", [S, 512], BF16), aux)
    return build


def build_row(layer):
    def build(nc, es):
        ex = {}
        if layer == 0:
            for nm in ("kvg", "qg"):
                ex[nm] = _din(nc, nm, [128, D])
            for nm in ("wk", "wv", "wq"):
                ex[nm] = _din(nc, nm, [D, D])
            ex["wf"] = _din(nc, "wf", [D, 16])
            ex["bf"] = _din(nc, "bf", [128, 16])
            ex["h_out"] = _dout(nc, "h_out", [TOK, D])
            ex["q2T"] = _dout(nc, "q2T", [16, 128, TOK], BF16)
            ex["k2T"] = _dout(nc, "k2T", [16, 128, TOK], BF16)
            ex["v2"] = _dout(nc, "v2", [4, TOK, 512], BF16)
            ex["logf"] = _dout(nc, "logf", [4, TOK, 4])
        else:
            ex["fg"] = _din(nc, "fg", [128, D])
            ex["out"] = _dout(nc, "out", [TOK, D])
        row_phase(nc, es, layer, _din(nc, "x", [TOK, D]), _din(nc, "o", [4, TOK, 512], BF16),
                  _din(nc, "ident", [128, 128], BF16), _din(nc, "wo", [D, D]), _din(nc, "gm", [128, D]),
                  _din(nc, "win", [D, DFF]), _din(nc, "wout", [DFF, D]), ex)
    return build


I32 = mybir.dt.int32


class SemPool:
    def __init__(self, stack):
        self.stack = stack
        self.free = []
        self.regs = []


GROUPS = [[0, 1, 2, 3], [4, 5, 6, 7]]


FOXH = 2
CHK = 524288


def comm_phase(nc, es, gsem, tag, cid_d, items):
    cx = Ctx(nc, es, gsem, tag)
    cid, cid_k = cx.sb("cid", [1, 8], I32)
    cx.dma("sp", cid[:], cid_d, cid_k, (), (cid_k,))
    if not gsem.regs:
        gsem.regs = [gsem.stack.enter_context(nc.gpsimd.register(f"cr{i}")) for i in range(6)]
        for i in range(6):
            o = cx.s.op("pool", lambda e, i=i: e.reg_load(gsem.regs[i], cid[:1, i:i + 1]), (cid_k,), ())
            o.nosig = True
    regs = gsem.regs
    cck = Tk("cc")
    inner = [[8192, CHK // 8192], [1, 8192]]
    for n, it in enumerate(items):
        gk, dk = Tk(f"g{n}"), Tk(f"l{n}")
        if isinstance(it, XItem):
            it.tks = {}
            it.issue(cx, cck, (0, 1, 2, 3))
            gall, dst = it.gall, it.dst
            cx.s.op("pool", lambda e, dst=dst, gall=gall: e.dma_start(
                out=bass.AP(dst, 0, [[32768, 16 * CHK // 32768], [1, 32768]]),
                in_=bass.AP(gall, regs[0], [[32768, 16 * CHK // 32768], [1, 32768]])),
                tuple(it.tks.values()), (dk,), dma=dk)
            it.reset()
        else:
            _, src, gath, dst, ri, rstride, nel = it
            cx.s.op("pool", lambda e, src=src, gath=gath: e.collective_compute(
                "AllGather", ALU.bypass, replica_groups=GROUPS, ins=[src.ap().opt()], outs=[gath.ap().opt()]),
                (), (gk,), dma=cck, inc=1)
            cx.s.op("pool", lambda e, dst=dst, gath=gath, ri=ri, rstride=rstride, nel=nel: e.dma_start(
                out=bass.AP(dst, 0, [[nel, 4], [1, nel]]), in_=bass.AP(gath, regs[ri], [[rstride, 4], [1, nel]])),
                (gk,), (dk,), dma=dk)
    cx.s.emit()


def build_fused(nc, upto=9):
    di = lambda name, shape, dt=F32: nc.dram_tensor(name, list(shape), dt, kind="ExternalInput").ap()
    x_d = di("x", [TOK, D])
    ident_d = di("ident", [128, 128], BF16)
    cid_d = di("cid", [1, 8], I32)
    g_attn0, g_attn1, g_mlp0, g_mlp1, g_kv, g_fin = [di(n, [128, D]) for n in
                                                     ("g_attn0", "g_attn1", "g_mlp0", "g_mlp1", "g_kv", "g_fin")]
    wqkv = di("wqkv", [D, 3 * D])
    wo0, wo1, wk, wv, wq = [di(n, [D, D]) for n in ("wo0", "wo1", "wk", "wv", "wq")]
    win0, win1 = di("win0", [D, DFF]), di("win1", [D, DFF])
    wout0, wout1 = di("wout0", [DFF, D]), di("wout1", [DFF, D])
    wf, bf = di("wf", [D, 16]), di("bf", [128, 16])
    aux0 = {"bias": di("bias", [128, 2, 640]), "subg": di("subg", [128, 256]), "lam": di("lam", [128, 4, 128])}
    trimask, cmats = di("trimask", [128, 128], BF16), di("cmats", [128, 3, 128])
    out_d = nc.dram_tensor("out", [TOK, D], F32, kind="ExternalOutput").ap()
    dt_ = lambda name, shape, dt=BF16: nc.dram_tensor(name, list(shape), dt)
    qT, kT, v = dt_("i_qT", [8192, 1024]), dt_("i_kT", [8192, 1024]), dt_("i_v", [4 * TOK, 512])
    Gq, Gk, Gv = dt_("i_Gq", [8192, TOK]), dt_("i_Gk", [8192, TOK]), dt_("i_Gv", [16 * TOK, 512])
    Lq, Lk, Lv = dt_("i_Lq", [8192, 1024]), dt_("i_Lk", [8192, 1024]), dt_("i_Lv", [4 * TOK, 512])
    o, Lo = dt_("i_o", [S, 512]), dt_("i_Lo", [4 * TOK, 512])
    h1 = dt_("i_h1", [TOK, D], F32)
    logf, Glogf, Llogf = dt_("i_logf", [4 * TOK, 4], F32), dt_("i_Glogf", [16 * TOK, 4], F32), dt_("i_Llogf", [4 * TOK, 4], F32)
    u3 = lambda t: t.ap().rearrange("(g rt ul d) t -> g rt ul d t", g=4, rt=4, ul=4)
    r4 = lambda t: t.ap().rearrange("(rt r ul d) t -> rt r ul d t", rt=4, r=4, ul=4)
    xq, xk_, xv = XItem(qT, Gq, Lq), XItem(kT, Gk, Lk), XItem(v, Gv, Lv)
    xo = XItem(o, Gv, Lo)
    c4 = lambda t: t.ap().rearrange("(c r i) v -> r c i v", c=4, r=4)
    g3 = lambda t: t.ap().rearrange("(g t) v -> g t v", g=4)
    LG = ("small", logf, Glogf, Llogf, 1, 4 * TOK * 4, TOK * 4)
    with ExitStack() as gstack:
        gsem = SemPool(gstack)
        if upto > 0:
            with ExitStack() as es:
                phase_a(nc, es, x_d, g_attn0, wqkv, ident_d, u3(qT), u3(kT), g3(v), gsem, "A_", (xq, xk_, xv))
        if upto > 1:
            with ExitStack() as es:
                comm_phase(nc, es, gsem, "X1_", cid_d, [xq, xk_, xv])
        if upto > 2:
            with ExitStack() as es:
                attention(nc, es, "diff", r4(Lq), r4(Lk), c4(Lv), o.ap(), aux0, gsem, "B1_", xo)
        if upto > 3:
            with ExitStack() as es:
                comm_phase(nc, es, gsem, "X2_", cid_d, [xo])
        if upto > 4:
            with ExitStack() as es:
                ex = {"kvg": g_kv, "qg": g_attn1, "wk": wk, "wv": wv, "wq": wq, "wf": wf, "bf": bf,
                      "h_out": h1.ap(), "q2T": u3(qT), "k2T": u3(kT), "v2": g3(v), "logf": g3(logf), "xis": (xq, xk_, xv)}
                row_phase(nc, es, 0, x_d, c4(Lo), ident_d, wo0, g_mlp0, win0, wout0, ex, gsem, "B2_")
        if upto > 5:
            with ExitStack() as es:
                comm_phase(nc, es, gsem, "X3_", cid_d, [xq, xk_, xv, LG])
        if upto > 6:
            with ExitStack() as es:
                aux1 = {"trimask": trimask, "cmats": cmats, "logf": g3(Llogf)}
                attention(nc, es, "fox", r4(Lq), r4(Lk), c4(Lv), o.ap(), aux1, gsem, "C1_", xo)
        if upto > 7:
            with ExitStack() as es:
                comm_phase(nc, es, gsem, "X4_", cid_d, [xo])
        if upto > 8:
            with ExitStack() as es:
                row_phase(nc, es, 1, h1.ap(), c4(Lo), ident_d, wo1, g_mlp1, win1, wout1,
                          {"fg": g_fin, "out": out_d}, gsem, "C2_")
        if upto < 9:
            with ExitStack() as es:
                cx = Ctx(nc, es, gsem, "Z_")
                k = Tk("z")
                cx.dma("sp", out_d, x_d, k, (), (k,))
                cx.s.emit()


def kernel(x, rel_bias_table, attn_norm_g, mlp_norm_g, w_qkv_a, lam_q1, lam_k1, lam_q2, lam_k2,
           subln_g, w_o_a, kv_norm_g, w_k_b, w_v_b, w_f_b, b_f_b, w_q_b, w_o_b, w_mlp_in, w_mlp_out,
           final_norm_g, _upto=9):
    f32 = lambda a: np.ascontiguousarray(np.asarray(a, np.float32))
    x = f32(x)
    kk = np.arange(128)[:, None]
    jj = np.arange(640)[None, :]
    bias_all = f32(rel_bias_table)[_t5_bucket_np(kk - jj)]
    tri = (np.arange(128)[None, :] >= np.arange(128)[:, None]).astype(np.float32)
    cm = np.zeros((128, 3, 128), np.float32)
    cm[:, 0, :] = tri
    cm[127, 1, :] = 1.0
    cm[0, 2, :] = 1.0
    shared = {
        "ident": IDENT, "g_attn0": _bc(attn_norm_g[0]), "g_attn1": _bc(attn_norm_g[1]),
        "g_mlp0": _bc(mlp_norm_g[0]), "g_mlp1": _bc(mlp_norm_g[1]), "g_kv": _bc(kv_norm_g),
        "g_fin": _bc(final_norm_g), "wqkv": f32(w_qkv_a[0]), "wo0": f32(w_o_a[0]), "wo1": f32(w_o_b[0]),
        "wk": f32(w_k_b), "wv": f32(w_v_b), "wq": f32(w_q_b[0]), "win0": f32(w_mlp_in[0]),
        "win1": f32(w_mlp_in[1]), "wout0": f32(w_mlp_out[0]), "wout1": f32(w_mlp_out[1]),
        "wf": f32(w_f_b), "bf": _bc(b_f_b), "subg": _bc(subln_g[0]),
        "lam": np.ascontiguousarray(np.broadcast_to(
            np.stack([f32(lam_q1[0]), f32(lam_k1[0]), f32(lam_q2[0]), f32(lam_k2[0])])[None], (128, 4, 128))),
        "trimask": tri.astype(ml_dtypes.bfloat16), "cmats": cm,
    }
    in_maps = []
    for c in range(NCORE):
        b, g = c // 4, c % 4
        m = dict(shared)
        m["x"] = np.ascontiguousarray(x[b, g * TOK:(g + 1) * TOK])
        m["bias"] = np.ascontiguousarray(bias_all[:, :, 2 * g:2 * g + 2].transpose(0, 2, 1))
        cid = np.zeros((1, 8), np.int32)
        cid[0, 0] = g * 16 * CHK
        for r in range(4):
            cid[0, 2 + r] = g * 16 * CHK + r * CHK
        cid[0, 1] = g * TOK * 4
        m["cid"] = cid
        in_maps.append(m)
    nc = bass.Bass("TRN2", target_bir_lowering=False)
    build_fused(nc, _upto)
    res = run_bass_kernel_spmd(nc, in_maps, core_ids=list(range(NCORE))).results
    out = np.empty((NB, S, D), np.float32)
    for c in range(NCORE):
        out[c // 4, (c % 4) * TOK:(c % 4 + 1) * TOK] = res[c]["out"]
    return out
```

```python
import math
from contextlib import ExitStack

import numpy as np
import ml_dtypes
import concourse.bass as bass
import concourse.mybir as mybir
from concourse.bass_utils import run_bass_kernel_spmd

F32 = mybir.dt.float32
BF16 = mybir.dt.bfloat16
AF = mybir.ActivationFunctionType
ALU = mybir.AluOpType
AX = mybir.AxisListType

D = 2048
S = 16384
NB = 2
DFF = 8192
NCORE = 8
TOK = 4096
CH = 2048
NEG = -30000.0
SCALE = 128 ** -0.5
EPS = 1e-6
SUBEPS = 1e-5
LAMBDA_INIT = 0.8 - 0.6 * math.exp(-0.3 * 0)


class Tk:
    __slots__ = ("name", "w", "r", "semcnt")

    def __init__(self, name):
        self.name = name
        self.w = None
        self.r = []
        self.semcnt = 0


class Op:
    __slots__ = ("eng", "fn", "deps", "needed", "is_dma", "tk", "val", "key", "inc", "nosig")


ENGS = ("pe", "act", "dve", "pool", "sp")
BLK = {"pe": "tensor", "act": "scalar", "dve": "vector", "pool": "gpsimd", "sp": "sync"}


class Sched:
    def __init__(self, nc, gsem=None, tag=""):
        self.nc = nc
        self.gsem = gsem
        self.tag = tag
        self.ops = {e: [] for e in ENGS}
        self.all = []

    def op(self, eng, fn, reads=(), writes=(), dma=None, inc=16):
        o = Op()
        o.inc = inc
        o.nosig = False
        o.eng = eng
        o.fn = fn
        o.needed = False
        o.is_dma = dma is not None
        o.tk = dma
        o.val = 0
        o.key = None
        deps = []
        for t in reads:
            if t.w is not None:
                deps.append(t.w)
        for t in writes:
            if t.w is not None:
                deps.append(t.w)
            deps.extend(t.r)
        o.deps = deps
        for d in deps:
            d.needed = True
        for t in reads:
            t.r.append(o)
        for t in writes:
            t.w = o
            t.r = []
        self.ops[eng].append(o)
        self.all.append(o)
        return o

    def emit(self):
        nc = self.nc
        engcnt = {e: 0 for e in ENGS}
        keys = {}
        for o in self.all:
            if o.is_dma:
                o.tk.semcnt += o.inc
                o.val = o.tk.semcnt
                o.key = ("t", id(o.tk))
                keys[o.key] = "d_" + o.tk.name
            elif o.needed:
                engcnt[o.eng] += 1
                o.val = engcnt[o.eng]
                o.key = ("e", o.eng)
                keys[o.key] = "e_" + o.eng
        if self.gsem is not None:
            for e in ENGS:
                for o in reversed(self.ops[e]):
                    if not o.is_dma and not o.nosig:
                        if not o.needed:
                            o.needed = True
                            engcnt[e] += 1
                            o.val = engcnt[e]
                            o.key = ("e", e)
                            keys[o.key] = "e_" + e
                        break
        final = {}
        for o in self.all:
            if o.key is not None:
                final[o.key] = max(final.get(o.key, 0), o.val)
        with ExitStack() as es:
            sems, base = {}, {}
            for k, n in keys.items():
                if self.gsem is None:
                    sems[k], base[k] = es.enter_context(nc.semaphore(self.tag + n)), 0
                elif self.gsem.free:
                    sems[k], base[k] = self.gsem.free.pop()
                else:
                    sems[k], base[k] = self.gsem.stack.enter_context(nc.semaphore(self.tag + n)), 0
            block = es.enter_context(nc.Block())
            for en in ENGS:
                def body(eng, en=en):
                    waited = {}
                    for o in self.ops[en]:
                        for d in o.deps:
                            if en == "pe" and d.eng == "pe" and not d.is_dma:
                                continue
                            if waited.get(d.key, 0) >= d.val:
                                continue
                            eng.wait_ge(sems[d.key], base[d.key] + d.val)
                            waited[d.key] = d.val
                        ins = o.fn(eng)
                        if o.is_dma:
                            ins.then_inc(sems[o.key], o.inc)
                        elif o.needed:
                            ins.then_inc(sems[o.key], 1)
                    if en == "sp" or self.gsem is not None:
                        for k, v in final.items():
                            if waited.get(k, 0) < v:
                                eng.wait_ge(sems[k], base[k] + v)
                getattr(block, BLK[en])(body)
            if self.gsem is not None:
                for k in keys:
                    self.gsem.free.append((sems[k], base[k] + final[k]))


class Ctx:
    def __init__(self, nc, es, gsem=None, tag=""):
        self.nc = nc
        self.es = es
        self.tag = tag
        self.s = Sched(nc, gsem, tag)
        self.rr = 0

    def sb(self, name, shape, dt):
        t = self.es.enter_context(self.nc.sbuf_tensor("sb_" + self.tag + name, list(shape), dt))
        return t, Tk(name)

    def ps(self, name, shape, dt=F32):
        t = self.es.enter_context(self.nc.psum_tensor("ps_" + self.tag + name, list(shape), dt))
        return t, Tk(name)

    def dma(self, q, out, in_, tk, reads=(), writes=()):
        return self.s.op(q, lambda e: e.dma_start(out=out, in_=in_), reads, writes, dma=tk)

    def mm(self, out, lhsT, rhs, start, stop, reads, writes):
        return self.s.op("pe", lambda e: e.matmul(out, lhsT, rhs, start=start, stop=stop),
                         reads, writes)

    def tr(self, out, in_, ident, reads, writes):
        return self.s.op("pe", lambda e: e.transpose(out, in_, ident), reads, writes)

    def act(self, out, in_, func, reads, writes, bias=0.0, scale=1.0, accum=None):
        if accum is None:
            return self.s.op("act", lambda e: e.activation(out, in_, func, bias=bias, scale=scale),
                             reads, writes)
        return self.s.op("act", lambda e: e.activation(out, in_, func, bias=bias, scale=scale,
                                                       accum_out=accum), reads, writes)

    def copy(self, out, in_, reads, writes, eng=None):
        if eng is None:
            self.rr ^= 1
            eng = "act" if self.rr else "dve"
        if eng == "act":
            return self.s.op("act", lambda e: e.copy(out, in_), reads, writes)
        return self.s.op(eng, lambda e: e.tensor_copy(out, in_), reads, writes)

    def ts(self, out, in0, s1, s2, op0, op1, reads, writes, eng="dve"):
        if op1 is None:
            return self.s.op(eng, lambda e: e.tensor_scalar(out, in0, s1, s2, op0), reads, writes)
        return self.s.op(eng, lambda e: e.tensor_scalar(out, in0, s1, s2, op0, op1), reads, writes)

    def stt(self, out, in0, scalar, in1, op0, op1, reads, writes, eng="dve"):
        return self.s.op(eng, lambda e: e.scalar_tensor_tensor(out, in0, scalar, in1, op0, op1),
                         reads, writes)

    def tt(self, out, in0, in1, op, reads, writes, eng="dve"):
        return self.s.op(eng, lambda e: e.tensor_tensor(out, in0, in1, op), reads, writes)

    def memset(self, ap, val, writes, eng="dve"):
        return self.s.op(eng, lambda e: e.memset(ap, val), (), writes)


class Slots:
    def __init__(self, cx, name, n, shape, dt, psum=False):
        self.items = []
        for i in range(n):
            self.items.append(cx.ps(f"{name}{i}", shape, dt) if psum else cx.sb(f"{name}{i}", shape, dt))
        self.i = 0

    def next(self):
        it = self.items[self.i % len(self.items)]
        self.i += 1
        return it


class RowKit:
    def __init__(self, cx, ident_d, nss=64, wslots=3, hnslots=2):
        self.cx = cx
        self.ident, self.ident_k = cx.sb("ident", [128, 128], BF16)
        cx.dma("sp", self.ident[:], ident_d, self.ident_k, (), (self.ident_k,))
        self.w = Slots(cx, "w", wslots, [128, 8192], BF16)
        self.pacc = Slots(cx, "pacc", 4, [128, 512], F32, psum=True)
        self.ptr = Slots(cx, "ptr", 2, [128, 1024], BF16, psum=True)
        self.hn = Slots(cx, "hn", hnslots, [128, 2048], BF16)
        self.ss, self.ss_k = cx.sb("ss", [128, nss], F32)
        self.rs, self.rs_k = cx.sb("rs", [128, nss], F32)
        cx.memset(self.ss[:], 0.0, (self.ss_k,))
        self.epsb, self.eps_k = cx.sb("epsb", [128, 2], F32)
        cx.memset(self.epsb[:, 0:1], EPS, (self.eps_k,))
        cx.memset(self.epsb[:, 1:2], SUBEPS, (self.eps_k,))
        self.eps_main = self.epsb[:, 0:1]
        self.eps_sub = self.epsb[:, 1:2]
        self.nss = 0
        self.pending = []
        self.per_load = 1

    def drain(self, n=None):
        while self.pending and (n is None or n > 0):
            self.pending.pop(0)()
            if n is not None:
                n -= 1

    def norm_stats(self, x_ap, x_k, junk=None, junk_k=None):
        cx = self.cx
        j = self.nss
        self.nss += 1
        ss = self.ss[:, j:j + 1]
        rs = self.rs[:, j:j + 1]
        if junk is None:
            junk, junk_k = self.hn.next()
        cx.act(junk[:], x_ap, AF.Square, (x_k,), (junk_k, self.ss_k), accum=ss)
        cx.act(rs, ss, AF.Ln, (self.ss_k, self.eps_k), (self.rs_k,), bias=self.eps_main, scale=1.0 / D)
        cx.act(rs, rs, AF.Exp, (self.rs_k,), (self.rs_k,), scale=-0.5)
        return rs

    def norm_T(self, x_ap, x_k, g, g_k, dstT, dstT_k, tcol):
        cx = self.cx
        hn, hn_k = self.hn.next()
        rs = self.norm_stats(x_ap, x_k, hn, hn_k)
        cx.stt(hn[:], x_ap, rs, g[:], ALU.mult, ALU.mult, (x_k, self.rs_k, g_k), (hn_k,))
        self.transpose_in(hn, hn_k, dstT, dstT_k, tcol)

    def transpose_in(self, src, src_k, dstT, dstT_k, tcol, nchunk=16, c0=0):
        cx = self.cx
        for half in range(0, nchunk, 8):
            n = min(8, nchunk - half)
            pt, pt_k = self.ptr.next()
            for i in range(n):
                cx.tr(pt[:, i * 128:(i + 1) * 128], src[:, (half + i) * 128:(half + i + 1) * 128],
                      self.ident[:], (src_k, self.ident_k), (pt_k,))
            cx.copy(dstT[:, c0 + half:c0 + half + n, tcol:tcol + 128],
                    pt[:, 0:n * 128].rearrange("p (c t) -> p c t", t=128), (pt_k,), (dstT_k,))

    def load_w(self, w_d, c0, ncols, nk=16, k0=0):
        cx = self.cx
        wt, wk = self.w.next()
        wv_s = wt[:, 0:nk * ncols].rearrange("p (k n) -> p k n", n=ncols)
        wv = w_d.rearrange("(kc p) n -> p kc n", p=128)
        st = max(1, nk // 4)
        for q in range(0, nk, st):
            cx.dma("pool", wv_s[:, q:q + st, :], wv[:, k0 + q:k0 + q + st, c0:c0 + ncols], wk, (), (wk,))
        self.drain(self.per_load)
        return wv_s, wk

    def lin_fm(self, srcT, srcT_k, ntok, wt, wk, ncols, cb, nk=16):
        cx = self.cx
        for oc in range(ncols // 128):
            for tt in range(ntok // 512):
                pa, pa_k = self.pacc.next()
                for kc in range(nk):
                    cx.mm(pa[:], wt[:, kc, oc * 128:(oc + 1) * 128], srcT[:, kc, tt * 512:(tt + 1) * 512],
                          kc == 0, kc == nk - 1, (wk, srcT_k), (pa_k,))
                cb(oc, tt, pa, pa_k)

    def lin_tm(self, srcT, srcT_k, ntok, wt, wk, ncols, cb, nk=16):
        cx = self.cx
        for tb in range(ntok // 128):
            pa, pa_k = self.pacc.next()
            for kc in range(nk):
                cx.mm(pa[:, 0:ncols], srcT[:, kc, tb * 128:(tb + 1) * 128], wt[:, kc, 0:ncols],
                      kc == 0, kc == nk - 1, (wk, srcT_k), (pa_k,))
            cb(tb, pa, pa_k)


class XItem:
    def __init__(self, src, gall, dst):
        self.src, self.gall, self.dst = src, gall, dst
        self.issued = set()
        self.tks = {}

    def reset(self):
        self.issued = set()
        self.tks = {}

    def tk(self, g, rt):
        if (g, rt) not in self.tks:
            self.tks[(g, rt)] = Tk(f"x{g}{rt}")
        return self.tks[(g, rt)]

    def issue(self, cx, cck, rts, gs=(0, 1, 2, 3), defer=None):
        for rt in rts:
            for g in gs:
                ci = 4 * g + rt
                if ci in self.issued:
                    continue
                self.issued.add(ci)
                src, gall = self.src, self.gall

                def rec(src=src, gall=gall, ci=ci, tk=self.tk(g, rt)):
                    cx.s.op("pool", lambda e: e.collective_compute(
                        "AllGather", ALU.bypass, replica_groups=GROUPS,
                        ins=[bass.AP(src, ci * CHK, [[8192, CHK // 8192], [1, 8192]])],
                        outs=[bass.AP(gall, ci * 4 * CHK, [[8192, 4 * CHK // 8192], [1, 8192]])]),
                        (), (tk,), dma=cck, inc=1)
                if defer is None:
                    rec()
                else:
                    defer.pending.append(rec)


def proj_qkv(cx, rk, hT, hT_k, ntok, t0, ost, specs, after_load=None):
    for si, (kind, w_d, col0, dst, xi) in enumerate(specs):
        for wb in range(4):
            wt, wk = rk.load_w(w_d, col0 + wb * 512, 512)
            if after_load is not None:
                after_load(si, wb)
            if kind == "fm":
                def cb(oc, tt, pa, pa_k, dst=dst, wb=wb, xi=xi):
                    o, ok = ost.next()
                    cx.copy(o[:], pa[:], (pa_k,), (ok,))
                    tok = t0 + tt * 512
                    rds = (ok,) if xi is None else (ok, xi.tk(wb, tok // 1024))
                    cx.dma("sp", dst[wb, tok // 1024, oc, :, tok % 1024:tok % 1024 + 512], o[:], ok, rds, ())
                rk.lin_fm(hT, hT_k, ntok, wt, wk, 512, cb)
            else:
                def cb(tb, pa, pa_k, dst=dst, wb=wb, xi=xi):
                    o, ok = ost.next()
                    cx.copy(o[:], pa[:], (pa_k,), (ok,))
                    tok = t0 + tb * 128
                    rds = (ok,) if xi is None else (ok, xi.tk(wb, tok // 1024))
                    cx.dma("sp", dst[wb, tok:tok + 128, :], o[:], ok, rds, ())
                rk.lin_tm(hT, hT_k, ntok, wt, wk, 512, cb)


def phase_a(nc, es, x_d, g_d, wqkv_d, ident_d, qT_d, kT_d, v_d, gsem=None, tag="", xis=(None, None, None)):
    cx = Ctx(nc, es, gsem, tag)
    rk = RowKit(cx, ident_d)
    rk.per_load = 2
    g, g_k = cx.sb("g", [128, 2048], F32)
    cx.dma("sp", g[:], g_d, g_k, (), (g_k,))
    xs = Slots(cx, "x", 2, [128, 2048], F32)
    hT, hT_k = cx.sb("hT", [128, 16, CH], BF16)
    ost = Slots(cx, "ost", 4, [128, 512], BF16)
    cck = Tk("cc")
    for half in range(2):
        t0 = half * CH
        if half == 1 and xis[0] is not None:
            for xi in xis:
                xi.issue(cx, cck, (0, 1), defer=rk)
        for tb in range(CH // 128):
            xt, xk = xs.next()
            cx.dma("sp", xt[:], x_d[t0 + tb * 128:t0 + (tb + 1) * 128, :], xk, (), (xk,))
            rk.norm_T(xt[:], xk, g, g_k, hT, hT_k, tb * 128)
        def early(si, wb, half=half):
            if half == 1 and xis[0] is not None and wb == 1 and si in (1, 2):
                xis[si - 1].issue(cx, cck, (2, 3), defer=rk)
        proj_qkv(cx, rk, hT, hT_k, CH, t0, ost,
                 [("fm", wqkv_d, 0, qT_d, xis[0]), ("fm", wqkv_d, 2048, kT_d, xis[1]),
                  ("tm", wqkv_d, 4096, v_d, xis[2])], early)
    rk.drain()
    cx.s.emit()


def attention(nc, es, mode, qT_d, kT_d, v_d, o_d, aux, gsem=None, tag="", xo=None):
    cx = Ctx(nc, es, gsem, tag)
    cck = Tk("cc")
    diff = mode == "diff"
    VD = 256 if diff else 128
    nheads = 2 if diff else 4
    nmaps = 2 if diff else 1
    NKB = S // 128
    KT = Slots(cx, "KT", 2, [128, S], BF16)
    Vt = Slots(cx, "Vt", 1 if diff else 2, [128, NKB, VD + 1], BF16)
    for vt, vk in Vt.items:
        cx.memset(vt[:, :, VD:VD + 1], 1.0, (vk,), eng="pool")
    QT = Slots(cx, "QT", 2, [128, nmaps, 512], BF16)
    if diff:
        PT = Slots(cx, "PT", 3, [128, 1024], BF16)
        psS = Slots(cx, "psS", 2, [128, 1024], F32, psum=True)
    else:
        PT = Slots(cx, "PT", 5, [128, 512], BF16)
        psS = Slots(cx, "psS", 4, [128, 512], F32, psum=True)
    psO = Slots(cx, "psO", 4, [128, 512], F32, psum=True)
    ostg = Slots(cx, "ostg", 2, [128, 4, VD], BF16)
    sm, sm_k = cx.sb("sm", [128, 16], F32)
    rec = Slots(cx, "rec", 4, [128, 2], F32)
    nssq = 0
    if diff:
        bw, bw_k = cx.sb("bw", [128, 2, 640], F32)
        cx.dma("sp", bw[:], aux["bias"], bw_k, (), (bw_k,))
        cx.memset(bw[64:128, :, 0:64], NEG, (bw_k,))
        tmpS = Slots(cx, "tmpS", 2, [128, 512], F32)
        A0, A0_k = cx.sb("A0", [128, 4, 256], F32)
        comb = Slots(cx, "comb", 2, [128, 256], F32)
        junk, junk_k = cx.sb("junk", [128, 256], BF16)
        gs, gs_k = cx.sb("gs", [128, 256], F32)
        cx.dma("sp", gs[:], aux["subg"], gs_k, (), (gs_k,))
        cx.ts(gs[:], gs[:], 1.0 - LAMBDA_INIT, None, ALU.mult, None, (gs_k,), (gs_k,))
        lamb, lamb_k = cx.sb("lamb", [128, 4, 128], F32)
        cx.dma("sp", lamb[:], aux["lam"], lamb_k, (), (lamb_k,))
        lp, lp_k = cx.sb("lp", [128, 2, 128], F32)
        cx.tt(lp[:, 0, :], lamb[:, 0, :], lamb[:, 1, :], ALU.mult, (lamb_k,), (lp_k,))
        cx.tt(lp[:, 1, :], lamb[:, 2, :], lamb[:, 3, :], ALU.mult, (lamb_k,), (lp_k,))
        cx.s.op("dve", lambda e: e.tensor_reduce(sm[:, 0:2], lp[:], AX.X, ALU.add), (lp_k,), (sm_k,))
        cx.act(sm[:, 2:4], sm[:, 0:2], AF.Exp, (sm_k,), (sm_k,))
        cx.tt(sm[:, 4:5], sm[:, 3:4], sm[:, 2:3], ALU.subtract, (sm_k,), (sm_k,))
        cx.ts(sm[:, 5:6], sm[:, 4:5], -LAMBDA_INIT, None, ALU.add, None, (sm_k,), (sm_k,))
        nlam = sm[:, 5:6]
        cx.memset(sm[:, 6:7], SUBEPS, (sm_k,))
        epsb = sm[:, 6:7]
        ssq, ssq_k = cx.sb("ssq", [128, 2 * 32 * 4], F32)
        cx.memset(ssq[:], 0.0, (ssq_k,))
    else:
        tri, tri_k = cx.sb("tri", [128, 128], BF16)
        cx.dma("sp", tri[:], aux["trimask"], tri_k, (), (tri_k,))
        cm, cm_k = cx.sb("cm", [128, 3, 128], F32)
        cx.dma("sp", cm[:], aux["cmats"], cm_k, (), (cm_k,))
        X, X_k = cx.sb("X", [128, NKB, 4], F32)
        for r in range(4):
            cx.dma("sp", X[:, r * 32:(r + 1) * 32, :],
                   aux["logf"][r].rearrange("(kb p) h -> p kb h", p=128), X_k, (), (X_k,))
        pre, pre_k = cx.sb("pre", [128, NKB * 4], F32)
        sA, sA_k = cx.sb("sA", [128, NKB * 4], F32)
        sB, sB_k = cx.sb("sB", [128, NKB * 4], F32)
        cT, cT_k = cx.sb("cT", [128, NKB, 4], F32)
        crefb, crefb_k = cx.sb("crefb", [128, NKB, 4], F32)
        Xf = X[:].rearrange("p k h -> p (k h)")
        pc, pc_k = psS.next()
        cx.mm(pc[:], cm[:, 0, :], Xf, True, True, (cm_k, X_k), (pc_k,))
        cx.copy(pre[:], pc[:], (pc_k,), (pre_k,), eng="dve")
        pc, pc_k = psS.next()
        cx.mm(pc[:], cm[:, 1, :], pre[:], True, True, (cm_k, pre_k), (pc_k,))
        cx.copy(sA[:], pc[:], (pc_k,), (sA_k,), eng="dve")
        cx.tt(pre[:], pre[:], sA[:], ALU.subtract, (pre_k, sA_k), (pre_k,))
        a, ak, b, bk = sA, sA_k, sB, sB_k
        sft = 4
        while sft < NKB * 4:
            cx.copy(b[:, 0:sft], a[:, 0:sft], (ak,), (bk,), eng="dve")
            cx.tt(b[:, sft:], a[:, sft:], a[:, 0:NKB * 4 - sft], ALU.add, (ak,), (bk,))
            a, ak, b, bk = b, bk, a, ak
            sft *= 2
        cx.tt(cT[:].rearrange("p k h -> p (k h)"), pre[:], a[:], ALU.add, (pre_k, ak), (cT_k,))
        pc, pc_k = psS.next()
        cx.mm(pc[:], cm[:, 2, :], cT[:].rearrange("p k h -> p (k h)"), True, True, (cm_k, cT_k), (pc_k,))
        cx.copy(crefb[:].rearrange("p k h -> p (k h)"), pc[:], (pc_k,), (crefb_k,), eng="dve")
        bcol = Slots(cx, "bcol", 3, [128, 2, NKB], F32)

    for hl in range(nheads):
        vt, vk = Vt.next()
        for r in range(4):
            for cl in range(4):
                src = v_d[r, cl].rearrange("(kb p) v -> p kb v", p=128)
                cx.dma("sp", vt[:, r * 32 + cl * 8:r * 32 + cl * 8 + 8, 0:VD],
                       src[:, :, hl * VD:(hl + 1) * VD], vk, (), (vk,))
        kts = []
        for c in range(nmaps):
            u = hl * nmaps + c
            kt, kk = KT.next()
            for r in range(4):
                for rt in range(4):
                    cx.dma("pool", kt[:, r * 4096 + rt * 1024:r * 4096 + (rt + 1) * 1024], kT_d[rt, r, u],
                           kk, (), (kk,))
            kts.append((kt, kk))
        tasks = []
        for qt in range(S // 512):
            for c in range(nmaps):
                kb, nfar = 0, (max(0, 4 * qt - 1) if diff else 0)
                while kb < 4 * qt + 4:
                    if kb + 1 < nfar:
                        tasks.append((qt, c, (kb, kb + 1)))
                        kb += 2
                    else:
                        tasks.append((qt, c, (kb,)))
                        kb += 1
        qtiles, ptiles, groups, ostage, bcols = {}, {}, {}, {}, {}

        def stage_s(i):
            nonlocal nssq
            qt, c, kbs = tasks[i]
            if qt not in qtiles:
                r, tq = qt // 8, (qt % 8) * 512
                q, qk = QT.next()
                for cc in range(nmaps):
                    cx.dma("sp", q[:, cc, :], qT_d[tq // 1024, r, hl * nmaps + cc, :, tq % 1024:tq % 1024 + 512],
                           qk, (), (qk,))
                qtiles[qt] = (q, qk)
                if not diff:
                    bc, bc_k = bcol.next()
                    nkb = 4 * qt + 4
                    for qh in range(FOXH):
                        refblk = 4 * qt + (2 if FOXH == 1 else 1 + 2 * qh)
                        cx.ts(bc[:, qh, 0:nkb], cT[:, 0:nkb, hl], -1.0, crefb[:, refblk, hl:hl + 1],
                              ALU.mult, ALU.add, (cT_k, crefb_k), (bc_k,))
                    bcols[qt] = (bc, bc_k)
            q, qk = qtiles[qt]
            kt, kk = kts[c]
            ps, ps_k = psS.next()
            p, pk = PT.next()
            for j, kb in enumerate(kbs):
                t = kb - 4 * qt
                ql = max(0, 128 * t)
                cx.mm(ps[:, j * 512 + ql:(j + 1) * 512], kt[:, kb * 128:(kb + 1) * 128], q[:, c, ql:512],
                      True, True, (kk, qk), (ps_k,))
            if diff:
                if t <= -2:
                    w_ = 512 * len(kbs)
                    cx.act(p[:, 0:w_], ps[:, 0:w_], AF.Exp, (ps_k, bw_k), (pk,), bias=bw[:, hl, 639:640], scale=SCALE)
                else:
                    tm, tmk = tmpS.next()
                    cx.stt(tm[:, ql:512], ps[:, ql:512], SCALE, bw[:, hl, ql - 128 * t:512 - 128 * t],
                           ALU.mult, ALU.add, (ps_k, bw_k), (tmk,))
                    cx.act(p[:, ql:512], tm[:, ql:512], AF.Exp, (tmk,), (pk,))
            else:
                bc, bc_k = bcols[qt]
                for qh in range(FOXH):
                    w_ = 512 // FOXH
                    lo, hi = max(ql, w_ * qh), w_ * (qh + 1)
                    if lo < hi:
                        cx.act(p[:, lo:hi], ps[:, lo:hi], AF.Exp, (ps_k, bc_k), (pk,),
                               bias=bc[:, qh, kb:kb + 1], scale=SCALE)
                if t >= 0:
                    cx.tt(p[:, ql:ql + 128], p[:, ql:ql + 128], tri[:], ALU.mult, (pk, tri_k), (pk,))
            ptiles[i] = (p, pk)

        def stage_av(i):
            nonlocal nssq
            qt, c, kbs = tasks[i]
            p, pk = ptiles.pop(i)
            if (qt, c) not in groups:
                groups[(qt, c)] = [psO.next() for _ in range(4)]
            Os = groups[(qt, c)]
            if qt not in ostage:
                ostage[qt] = ostg.next()
            os_, os_k = ostage[qt]
            for j, kb in enumerate(kbs):
                t = kb - 4 * qt
                for qs in range(max(t, 0), 4):
                    O, Ok = Os[qs]
                    cx.mm(O[:, 0:VD + 1], p[:, j * 512 + qs * 128:j * 512 + (qs + 1) * 128], vt[:, kb, :],
                          kb == 0, kb == 4 * qt + qs, (pk, vk), (Ok,))
            if kb != 4 * qt + 3:
                return
            for qs in range(4):
                O, Ok = Os[qs]
                rc, rck = rec.next()
                cx.s.op("dve", lambda e, rc=rc, O=O: e.reciprocal(rc[:, 0:1], O[:, VD:VD + 1]), (Ok,), (rck,))
                if not diff:
                    cx.ts(os_[:, qs, :], O[:, 0:VD], rc[:, 0:1], None, ALU.mult, None, (Ok, rck), (os_k,))
                elif c == 0:
                    cx.ts(A0[:, qs, :], O[:, 0:VD], rc[:, 0:1], None, ALU.mult, None, (Ok, rck), (A0_k,))
                else:
                    cx.tt(rc[:, 1:2], rc[:, 0:1], nlam, ALU.mult, (rck, sm_k), (rck,))
                    cb_, cbk = comb.next()
                    cx.stt(cb_[:], O[:, 0:VD], rc[:, 1:2], A0[:, qs, :], ALU.mult, ALU.add,
                           (Ok, rck, A0_k), (cbk,))
                    j = nssq
                    nssq += 1
                    cx.act(junk[:], cb_[:], AF.Square, (cbk,), (junk_k, ssq_k), accum=ssq[:, j:j + 1])
                    cx.act(ssq[:, j:j + 1], ssq[:, j:j + 1], AF.Ln, (ssq_k, sm_k), (ssq_k,),
                           bias=epsb, scale=1.0 / 256)
                    cx.act(ssq[:, j:j + 1], ssq[:, j:j + 1], AF.Exp, (ssq_k,), (ssq_k,), scale=-0.5)
                    cx.stt(os_[:, qs, :], cb_[:], ssq[:, j:j + 1], gs[:], ALU.mult, ALU.mult,
                           (cbk, ssq_k, gs_k), (os_k,))
            if c == nmaps - 1:
                ci = qt // 2
                rds = (os_k,) if xo is None else (os_k, xo.tk(ci // 4, ci % 4))
                cx.dma("sp", o_d[qt * 512:(qt + 1) * 512, hl * VD:(hl + 1) * VD].rearrange("(qs p) v -> p qs v", p=128),
                       os_[:], os_k, rds, ())
                if xo is not None and hl == nheads - 1 and qt % 2 == 1:
                    xo.issue(cx, cck, (ci % 4,), gs=(ci // 4,))
                del qtiles[qt]

        LOOK = 1 if diff else 3
        for i in range(min(LOOK, len(tasks))):
            stage_s(i)
        for i in range(len(tasks)):
            if i + LOOK < len(tasks):
                stage_s(i + LOOK)
            stage_av(i)
    cx.s.emit()


TT = 1024


def row_phase(nc, es, layer, x_d, o_d, ident_d, wo_d, gm_d, win_d, wout_d, ex, gsem=None, tag=""):
    cx = Ctx(nc, es, gsem, tag)
    rk = RowKit(cx, ident_d, nss=128, wslots=3, hnslots=1)
    h, _ = cx.sb("h", [128, 8, 2048], F32)
    hk = [Tk(f"h{i}") for i in range(8)]
    aT, aT_k = cx.sb("aT", [128, 16, TT], BF16)
    hid = Slots(cx, "hid", 2, [128, 4, TT], BF16)
    g, g_k = cx.sb("g", [128, 2048], F32)
    ob, ob_k = cx.sb("ob", [128, 2048], BF16)
    rr = Slots(cx, "rr", 2, [128, 512], BF16)
    ost = Slots(cx, "ost", 4, [128, 512], BF16)
    if layer == 0:
        wf, wf_k = cx.sb("wf", [128, 16, 16], BF16)
        cx.dma("pool", wf[:], ex["wf"].rearrange("(kc p) n -> p kc n", p=128), wf_k, (), (wf_k,))
        bfb, bfb_k = cx.sb("bfb", [128, 16], F32)
        cx.dma("sp", bfb[:], ex["bf"], bfb_k, (), (bfb_k,))
        zs = Slots(cx, "zs", 2, [128, 16], F32)
        one, one_k = cx.sb("one", [128, 1], F32)
        cx.memset(one[:], 1.0, (one_k,))
    else:
        fin = Slots(cx, "fin", 1, [128, 2048], F32)

    cck = Tk("cc")
    xis_ = ex.get("xis") or (None, None, None)
    for rt in range(TOK // TT):
        r0 = rt * TT
        for tb in range(8):
            cx.dma("sp", h[:, tb, :], x_d[r0 + tb * 128:r0 + (tb + 1) * 128, :], hk[tb], (), (hk[tb],))
        for tb in range(8):
            for gg in range(4):
                rw = r0 + tb * 128
                cx.dma("sp", ob[:, gg * 512:(gg + 1) * 512], o_d[gg, rw // 1024, rw % 1024:rw % 1024 + 128, :],
                       ob_k, (), (ob_k,))
            rk.transpose_in(ob, ob_k, aT, aT_k, tb * 128)
        for wb in range(4):
            wt, wk = rk.load_w(wo_d, wb * 512, 512)
            if layer == 0 and wb == 1 and rt > 0 and ex.get("xis") is not None:
                for xi in ex["xis"]:
                    xi.issue(cx, cck, (rt - 1,), defer=rk)

            def cb(tb, pa, pa_k, wb=wb):
                hs = h[:, tb, wb * 512:(wb + 1) * 512]
                cx.tt(hs, hs, pa[:], ALU.add, (pa_k, hk[tb]), (hk[tb],))
            rk.lin_tm(aT, aT_k, TT, wt, wk, 512, cb)
        cx.dma("sp", g[:], gm_d, g_k, (), (g_k,))
        for tb in range(8):
            rk.norm_T(h[:, tb, :], hk[tb], g, g_k, aT, aT_k, tb * 128)
        for hb in range(DFF // 512):
            wt, wk = rk.load_w(win_d, hb * 512, 512)
            hd, hd_k = hid.next()

            def cb(oc, tt, pa, pa_k, hd=hd, hd_k=hd_k):
                r_, rk_ = rr.next()
                cx.act(r_[:], pa[:], AF.Relu, (pa_k,), (rk_,))
                cx.tt(hd[:, oc, tt * 512:(tt + 1) * 512], r_[:], r_[:], ALU.mult, (rk_,), (hd_k,))
            rk.lin_fm(aT, aT_k, TT, wt, wk, 512, cb)
            wt, wk = rk.load_w(wout_d, 0, 2048, nk=4, k0=hb * 4)
            wv = wt
            for tb in range(8):
                for cg in range(4):
                    pa, pa_k = rk.pacc.next()
                    for kc in range(4):
                        cx.mm(pa[:], hd[:, kc, tb * 128:(tb + 1) * 128], wv[:, kc, cg * 512:(cg + 1) * 512],
                              kc == 0, kc == 3, (hd_k, wk), (pa_k,))
                    hs = h[:, tb, cg * 512:(cg + 1) * 512]
                    cx.tt(hs, hs, pa[:], ALU.add, (pa_k, hk[tb]), (hk[tb],))
        if layer == 0:
            for tb in range(8):
                cx.dma("sp", ex["h_out"][r0 + tb * 128:r0 + (tb + 1) * 128, :], h[:, tb, :], hk[tb], (hk[tb],), ())
            cx.dma("sp", g[:], ex["kvg"], g_k, (), (g_k,))
            for tb in range(8):
                rk.norm_T(h[:, tb, :], hk[tb], g, g_k, aT, aT_k, tb * 128)
            for tb in range(8):
                pa, pa_k = rk.pacc.next()
                for kc in range(16):
                    cx.mm(pa[:, 0:16], aT[:, kc, tb * 128:(tb + 1) * 128], wf[:, kc, :], kc == 0, kc == 15,
                          (aT_k, wf_k), (pa_k,))
                z, zk = zs.next()
                cx.tt(z[:], pa[:, 0:16], bfb[:], ALU.add, (pa_k, bfb_k), (zk,))
                cx.act(z[:], z[:], AF.Exp, (zk,), (zk,), scale=-1.0)
                cx.act(z[:], z[:], AF.Ln, (zk, one_k), (zk,), bias=one[:, 0:1])
                cx.ts(z[:], z[:], -1.0, None, ALU.mult, None, (zk,), (zk,))
                for gg in range(4):
                    cx.dma("sp", ex["logf"][gg, r0 + tb * 128:r0 + (tb + 1) * 128, :], z[:, gg * 4:(gg + 1) * 4],
                           zk, (zk,), ())
            def early_kv(si, wb, rt=rt):
                if rt == 3 and xis_[0] is not None and si == 1 and wb == 1:
                    xis_[1].issue(cx, cck, (3,), defer=rk)

            def early_q(si, wb, rt=rt):
                if rt == 3 and xis_[0] is not None and wb == 1:
                    xis_[2].issue(cx, cck, (3,), defer=rk)
            proj_qkv(cx, rk, aT, aT_k, TT, r0, ost,
                     [("fm", ex["wk"], 0, ex["k2T"], xis_[1]), ("tm", ex["wv"], 0, ex["v2"], xis_[2])], early_kv)
            cx.dma("sp", g[:], ex["qg"], g_k, (), (g_k,))
            for tb in range(8):
                rk.norm_T(h[:, tb, :], hk[tb], g, g_k, aT, aT_k, tb * 128)
            proj_qkv(cx, rk, aT, aT_k, TT, r0, ost, [("fm", ex["wq"], 0, ex["q2T"], xis_[0])], early_q)
        else:
            cx.dma("sp", g[:], ex["fg"], g_k, (), (g_k,))
            for tb in range(8):
                f_, fk = fin.next()
                rs = rk.norm_stats(h[:, tb, :], hk[tb])
                cx.stt(f_[:], h[:, tb, :], rs, g[:], ALU.mult, ALU.mult, (hk[tb], rk.rs_k, g_k), (fk,))
                cx.dma("sp", ex["out"][r0 + tb * 128:r0 + (tb + 1) * 128, :], f_[:], fk, (fk,), ())
    rk.drain()
    cx.s.emit()


def _t5_bucket_np(rel):
    half, max_exact = 16, 8
    ret = np.where(rel > 0, half, 0)
    n = np.abs(rel)
    nf = np.maximum(n, 1).astype(np.float32)
    large = max_exact + (np.log(nf / np.float32(max_exact)) / np.float32(math.log(128 / max_exact))
                         * np.float32(half - max_exact)).astype(np.int32)
    large = np.minimum(large, half - 1)
    return ret + np.where(n < max_exact, n, large)


def _bc(v, n=128):
    v = np.asarray(v, np.float32).reshape(1, -1)
    return np.ascontiguousarray(np.broadcast_to(v, (n, v.shape[1])))


def _a2a(outs, key, b):
    return [np.ascontiguousarray(np.stack([outs[b * 4 + r][key][g] for r in range(4)])) for g in range(4)]


def _launch(build, in_maps):
    nc = bass.Bass("TRN2", target_bir_lowering=False)
    with ExitStack() as es:
        build(nc, es)
    res = run_bass_kernel_spmd(nc, in_maps, core_ids=list(range(NCORE)))
    return res.results


def _din(nc, name, shape, dt=F32):
    return nc.dram_tensor(name, list(shape), dt, kind="ExternalInput").ap()


def _dout(nc, name, shape, dt=F32):
    return nc.dram_tensor(name, list(shape), dt, kind="ExternalOutput").ap()


IDENT = np.eye(128, dtype=np.float32).astype(ml_dtypes.bfloat16)


def build_a(nc, es):
    phase_a(nc, es, _din(nc, "x", [TOK, D]), _din(nc, "g", [128, D]), _din(nc, "wqkv", [D, 3 * D]),
            _din(nc, "ident", [128, 128], BF16), _dout(nc, "qT", [16, 128, TOK], BF16),
            _dout(nc, "kT", [16, 128, TOK], BF16), _dout(nc, "v", [4, TOK, 512], BF16))


def build_attn(mode):
    def build(nc, es):
        aux = {}
        if mode == "diff":
            aux["bias"] = _din(nc, "bias", [128, 2, 640])
            aux["subg"] = _din(nc, "subg", [128, 256])
            aux["lam"] = _din(nc, "lam", [128, 4, 128])
        else:
            aux["trimask"] = _din(nc, "trimask", [128, 128], BF16)
            aux["cmats"] = _din(nc, "cmats", [128, 3, 128])
            aux["logf"] = _din(nc, "logf", [4, TOK, 4])
        attention(nc, es, mode, _din(nc, "qT", [4, 4, 128, TOK], BF16), _din(nc, "kT", [4, 4, 128, TOK], BF16),
                  _din(nc, "v", [4, TOK, 512], BF16), _dout(nc, "o", [S, 512], BF16), aux)
    return build


def build_row(layer):
    def build(nc, es):
        ex = {}
        if layer == 0:
            for nm in ("kvg", "qg"):
                ex[nm] = _din(nc, nm, [128, D])
            for nm in ("wk", "wv", "wq"):
                ex[nm] = _din(nc, nm, [D, D])
            ex["wf"] = _din(nc, "wf", [D, 16])
            ex["bf"] = _din(nc, "bf", [128, 16])
            ex["h_out"] = _dout(nc, "h_out", [TOK, D])
            ex["q2T"] = _dout(nc, "q2T", [16, 128, TOK], BF16)
            ex["k2T"] = _dout(nc, "k2T", [16, 128, TOK], BF16)
            ex["v2"] = _dout(nc, "v2", [4, TOK, 512], BF16)
            ex["logf"] = _dout(nc, "logf", [4, TOK, 4])
        else:
            ex["fg"] = _din(nc, "fg", [128, D])
            ex["out"] = _dout(nc, "out", [TOK, D])
        row_phase(nc, es, layer, _din(nc, "x", [TOK, D]), _din(nc, "o", [4, TOK, 512], BF16),
                  _din(nc, "ident", [128, 128], BF16), _din(nc, "wo", [D, D]), _din(nc, "gm", [128, D]),
                  _din(nc, "win", [D, DFF]), _din(nc, "wout", [DFF, D]), ex)
    return build


I32 = mybir.dt.int32


class SemPool:
    def __init__(self, stack):
        self.stack = stack
        self.free = []
        self.regs = []


GROUPS = [[0, 1, 2, 3], [4, 5, 6, 7]]


FOXH = 2
CHK = 524288


def comm_phase(nc, es, gsem, tag, cid_d, items):
    cx = Ctx(nc, es, gsem, tag)
    cid, cid_k = cx.sb("cid", [1, 8], I32)
    cx.dma("sp", cid[:], cid_d, cid_k, (), (cid_k,))
    if not gsem.regs:
        gsem.regs = [gsem.stack.enter_context(nc.gpsimd.register(f"cr{i}")) for i in range(6)]
        for i in range(6):
            o = cx.s.op("pool", lambda e, i=i: e.reg_load(gsem.regs[i], cid[:1, i:i + 1]), (cid_k,), ())
            o.nosig = True
    regs = gsem.regs
    cck = Tk("cc")
    inner = [[8192, CHK // 8192], [1, 8192]]
    for n, it in enumerate(items):
        gk, dk = Tk(f"g{n}"), Tk(f"l{n}")
        if isinstance(it, XItem):
            it.tks = {}
            it.issue(cx, cck, (0, 1, 2, 3))
            gall, dst = it.gall, it.dst
            cx.s.op("pool", lambda e, dst=dst, gall=gall: e.dma_start(
                out=bass.AP(dst, 0, [[32768, 16 * CHK // 32768], [1, 32768]]),
                in_=bass.AP(gall, regs[0], [[32768, 16 * CHK // 32768], [1, 32768]])),
                tuple(it.tks.values()), (dk,), dma=dk)
            it.reset()
        else:
            _, src, gath, dst, ri, rstride, nel = it
            cx.s.op("pool", lambda e, src=src, gath=gath: e.collective_compute(
                "AllGather", ALU.bypass, replica_groups=GROUPS, ins=[src.ap().opt()], outs=[gath.ap().opt()]),
                (), (gk,), dma=cck, inc=1)
            cx.s.op("pool", lambda e, dst=dst, gath=gath, ri=ri, rstride=rstride, nel=nel: e.dma_start(
                out=bass.AP(dst, 0, [[nel, 4], [1, nel]]), in_=bass.AP(gath, regs[ri], [[rstride, 4], [1, nel]])),
                (gk,), (dk,), dma=dk)
    cx.s.emit()


def build_fused(nc, upto=9):
    di = lambda name, shape, dt=F32: nc.dram_tensor(name, list(shape), dt, kind="ExternalInput").ap()
    x_d = di("x", [TOK, D])
    ident_d = di("ident", [128, 128], BF16)
    cid_d = di("cid", [1, 8], I32)
    g_attn0, g_attn1, g_mlp0, g_mlp1, g_kv, g_fin = [di(n, [128, D]) for n in
                                                     ("g_attn0", "g_attn1", "g_mlp0", "g_mlp1", "g_kv", "g_fin")]
    wqkv = di("wqkv", [D, 3 * D])
    wo0, wo1, wk, wv, wq = [di(n, [D, D]) for n in ("wo0", "wo1", "wk", "wv", "wq")]
    win0, win1 = di("win0", [D, DFF]), di("win1", [D, DFF])
    wout0, wout1 = di("wout0", [DFF, D]), di("wout1", [DFF, D])
    wf, bf = di("wf", [D, 16]), di("bf", [128, 16])
    aux0 = {"bias": di("bias", [128, 2, 640]), "subg": di("subg", [128, 256]), "lam": di("lam", [128, 4, 128])}
    trimask, cmats = di("trimask", [128, 128], BF16), di("cmats", [128, 3, 128])
    out_d = nc.dram_tensor("out", [TOK, D], F32, kind="ExternalOutput").ap()
    dt_ = lambda name, shape, dt=BF16: nc.dram_tensor(name, list(shape), dt)
    qT, kT, v = dt_("i_qT", [8192, 1024]), dt_("i_kT", [8192, 1024]), dt_("i_v", [4 * TOK, 512])
    Gq, Gk, Gv = dt_("i_Gq", [8192, TOK]), dt_("i_Gk", [8192, TOK]), dt_("i_Gv", [16 * TOK, 512])
    Lq, Lk, Lv = dt_("i_Lq", [8192, 1024]), dt_("i_Lk", [8192, 1024]), dt_("i_Lv", [4 * TOK, 512])
    o, Lo = dt_("i_o", [S, 512]), dt_("i_Lo", [4 * TOK, 512])
    h1 = dt_("i_h1", [TOK, D], F32)
    logf, Glogf, Llogf = dt_("i_logf", [4 * TOK, 4], F32), dt_("i_Glogf", [16 * TOK, 4], F32), dt_("i_Llogf", [4 * TOK, 4], F32)
    u3 = lambda t: t.ap().rearrange("(g rt ul d) t -> g rt ul d t", g=4, rt=4, ul=4)
    r4 = lambda t: t.ap().rearrange("(rt r ul d) t -> rt r ul d t", rt=4, r=4, ul=4)
    xq, xk_, xv = XItem(qT, Gq, Lq), XItem(kT, Gk, Lk), XItem(v, Gv, Lv)
    xo = XItem(o, Gv, Lo)
    c4 = lambda t: t.ap().rearrange("(c r i) v -> r c i v", c=4, r=4)
    g3 = lambda t: t.ap().rearrange("(g t) v -> g t v", g=4)
    LG = ("small", logf, Glogf, Llogf, 1, 4 * TOK * 4, TOK * 4)
    with ExitStack() as gstack:
        gsem = SemPool(gstack)
        if upto > 0:
            with ExitStack() as es:
                phase_a(nc, es, x_d, g_attn0, wqkv, ident_d, u3(qT), u3(kT), g3(v), gsem, "A_", (xq, xk_, xv))
        if upto > 1:
            with ExitStack() as es:
                comm_phase(nc, es, gsem, "X1_", cid_d, [xq, xk_, xv])
        if upto > 2:
            with ExitStack() as es:
                attention(nc, es, "diff", r4(Lq), r4(Lk), c4(Lv), o.ap(), aux0, gsem, "B1_", xo)
        if upto > 3:
            with ExitStack() as es:
                comm_phase(nc, es, gsem, "X2_", cid_d, [xo])
        if upto > 4:
            with ExitStack() as es:
                ex = {"kvg": g_kv, "qg": g_attn1, "wk": wk, "wv": wv, "wq": wq, "wf": wf, "bf": bf,
                      "h_out": h1.ap(), "q2T": u3(qT), "k2T": u3(kT), "v2": g3(v), "logf": g3(logf), "xis": (xq, xk_, xv)}
                row_phase(nc, es, 0, x_d, c4(Lo), ident_d, wo0, g_mlp0, win0, wout0, ex, gsem, "B2_")
        if upto > 5:
            with ExitStack() as es:
                comm_phase(nc, es, gsem, "X3_", cid_d, [xq, xk_, xv, LG])
        if upto > 6:
            with ExitStack() as es:
                aux1 = {"trimask": trimask, "cmats": cmats, "logf": g3(Llogf)}
                attention(nc, es, "fox", r4(Lq), r4(Lk), c4(Lv), o.ap(), aux1, gsem, "C1_", xo)
        if upto > 7:
            with ExitStack() as es:
                comm_phase(nc, es, gsem, "X4_", cid_d, [xo])
        if upto > 8:
            with ExitStack() as es:
                row_phase(nc, es, 1, h1.ap(), c4(Lo), ident_d, wo1, g_mlp1, win1, wout1,
                          {"fg": g_fin, "out": out_d}, gsem, "C2_")
        if upto < 9:
            with ExitStack() as es:
                cx = Ctx(nc, es, gsem, "Z_")
                k = Tk("z")
                cx.dma("sp", out_d, x_d, k, (), (k,))
                cx.s.emit()


def kernel(x, rel_bias_table, attn_norm_g, mlp_norm_g, w_qkv_a, lam_q1, lam_k1, lam_q2, lam_k2,
           subln_g, w_o_a, kv_norm_g, w_k_b, w_v_b, w_f_b, b_f_b, w_q_b, w_o_b, w_mlp_in, w_mlp_out,
           final_norm_g, _upto=9):
    f32 = lambda a: np.ascontiguousarray(np.asarray(a, np.float32))
    x = f32(x)
    kk = np.arange(128)[:, None]
    jj = np.arange(640)[None, :]
    bias_all = f32(rel_bias_table)[_t5_bucket_np(kk - jj)]
    tri = (np.arange(128)[None, :] >= np.arange(128)[:, None]).astype(np.float32)
    cm = np.zeros((128, 3, 128), np.float32)
    cm[:, 0, :] = tri
    cm[127, 1, :] = 1.0
    cm[0, 2, :] = 1.0
    shared = {
        "ident": IDENT, "g_attn0": _bc(attn_norm_g[0]), "g_attn1": _bc(attn_norm_g[1]),
        "g_mlp0": _bc(mlp_norm_g[0]), "g_mlp1": _bc(mlp_norm_g[1]), "g_kv": _bc(kv_norm_g),
        "g_fin": _bc(final_norm_g), "wqkv": f32(w_qkv_a[0]), "wo0": f32(w_o_a[0]), "wo1": f32(w_o_b[0]),
        "wk": f32(w_k_b), "wv": f32(w_v_b), "wq": f32(w_q_b[0]), "win0": f32(w_mlp_in[0]),
        "win1": f32(w_mlp_in[1]), "wout0": f32(w_mlp_out[0]), "wout1": f32(w_mlp_out[1]),
        "wf": f32(w_f_b), "bf": _bc(b_f_b), "subg": _bc(subln_g[0]),
        "lam": np.ascontiguousarray(np.broadcast_to(
            np.stack([f32(lam_q1[0]), f32(lam_k1[0]), f32(lam_q2[0]), f32(lam_k2[0])])[None], (128, 4, 128))),
        "trimask": tri.astype(ml_dtypes.bfloat16), "cmats": cm,
    }
    in_maps = []
    for c in range(NCORE):
        b, g = c // 4, c % 4
        m = dict(shared)
        m["x"] = np.ascontiguousarray(x[b, g * TOK:(g + 1) * TOK])
        m["bias"] = np.ascontiguousarray(bias_all[:, :, 2 * g:2 * g + 2].transpose(0, 2, 1))
        cid = np.zeros((1, 8), np.int32)
        cid[0, 0] = g * 16 * CHK
        for r in range(4):
            cid[0, 2 + r] = g * 16 * CHK + r * CHK
        cid[0, 1] = g * TOK * 4
        m["cid"] = cid
        in_maps.append(m)
    nc = bass.Bass("TRN2", target_bir_lowering=False)
    build_fused(nc, _upto)
    res = run_bass_kernel_spmd(nc, in_maps, core_ids=list(range(NCORE))).results
    out = np.empty((NB, S, D), np.float32)
    for c in range(NCORE):
        out[c // 4, (c % 4) * TOK:(c % 4 + 1) * TOK] = res[c]["out"]
    return out
```

```python
import math
from contextlib import ExitStack

import numpy as np
import ml_dtypes
import concourse.bass as bass
import concourse.mybir as mybir
from concourse.bass_utils import run_bass_kernel_spmd

F32 = mybir.dt.float32
BF16 = mybir.dt.bfloat16
AF = mybir.ActivationFunctionType
ALU = mybir.AluOpType
AX = mybir.AxisListType

D = 2048
S = 16384
NB = 2
DFF = 8192
NCORE = 8
TOK = 4096
CH = 2048
NEG = -30000.0
SCALE = 128 ** -0.5
EPS = 1e-6
SUBEPS = 1e-5
LAMBDA_INIT = 0.8 - 0.6 * math.exp(-0.3 * 0)


class Tk:
    __slots__ = ("name", "w", "r", "semcnt")

    def __init__(self, name):
        self.name = name
        self.w = None
        self.r = []
        self.semcnt = 0


class Op:
    __slots__ = ("eng", "fn", "deps", "needed", "is_dma", "tk", "val", "key", "inc", "nosig")


ENGS = ("pe", "act", "dve", "pool", "sp")
BLK = {"pe": "tensor", "act": "scalar", "dve": "vector", "pool": "gpsimd", "sp": "sync"}


class Sched:
    def __init__(self, nc, gsem=None, tag=""):
        self.nc = nc
        self.gsem = gsem
        self.tag = tag
        self.ops = {e: [] for e in ENGS}
        self.all = []

    def op(self, eng, fn, reads=(), writes=(), dma=None, inc=16):
        o = Op()
        o.inc = inc
        o.nosig = False
        o.eng = eng
        o.fn = fn
        o.needed = False
        o.is_dma = dma is not None
        o.tk = dma
        o.val = 0
        o.key = None
        deps = []
        for t in reads:
            if t.w is not None:
                deps.append(t.w)
        for t in writes:
            if t.w is not None:
                deps.append(t.w)
            deps.extend(t.r)
        o.deps = deps
        for d in deps:
            d.needed = True
        for t in reads:
            t.r.append(o)
        for t in writes:
            t.w = o
            t.r = []
        self.ops[eng].append(o)
        self.all.append(o)
        return o

    def emit(self):
        nc = self.nc
        engcnt = {e: 0 for e in ENGS}
        keys = {}
        for o in self.all:
            if o.is_dma:
                o.tk.semcnt += o.inc
                o.val = o.tk.semcnt
                o.key = ("t", id(o.tk))
                keys[o.key] = "d_" + o.tk.name
            elif o.needed:
                engcnt[o.eng] += 1
                o.val = engcnt[o.eng]
                o.key = ("e", o.eng)
                keys[o.key] = "e_" + o.eng
        if self.gsem is not None:
            for e in ENGS:
                for o in reversed(self.ops[e]):
                    if not o.is_dma and not o.nosig:
                        if not o.needed:
                            o.needed = True
                            engcnt[e] += 1
                            o.val = engcnt[e]
                            o.key = ("e", e)
                            keys[o.key] = "e_" + e
                        break
        final = {}
        for o in self.all:
            if o.key is not None:
                final[o.key] = max(final.get(o.key, 0), o.val)
        with ExitStack() as es:
            sems, base = {}, {}
            for k, n in keys.items():
                if self.gsem is None:
                    sems[k], base[k] = es.enter_context(nc.semaphore(self.tag + n)), 0
                elif self.gsem.free:
                    sems[k], base[k] = self.gsem.free.pop()
                else:
                    sems[k], base[k] = self.gsem.stack.enter_context(nc.semaphore(self.tag + n)), 0
            block = es.enter_context(nc.Block())
            for en in ENGS:
                def body(eng, en=en):
                    waited = {}
                    for o in self.ops[en]:
                        for d in o.deps:
                            if en == "pe" and d.eng == "pe" and not d.is_dma:
                                continue
                            if waited.get(d.key, 0) >= d.val:
                                continue
                            eng.wait_ge(sems[d.key], base[d.key] + d.val)
                            waited[d.key] = d.val
                        ins = o.fn(eng)
                        if o.is_dma:
                            ins.then_inc(sems[o.key], o.inc)
                        elif o.needed:
                            ins.then_inc(sems[o.key], 1)
                    if en == "sp" or self.gsem is not None:
                        for k, v in final.items():
                            if waited.get(k, 0) < v:
                                eng.wait_ge(sems[k], base[k] + v)
                getattr(block, BLK[en])(body)
            if self.gsem is not None:
                for k in keys:
                    self.gsem.free.append((sems[k], base[k] + final[k]))


class Ctx:
    def __init__(self, nc, es, gsem=None, tag=""):
        self.nc = nc
        self.es = es
        self.tag = tag
        self.s = Sched(nc, gsem, tag)
        self.rr = 0

    def sb(self, name, shape, dt):
        t = self.es.enter_context(self.nc.sbuf_tensor("sb_" + self.tag + name, list(shape), dt))
        return t, Tk(name)

    def ps(self, name, shape, dt=F32):
        t = self.es.enter_context(self.nc.psum_tensor("ps_" + self.tag + name, list(shape), dt))
        return t, Tk(name)

    def dma(self, q, out, in_, tk, reads=(), writes=()):
        return self.s.op(q, lambda e: e.dma_start(out=out, in_=in_), reads, writes, dma=tk)

    def mm(self, out, lhsT, rhs, start, stop, reads, writes):
        return self.s.op("pe", lambda e: e.matmul(out, lhsT, rhs, start=start, stop=stop),
                         reads, writes)

    def tr(self, out, in_, ident, reads, writes):
        return self.s.op("pe", lambda e: e.transpose(out, in_, ident), reads, writes)

    def act(self, out, in_, func, reads, writes, bias=0.0, scale=1.0, accum=None):
        if accum is None:
            return self.s.op("act", lambda e: e.activation(out, in_, func, bias=bias, scale=scale),
                             reads, writes)
        return self.s.op("act", lambda e: e.activation(out, in_, func, bias=bias, scale=scale,
                                                       accum_out=accum), reads, writes)

    def copy(self, out, in_, reads, writes, eng=None):
        if eng is None:
            self.rr ^= 1
            eng = "act" if self.rr else "dve"
        if eng == "act":
            return self.s.op("act", lambda e: e.copy(out, in_), reads, writes)
        return self.s.op(eng, lambda e: e.tensor_copy(out, in_), reads, writes)

    def ts(self, out, in0, s1, s2, op0, op1, reads, writes, eng="dve"):
        if op1 is None:
            return self.s.op(eng, lambda e: e.tensor_scalar(out, in0, s1, s2, op0), reads, writes)
        return self.s.op(eng, lambda e: e.tensor_scalar(out, in0, s1, s2, op0, op1), reads, writes)

    def stt(self, out, in0, scalar, in1, op0, op1, reads, writes, eng="dve"):
        return self.s.op(eng, lambda e: e.scalar_tensor_tensor(out, in0, scalar, in1, op0, op1),
                         reads, writes)

    def tt(self, out, in0, in1, op, reads, writes, eng="dve"):
        return self.s.op(eng, lambda e: e.tensor_tensor(out, in0, in1, op), reads, writes)

    def memset(self, ap, val, writes, eng="dve"):
        return self.s.op(eng, lambda e: e.memset(ap, val), (), writes)


class Slots:
    def __init__(self, cx, name, n, shape, dt, psum=False):
        self.items = []
        for i in range(n):
            self.items.append(cx.ps(f"{name}{i}", shape, dt) if psum else cx.sb(f"{name}{i}", shape, dt))
        self.i = 0

    def next(self):
        it = self.items[self.i % len(self.items)]
        self.i += 1
        return it


class WKeys:
    def __init__(self, tks, st):
        self.tks, self.st = tks, st

    def for_kc(self, kc):
        return self.tks[kc // self.st]


class RowKit:
    def __init__(self, cx, ident_d, nss=64, wslots=3, hnslots=2):
        self.cx = cx
        self.ident, self.ident_k = cx.sb("ident", [128, 128], BF16)
        cx.dma("sp", self.ident[:], ident_d, self.ident_k, (), (self.ident_k,))
        self.w = Slots(cx, "w", wslots, [128, 8192], BF16)
        self.pacc = Slots(cx, "pacc", 4, [128, 512], F32, psum=True)
        self.ptr = Slots(cx, "ptr", 2, [128, 1024], BF16, psum=True)
        self.hn = Slots(cx, "hn", hnslots, [128, 2048], BF16)
        self.ss, self.ss_k = cx.sb("ss", [128, nss], F32)
        self.rs, self.rs_k = cx.sb("rs", [128, nss], F32)
        cx.memset(self.ss[:], 0.0, (self.ss_k,))
        self.epsb, self.eps_k = cx.sb("epsb", [128, 2], F32)
        cx.memset(self.epsb[:, 0:1], EPS, (self.eps_k,))
        cx.memset(self.epsb[:, 1:2], SUBEPS, (self.eps_k,))
        self.eps_main = self.epsb[:, 0:1]
        self.eps_sub = self.epsb[:, 1:2]
        self.nss = 0
        self.wsub = {}
        self.pending = []
        self.per_load = 1

    def drain(self, n=None):
        while self.pending and (n is None or n > 0):
            self.pending.pop(0)()
            if n is not None:
                n -= 1

    def norm_stats(self, x_ap, x_k, junk=None, junk_k=None):
        cx = self.cx
        j = self.nss
        self.nss += 1
        ss = self.ss[:, j:j + 1]
        rs = self.rs[:, j:j + 1]
        if junk is None:
            junk, junk_k = self.hn.next()
        cx.act(junk[:], x_ap, AF.Square, (x_k,), (junk_k, self.ss_k), accum=ss)
        cx.act(rs, ss, AF.Ln, (self.ss_k, self.eps_k), (self.rs_k,), bias=self.eps_main, scale=1.0 / D)
        cx.act(rs, rs, AF.Exp, (self.rs_k,), (self.rs_k,), scale=-0.5)
        return rs

    def norm_T(self, x_ap, x_k, g, g_k, dstT, dstT_k, tcol):
        cx = self.cx
        hn, hn_k = self.hn.next()
        rs = self.norm_stats(x_ap, x_k, hn, hn_k)
        cx.stt(hn[:], x_ap, rs, g[:], ALU.mult, ALU.mult, (x_k, self.rs_k, g_k), (hn_k,))
        self.transpose_in(hn, hn_k, dstT, dstT_k, tcol)

    def transpose_in(self, src, src_k, dstT, dstT_k, tcol, nchunk=16, c0=0):
        cx = self.cx
        for half in range(0, nchunk, 8):
            n = min(8, nchunk - half)
            pt, pt_k = self.ptr.next()
            for i in range(n):
                cx.tr(pt[:, i * 128:(i + 1) * 128], src[:, (half + i) * 128:(half + i + 1) * 128],
                      self.ident[:], (src_k, self.ident_k), (pt_k,))
            cx.copy(dstT[:, c0 + half:c0 + half + n, tcol:tcol + 128],
                    pt[:, 0:n * 128].rearrange("p (c t) -> p c t", t=128), (pt_k,), (dstT_k,))

    def load_w(self, w_d, c0, ncols, nk=16, k0=0):
        cx = self.cx
        wt, wk = self.w.next()
        if id(wk) not in self.wsub:
            self.wsub[id(wk)] = [Tk(wk.name + f"_{i}") for i in range(4)]
        subs = self.wsub[id(wk)]
        wv_s = wt[:, 0:nk * ncols].rearrange("p (k n) -> p k n", n=ncols)
        wv = w_d.rearrange("(kc p) n -> p kc n", p=128)
        st = max(1, nk // 4)
        for i, q in enumerate(range(0, nk, st)):
            cx.dma("pool", wv_s[:, q:q + st, :], wv[:, k0 + q:k0 + q + st, c0:c0 + ncols], subs[i], (), (subs[i],))
        self.drain(self.per_load)
        return wv_s, WKeys(subs, st)

    def lin_fm(self, srcT, srcT_k, ntok, wt, wk, ncols, cb, nk=16):
        cx = self.cx
        for oc in range(ncols // 128):
            for tt in range(ntok // 512):
                pa, pa_k = self.pacc.next()
                for kc in range(nk):
                    cx.mm(pa[:], wt[:, kc, oc * 128:(oc + 1) * 128], srcT[:, kc, tt * 512:(tt + 1) * 512],
                          kc == 0, kc == nk - 1, (wk.for_kc(kc), srcT_k), (pa_k,))
                cb(oc, tt, pa, pa_k)

    def lin_tm(self, srcT, srcT_k, ntok, wt, wk, ncols, cb, nk=16):
        cx = self.cx
        for tb in range(ntok // 128):
            pa, pa_k = self.pacc.next()
            for kc in range(nk):
                cx.mm(pa[:, 0:ncols], srcT[:, kc, tb * 128:(tb + 1) * 128], wt[:, kc, 0:ncols],
                      kc == 0, kc == nk - 1, (wk.for_kc(kc), srcT_k), (pa_k,))
            cb(tb, pa, pa_k)


class XItem:
    def __init__(self, src, gall, dst):
        self.src, self.gall, self.dst = src, gall, dst
        self.issued = set()
        self.tks = {}

    def reset(self):
        self.issued = set()
        self.tks = {}

    def tk(self, g, rt):
        if (g, rt) not in self.tks:
            self.tks[(g, rt)] = Tk(f"x{g}{rt}")
        return self.tks[(g, rt)]

    def issue(self, cx, cck, rts, gs=(0, 1, 2, 3), defer=None):
        for rt in rts:
            for g in gs:
                ci = 4 * g + rt
                if ci in self.issued:
                    continue
                self.issued.add(ci)
                src, gall = self.src, self.gall

                def rec(src=src, gall=gall, ci=ci, tk=self.tk(g, rt)):
                    cx.s.op("pool", lambda e: e.collective_compute(
                        "AllGather", ALU.bypass, replica_groups=GROUPS,
                        ins=[bass.AP(src, ci * CHK, [[8192, CHK // 8192], [1, 8192]])],
                        outs=[bass.AP(gall, ci * 4 * CHK, [[8192, 4 * CHK // 8192], [1, 8192]])]),
                        (), (tk,), dma=cck, inc=1)
                if defer is None:
                    rec()
                else:
                    defer.pending.append(rec)


def proj_qkv(cx, rk, hT, hT_k, ntok, t0, ost, specs, after_load=None):
    for si, (kind, w_d, col0, dst, xi) in enumerate(specs):
        for wb in range(4):
            wt, wk = rk.load_w(w_d, col0 + wb * 512, 512)
            if after_load is not None:
                after_load(si, wb)
            if kind == "fm":
                def cb(oc, tt, pa, pa_k, dst=dst, wb=wb, xi=xi):
                    o, ok = ost.next()
                    cx.copy(o[:], pa[:], (pa_k,), (ok,))
                    tok = t0 + tt * 512
                    rds = (ok,) if xi is None else (ok, xi.tk(wb, tok // 1024))
                    cx.dma("sp", dst[wb, tok // 1024, oc, :, tok % 1024:tok % 1024 + 512], o[:], ok, rds, ())
                rk.lin_fm(hT, hT_k, ntok, wt, wk, 512, cb)
            else:
                def cb(tb, pa, pa_k, dst=dst, wb=wb, xi=xi):
                    o, ok = ost.next()
                    cx.copy(o[:], pa[:], (pa_k,), (ok,))
                    tok = t0 + tb * 128
                    rds = (ok,) if xi is None else (ok, xi.tk(wb, tok // 1024))
                    cx.dma("sp", dst[wb, tok:tok + 128, :], o[:], ok, rds, ())
                rk.lin_tm(hT, hT_k, ntok, wt, wk, 512, cb)


def phase_a(nc, es, x_d, g_d, wqkv_d, ident_d, qT_d, kT_d, v_d, gsem=None, tag="", xis=(None, None, None)):
    cx = Ctx(nc, es, gsem, tag)
    rk = RowKit(cx, ident_d)
    rk.per_load = 2
    g, g_k = cx.sb("g", [128, 2048], F32)
    cx.dma("sp", g[:], g_d, g_k, (), (g_k,))
    xs = Slots(cx, "x", 2, [128, 2048], F32)
    hT, hT_k = cx.sb("hT", [128, 16, CH], BF16)
    ost = Slots(cx, "ost", 4, [128, 512], BF16)
    cck = Tk("cc")
    for half in range(2):
        t0 = half * CH
        if half == 1 and xis[0] is not None:
            for xi in xis:
                xi.issue(cx, cck, (0, 1), defer=rk)
        for tb in range(CH // 128):
            xt, xk = xs.next()
            cx.dma("sp", xt[:], x_d[t0 + tb * 128:t0 + (tb + 1) * 128, :], xk, (), (xk,))
            rk.norm_T(xt[:], xk, g, g_k, hT, hT_k, tb * 128)
        def early(si, wb, half=half):
            if half == 1 and xis[0] is not None and wb == 1 and si in (1, 2):
                xis[si - 1].issue(cx, cck, (2, 3), defer=rk)
        proj_qkv(cx, rk, hT, hT_k, CH, t0, ost,
                 [("fm", wqkv_d, 0, qT_d, xis[0]), ("fm", wqkv_d, 2048, kT_d, xis[1]),
                  ("tm", wqkv_d, 4096, v_d, xis[2])], early)
    rk.drain()
    cx.s.emit()


def attention(nc, es, mode, qT_d, kT_d, v_d, o_d, aux, gsem=None, tag="", xo=None):
    cx = Ctx(nc, es, gsem, tag)
    cck = Tk("cc")
    diff = mode == "diff"
    VD = 256 if diff else 128
    nheads = 2 if diff else 4
    nmaps = 2 if diff else 1
    NKB = S // 128
    KT = Slots(cx, "KT", 2, [128, S], BF16)
    Vt = Slots(cx, "Vt", 1 if diff else 2, [128, NKB, VD + 1], BF16)
    vkeys = {}
    for vt, vk in Vt.items:
        vkeys[id(vk)] = [Tk(vk.name + f"_{i}") for i in range(16)]
        cx.memset(vt[:, :, VD:VD + 1], 1.0, tuple(vkeys[id(vk)]), eng="pool")
    kkeys = {id(kk_): [Tk(kk_.name + f"_{i}") for i in range(16)] for _, kk_ in KT.items}
    QT = Slots(cx, "QT", 2, [128, nmaps, 512], BF16)
    if diff:
        PT = Slots(cx, "PT", 3, [128, 1024], BF16)
        psS = Slots(cx, "psS", 2, [128, 1024], F32, psum=True)
    else:
        PT = Slots(cx, "PT", 5, [128, 512], BF16)
        psS = Slots(cx, "psS", 4, [128, 512], F32, psum=True)
    psO = Slots(cx, "psO", 4, [128, 512], F32, psum=True)
    ostg = Slots(cx, "ostg", 2, [128, 4, VD], BF16)
    sm, sm_k = cx.sb("sm", [128, 16], F32)
    rec = Slots(cx, "rec", 4, [128, 2], F32)
    nssq = 0
    if diff:
        bw, bw_k = cx.sb("bw", [128, 2, 640], F32)
        cx.dma("sp", bw[:], aux["bias"], bw_k, (), (bw_k,))
        cx.memset(bw[64:128, :, 0:64], NEG, (bw_k,))
        tmpS = Slots(cx, "tmpS", 2, [128, 512], F32)
        A0, A0_k = cx.sb("A0", [128, 4, 256], F32)
        comb = Slots(cx, "comb", 2, [128, 256], F32)
        junk, junk_k = cx.sb("junk", [128, 256], BF16)
        gs, gs_k = cx.sb("gs", [128, 256], F32)
        cx.dma("sp", gs[:], aux["subg"], gs_k, (), (gs_k,))
        cx.ts(gs[:], gs[:], 1.0 - LAMBDA_INIT, None, ALU.mult, None, (gs_k,), (gs_k,))
        lamb, lamb_k = cx.sb("lamb", [128, 4, 128], F32)
        cx.dma("sp", lamb[:], aux["lam"], lamb_k, (), (lamb_k,))
        lp, lp_k = cx.sb("lp", [128, 2, 128], F32)
        cx.tt(lp[:, 0, :], lamb[:, 0, :], lamb[:, 1, :], ALU.mult, (lamb_k,), (lp_k,))
        cx.tt(lp[:, 1, :], lamb[:, 2, :], lamb[:, 3, :], ALU.mult, (lamb_k,), (lp_k,))
        cx.s.op("dve", lambda e: e.tensor_reduce(sm[:, 0:2], lp[:], AX.X, ALU.add), (lp_k,), (sm_k,))
        cx.act(sm[:, 2:4], sm[:, 0:2], AF.Exp, (sm_k,), (sm_k,))
        cx.tt(sm[:, 4:5], sm[:, 3:4], sm[:, 2:3], ALU.subtract, (sm_k,), (sm_k,))
        cx.ts(sm[:, 5:6], sm[:, 4:5], -LAMBDA_INIT, None, ALU.add, None, (sm_k,), (sm_k,))
        nlam = sm[:, 5:6]
        cx.memset(sm[:, 6:7], SUBEPS, (sm_k,))
        epsb = sm[:, 6:7]
        ssq, ssq_k = cx.sb("ssq", [128, 2 * 32 * 4], F32)
        cx.memset(ssq[:], 0.0, (ssq_k,))
    else:
        tri, tri_k = cx.sb("tri", [128, 128], BF16)
        cx.dma("sp", tri[:], aux["trimask"], tri_k, (), (tri_k,))
        cm, cm_k = cx.sb("cm", [128, 3, 128], F32)
        cx.dma("sp", cm[:], aux["cmats"], cm_k, (), (cm_k,))
        X, X_k = cx.sb("X", [128, NKB, 4], F32)
        for r in range(4):
            cx.dma("sp", X[:, r * 32:(r + 1) * 32, :],
                   aux["logf"][r].rearrange("(kb p) h -> p kb h", p=128), X_k, (), (X_k,))
        pre, pre_k = cx.sb("pre", [128, NKB * 4], F32)
        sA, sA_k = cx.sb("sA", [128, NKB * 4], F32)
        sB, sB_k = cx.sb("sB", [128, NKB * 4], F32)
        cT, cT_k = cx.sb("cT", [128, NKB, 4], F32)
        crefb, crefb_k = cx.sb("crefb", [128, NKB, 4], F32)
        Xf = X[:].rearrange("p k h -> p (k h)")
        pc, pc_k = psS.next()
        cx.mm(pc[:], cm[:, 0, :], Xf, True, True, (cm_k, X_k), (pc_k,))
        cx.copy(pre[:], pc[:], (pc_k,), (pre_k,), eng="dve")
        pc, pc_k = psS.next()
        cx.mm(pc[:], cm[:, 1, :], pre[:], True, True, (cm_k, pre_k), (pc_k,))
        cx.copy(sA[:], pc[:], (pc_k,), (sA_k,), eng="dve")
        cx.tt(pre[:], pre[:], sA[:], ALU.subtract, (pre_k, sA_k), (pre_k,))
        a, ak, b, bk = sA, sA_k, sB, sB_k
        sft = 4
        while sft < NKB * 4:
            cx.copy(b[:, 0:sft], a[:, 0:sft], (ak,), (bk,), eng="dve")
            cx.tt(b[:, sft:], a[:, sft:], a[:, 0:NKB * 4 - sft], ALU.add, (ak,), (bk,))
            a, ak, b, bk = b, bk, a, ak
            sft *= 2
        cx.tt(cT[:].rearrange("p k h -> p (k h)"), pre[:], a[:], ALU.add, (pre_k, ak), (cT_k,))
        pc, pc_k = psS.next()
        cx.mm(pc[:], cm[:, 2, :], cT[:].rearrange("p k h -> p (k h)"), True, True, (cm_k, cT_k), (pc_k,))
        cx.copy(crefb[:].rearrange("p k h -> p (k h)"), pc[:], (pc_k,), (crefb_k,), eng="dve")
        bcol = Slots(cx, "bcol", 3, [128, 2, NKB], F32)

    for hl in range(nheads):
        vt, vk = Vt.next()
        vk = vkeys[id(vk)]
        for r in range(4):
            for cl in range(4):
                src = v_d[r, cl].rearrange("(kb p) v -> p kb v", p=128)
                cx.dma("sp", vt[:, r * 32 + cl * 8:r * 32 + cl * 8 + 8, 0:VD],
                       src[:, :, hl * VD:(hl + 1) * VD], vk[r * 4 + cl], (), (vk[r * 4 + cl],))
        kts = []
        for c in range(nmaps):
            u = hl * nmaps + c
            kt, kk = KT.next()
            kk = kkeys[id(kk)]
            for r in range(4):
                for rt in range(4):
                    cx.dma("pool", kt[:, r * 4096 + rt * 1024:r * 4096 + (rt + 1) * 1024], kT_d[rt, r, u],
                           kk[r * 4 + rt], (), (kk[r * 4 + rt],))
            kts.append((kt, kk))
        tasks = []
        for qt in range(S // 512):
            for c in range(nmaps):
                kb, nfar = 0, (max(0, 4 * qt - 1) if diff else 0)
                while kb < 4 * qt + 4:
                    if kb + 1 < nfar:
                        tasks.append((qt, c, (kb, kb + 1)))
                        kb += 2
                    else:
                        tasks.append((qt, c, (kb,)))
                        kb += 1
        qtiles, ptiles, groups, ostage, bcols = {}, {}, {}, {}, {}

        def stage_s(i):
            nonlocal nssq
            qt, c, kbs = tasks[i]
            if qt not in qtiles:
                r, tq = qt // 8, (qt % 8) * 512
                q, qk = QT.next()
                for cc in range(nmaps):
                    cx.dma("sp", q[:, cc, :], qT_d[tq // 1024, r, hl * nmaps + cc, :, tq % 1024:tq % 1024 + 512],
                           qk, (), (qk,))
                qtiles[qt] = (q, qk)
                if not diff:
                    bc, bc_k = bcol.next()
                    nkb = 4 * qt + 4
                    for qh in range(FOXH):
                        refblk = 4 * qt + (2 if FOXH == 1 else 1 + 2 * qh)
                        cx.ts(bc[:, qh, 0:nkb], cT[:, 0:nkb, hl], -1.0, crefb[:, refblk, hl:hl + 1],
                              ALU.mult, ALU.add, (cT_k, crefb_k), (bc_k,))
                    bcols[qt] = (bc, bc_k)
            q, qk = qtiles[qt]
            kt, kk = kts[c]
            ps, ps_k = psS.next()
            p, pk = PT.next()
            for j, kb in enumerate(kbs):
                t = kb - 4 * qt
                ql = max(0, 128 * t)
                cx.mm(ps[:, j * 512 + ql:(j + 1) * 512], kt[:, kb * 128:(kb + 1) * 128], q[:, c, ql:512],
                      True, True, (kk[kb // 8], qk), (ps_k,))
            if diff:
                if t <= -2:
                    w_ = 512 * len(kbs)
                    cx.act(p[:, 0:w_], ps[:, 0:w_], AF.Exp, (ps_k, bw_k), (pk,), bias=bw[:, hl, 639:640], scale=SCALE)
                else:
                    tm, tmk = tmpS.next()
                    cx.stt(tm[:, ql:512], ps[:, ql:512], SCALE, bw[:, hl, ql - 128 * t:512 - 128 * t],
                           ALU.mult, ALU.add, (ps_k, bw_k), (tmk,))
                    cx.act(p[:, ql:512], tm[:, ql:512], AF.Exp, (tmk,), (pk,))
            else:
                bc, bc_k = bcols[qt]
                for qh in range(FOXH):
                    w_ = 512 // FOXH
                    lo, hi = max(ql, w_ * qh), w_ * (qh + 1)
                    if lo < hi:
                        cx.act(p[:, lo:hi], ps[:, lo:hi], AF.Exp, (ps_k, bc_k), (pk,),
                               bias=bc[:, qh, kb:kb + 1], scale=SCALE)
                if t >= 0:
                    cx.tt(p[:, ql:ql + 128], p[:, ql:ql + 128], tri[:], ALU.mult, (pk, tri_k), (pk,))
            ptiles[i] = (p, pk)

        def stage_av(i):
            nonlocal nssq
            qt, c, kbs = tasks[i]
            p, pk = ptiles.pop(i)
            if (qt, c) not in groups:
                groups[(qt, c)] = [psO.next() for _ in range(4)]
            Os = groups[(qt, c)]
            if qt not in ostage:
                ostage[qt] = ostg.next()
            os_, os_k = ostage[qt]
            for j, kb in enumerate(kbs):
                t = kb - 4 * qt
                for qs in range(max(t, 0), 4):
                    O, Ok = Os[qs]
                    cx.mm(O[:, 0:VD + 1], p[:, j * 512 + qs * 128:j * 512 + (qs + 1) * 128], vt[:, kb, :],
                          kb == 0, kb == 4 * qt + qs, (pk, vk[kb // 8]), (Ok,))
            if kb != 4 * qt + 3:
                return
            for qs in range(4):
                O, Ok = Os[qs]
                rc, rck = rec.next()
                cx.s.op("dve", lambda e, rc=rc, O=O: e.reciprocal(rc[:, 0:1], O[:, VD:VD + 1]), (Ok,), (rck,))
                if not diff:
                    cx.ts(os_[:, qs, :], O[:, 0:VD], rc[:, 0:1], None, ALU.mult, None, (Ok, rck), (os_k,))
                elif c == 0:
                    cx.ts(A0[:, qs, :], O[:, 0:VD], rc[:, 0:1], None, ALU.mult, None, (Ok, rck), (A0_k,))
                else:
                    cx.tt(rc[:, 1:2], rc[:, 0:1], nlam, ALU.mult, (rck, sm_k), (rck,))
                    cb_, cbk = comb.next()
                    cx.stt(cb_[:], O[:, 0:VD], rc[:, 1:2], A0[:, qs, :], ALU.mult, ALU.add,
                           (Ok, rck, A0_k), (cbk,))
                    j = nssq
                    nssq += 1
                    cx.act(junk[:], cb_[:], AF.Square, (cbk,), (junk_k, ssq_k), accum=ssq[:, j:j + 1])
                    cx.act(ssq[:, j:j + 1], ssq[:, j:j + 1], AF.Ln, (ssq_k, sm_k), (ssq_k,),
                           bias=epsb, scale=1.0 / 256)
                    cx.act(ssq[:, j:j + 1], ssq[:, j:j + 1], AF.Exp, (ssq_k,), (ssq_k,), scale=-0.5)
                    cx.stt(os_[:, qs, :], cb_[:], ssq[:, j:j + 1], gs[:], ALU.mult, ALU.mult,
                           (cbk, ssq_k, gs_k), (os_k,))
            if c == nmaps - 1:
                ci = qt // 2
                rds = (os_k,) if xo is None else (os_k, xo.tk(ci // 4, ci % 4))
                cx.dma("sp", o_d[qt * 512:(qt + 1) * 512, hl * VD:(hl + 1) * VD].rearrange("(qs p) v -> p qs v", p=128),
                       os_[:], os_k, rds, ())
                if xo is not None and hl == nheads - 1 and qt % 2 == 1:
                    xo.issue(cx, cck, (ci % 4,), gs=(ci // 4,))
                del qtiles[qt]

        LOOK = 1 if diff else 3
        for i in range(min(LOOK, len(tasks))):
            stage_s(i)
        for i in range(len(tasks)):
            if i + LOOK < len(tasks):
                stage_s(i + LOOK)
            stage_av(i)
    cx.s.emit()


TT = 1024


def row_phase(nc, es, layer, x_d, o_d, ident_d, wo_d, gm_d, win_d, wout_d, ex, gsem=None, tag=""):
    cx = Ctx(nc, es, gsem, tag)
    rk = RowKit(cx, ident_d, nss=128, wslots=2, hnslots=1)
    h, _ = cx.sb("h", [128, 8, 2048], F32)
    hk = [Tk(f"h{i}") for i in range(8)]
    aT, aT_k = cx.sb("aT", [128, 16, TT], BF16)
    hid = Slots(cx, "hid", 2, [128, 4, TT], BF16)
    g, g_k = cx.sb("g", [128, 2048], F32)
    ob, ob_k = cx.sb("ob", [128, 2048], BF16)
    rr = Slots(cx, "rr", 2, [128, 512], BF16)
    ost = Slots(cx, "ost", 4, [128, 512], BF16)
    if layer == 0:
        wf, wf_k = cx.sb("wf", [128, 16, 16], BF16)
        cx.dma("pool", wf[:], ex["wf"].rearrange("(kc p) n -> p kc n", p=128), wf_k, (), (wf_k,))
        bfb, bfb_k = cx.sb("bfb", [128, 16], F32)
        cx.dma("sp", bfb[:], ex["bf"], bfb_k, (), (bfb_k,))
        zs = Slots(cx, "zs", 2, [128, 16], F32)
        one, one_k = cx.sb("one", [128, 1], F32)
        cx.memset(one[:], 1.0, (one_k,))
    else:
        fin = Slots(cx, "fin", 1, [128, 2048], F32)

    cck = Tk("cc")
    xis_ = ex.get("xis") or (None, None, None)
    for rt in range(TOK // TT):
        r0 = rt * TT
        for tb in range(8):
            cx.dma("sp", h[:, tb, :], x_d[r0 + tb * 128:r0 + (tb + 1) * 128, :], hk[tb], (), (hk[tb],))
        for tb in range(8):
            for gg in range(4):
                rw = r0 + tb * 128
                cx.dma("sp", ob[:, gg * 512:(gg + 1) * 512], o_d[gg, rw // 1024, rw % 1024:rw % 1024 + 128, :],
                       ob_k, (), (ob_k,))
            rk.transpose_in(ob, ob_k, aT, aT_k, tb * 128)
        for wb in range(4):
            wt, wk = rk.load_w(wo_d, wb * 512, 512)
            if layer == 0 and wb == 1 and rt > 0 and ex.get("xis") is not None:
                for xi in ex["xis"]:
                    xi.issue(cx, cck, (rt - 1,), defer=rk)

            def cb(tb, pa, pa_k, wb=wb):
                hs = h[:, tb, wb * 512:(wb + 1) * 512]
                cx.tt(hs, hs, pa[:], ALU.add, (pa_k, hk[tb]), (hk[tb],))
            rk.lin_tm(aT, aT_k, TT, wt, wk, 512, cb)
        cx.dma("sp", g[:], gm_d, g_k, (), (g_k,))
        for tb in range(8):
            rk.norm_T(h[:, tb, :], hk[tb], g, g_k, aT, aT_k, tb * 128)
        for hb in range(DFF // 512):
            wt, wk = rk.load_w(win_d, hb * 512, 512)
            hd, hd_k = hid.next()

            def cb(oc, tt, pa, pa_k, hd=hd, hd_k=hd_k):
                r_, rk_ = rr.next()
                cx.act(r_[:], pa[:], AF.Relu, (pa_k,), (rk_,))
                cx.tt(hd[:, oc, tt * 512:(tt + 1) * 512], r_[:], r_[:], ALU.mult, (rk_,), (hd_k,))
            rk.lin_fm(aT, aT_k, TT, wt, wk, 512, cb)
            wt, wk = rk.load_w(wout_d, 0, 2048, nk=4, k0=hb * 4)
            wv = wt
            for tb in range(8):
                for cg in range(4):
                    pa, pa_k = rk.pacc.next()
                    for kc in range(4):
                        cx.mm(pa[:], hd[:, kc, tb * 128:(tb + 1) * 128], wv[:, kc, cg * 512:(cg + 1) * 512],
                              kc == 0, kc == 3, (hd_k, wk.for_kc(kc)), (pa_k,))
                    hs = h[:, tb, cg * 512:(cg + 1) * 512]
                    cx.tt(hs, hs, pa[:], ALU.add, (pa_k, hk[tb]), (hk[tb],))
        if layer == 0:
            for tb in range(8):
                cx.dma("sp", ex["h_out"][r0 + tb * 128:r0 + (tb + 1) * 128, :], h[:, tb, :], hk[tb], (hk[tb],), ())
            cx.dma("sp", g[:], ex["kvg"], g_k, (), (g_k,))
            for tb in range(8):
                rk.norm_T(h[:, tb, :], hk[tb], g, g_k, aT, aT_k, tb * 128)
            for tb in range(8):
                pa, pa_k = rk.pacc.next()
                for kc in range(16):
                    cx.mm(pa[:, 0:16], aT[:, kc, tb * 128:(tb + 1) * 128], wf[:, kc, :], kc == 0, kc == 15,
                          (aT_k, wf_k), (pa_k,))
                z, zk = zs.next()
                cx.tt(z[:], pa[:, 0:16], bfb[:], ALU.add, (pa_k, bfb_k), (zk,))
                cx.act(z[:], z[:], AF.Exp, (zk,), (zk,), scale=-1.0)
                cx.act(z[:], z[:], AF.Ln, (zk, one_k), (zk,), bias=one[:, 0:1])
                cx.ts(z[:], z[:], -1.0, None, ALU.mult, None, (zk,), (zk,))
                for gg in range(4):
                    cx.dma("sp", ex["logf"][gg, r0 + tb * 128:r0 + (tb + 1) * 128, :], z[:, gg * 4:(gg + 1) * 4],
                           zk, (zk,), ())
            def early_kv(si, wb, rt=rt):
                if rt == 3 and xis_[0] is not None and si == 1 and wb == 1:
                    xis_[1].issue(cx, cck, (3,), defer=rk)

            def early_q(si, wb, rt=rt):
                if rt == 3 and xis_[0] is not None and wb == 1:
                    xis_[2].issue(cx, cck, (3,), defer=rk)
            proj_qkv(cx, rk, aT, aT_k, TT, r0, ost,
                     [("fm", ex["wk"], 0, ex["k2T"], xis_[1]), ("tm", ex["wv"], 0, ex["v2"], xis_[2])], early_kv)
            cx.dma("sp", g[:], ex["qg"], g_k, (), (g_k,))
            for tb in range(8):
                rk.norm_T(h[:, tb, :], hk[tb], g, g_k, aT, aT_k, tb * 128)
            proj_qkv(cx, rk, aT, aT_k, TT, r0, ost, [("fm", ex["wq"], 0, ex["q2T"], xis_[0])], early_q)
        else:
            cx.dma("sp", g[:], ex["fg"], g_k, (), (g_k,))
            for tb in range(8):
                f_, fk = fin.next()
                rs = rk.norm_stats(h[:, tb, :], hk[tb])
                cx.stt(f_[:], h[:, tb, :], rs, g[:], ALU.mult, ALU.mult, (hk[tb], rk.rs_k, g_k), (fk,))
                cx.dma("sp", ex["out"][r0 + tb * 128:r0 + (tb + 1) * 128, :], f_[:], fk, (fk,), ())
    rk.drain()
    cx.s.emit()


def _t5_bucket_np(rel):
    half, max_exact = 16, 8
    ret = np.where(rel > 0, half, 0)
    n = np.abs(rel)
    nf = np.maximum(n, 1).astype(np.float32)
    large = max_exact + (np.log(nf / np.float32(max_exact)) / np.float32(math.log(128 / max_exact))
                         * np.float32(half - max_exact)).astype(np.int32)
    large = np.minimum(large, half - 1)
    return ret + np.where(n < max_exact, n, large)


def _bc(v, n=128):
    v = np.asarray(v, np.float32).reshape(1, -1)
    return np.ascontiguousarray(np.broadcast_to(v, (n, v.shape[1])))


def _a2a(outs, key, b):
    return [np.ascontiguousarray(np.stack([outs[b * 4 + r][key][g] for r in range(4)])) for g in range(4)]


def _launch(build, in_maps):
    nc = bass.Bass("TRN2", target_bir_lowering=False)
    with ExitStack() as es:
        build(nc, es)
    res = run_bass_kernel_spmd(nc, in_maps, core_ids=list(range(NCORE)))
    return res.results


def _din(nc, name, shape, dt=F32):
    return nc.dram_tensor(name, list(shape), dt, kind="ExternalInput").ap()


def _dout(nc, name, shape, dt=F32):
    return nc.dram_tensor(name, list(shape), dt, kind="ExternalOutput").ap()


IDENT = np.eye(128, dtype=np.float32).astype(ml_dtypes.bfloat16)


def build_a(nc, es):
    phase_a(nc, es, _din(nc, "x", [TOK, D]), _din(nc, "g", [128, D]), _din(nc, "wqkv", [D, 3 * D]),
            _din(nc, "ident", [128, 128], BF16), _dout(nc, "qT", [16, 128, TOK], BF16),
            _dout(nc, "kT", [16, 128, TOK], BF16), _dout(nc, "v", [4, TOK, 512], BF16))


def build_attn(mode):
    def build(nc, es):
        aux = {}
        if mode == "diff":
            aux["bias"] = _din(nc, "bias", [128, 2, 640])
            aux["subg"] = _din(nc, "subg", [128, 256])
            aux["lam"] = _din(nc, "lam", [128, 4, 128])
        else:
            aux["trimask"] = _din(nc, "trimask", [128, 128], BF16)
            aux["cmats"] = _din(nc, "cmats", [128, 3, 128])
            aux["logf"] = _din(nc, "logf", [4, TOK, 4])
        attention(nc, es, mode, _din(nc, "qT", [4, 4, 128, TOK], BF16), _din(nc, "kT", [4, 4, 128, TOK], BF16),
                  _din(nc, "v", [4, TOK, 512], BF16), _dout(nc, "o", [S, 512], BF16), aux)
    return build


def build_row(layer):
    def build(nc, es):
        ex = {}
        if layer == 0:
            for nm in ("kvg", "qg"):
                ex[nm] = _din(nc, nm, [128, D])
            for nm in ("wk", "wv", "wq"):
                ex[nm] = _din(nc, nm, [D, D])
            ex["wf"] = _din(nc, "wf", [D, 16])
            ex["bf"] = _din(nc, "bf", [128, 16])
            ex["h_out"] = _dout(nc, "h_out", [TOK, D])
            ex["q2T"] = _dout(nc, "q2T", [16, 128, TOK], BF16)
            ex["k2T"] = _dout(nc, "k2T", [16, 128, TOK], BF16)
            ex["v2"] = _dout(nc, "v2", [4, TOK, 512], BF16)
            ex["logf"] = _dout(nc, "logf", [4, TOK, 4])
        else:
            ex["fg"] = _din(nc, "fg", [128, D])
            ex["out"] = _dout(nc, "out", [TOK, D])
        row_phase(nc, es, layer, _din(nc, "x", [TOK, D]), _din(nc, "o", [4, TOK, 512], BF16),
                  _din(nc, "ident", [128, 128], BF16), _din(nc, "wo", [D, D]), _din(nc, "gm", [128, D]),
                  _din(nc, "win", [D, DFF]), _din(nc, "wout", [DFF, D]), ex)
    return build


I32 = mybir.dt.int32


class SemPool:
    def __init__(self, stack):
        self.stack = stack
        self.free = []
        self.regs = []


GROUPS = [[0, 1, 2, 3], [4, 5, 6, 7]]


FOXH = 2
CHK = 524288


def comm_phase(nc, es, gsem, tag, cid_d, items):
    cx = Ctx(nc, es, gsem, tag)
    cid, cid_k = cx.sb("cid", [1, 8], I32)
    cx.dma("sp", cid[:], cid_d, cid_k, (), (cid_k,))
    if not gsem.regs:
        gsem.regs = [gsem.stack.enter_context(nc.gpsimd.register(f"cr{i}")) for i in range(6)]
        for i in range(6):
            o = cx.s.op("pool", lambda e, i=i: e.reg_load(gsem.regs[i], cid[:1, i:i + 1]), (cid_k,), ())
            o.nosig = True
    regs = gsem.regs
    cck = Tk("cc")
    inner = [[8192, CHK // 8192], [1, 8192]]
    for n, it in enumerate(items):
        gk, dk = Tk(f"g{n}"), Tk(f"l{n}")
        if isinstance(it, XItem):
            it.tks = {}
            it.issue(cx, cck, (0, 1, 2, 3))
            gall, dst = it.gall, it.dst
            cx.s.op("pool", lambda e, dst=dst, gall=gall: e.dma_start(
                out=bass.AP(dst, 0, [[32768, 16 * CHK // 32768], [1, 32768]]),
                in_=bass.AP(gall, regs[0], [[32768, 16 * CHK // 32768], [1, 32768]])),
                tuple(it.tks.values()), (dk,), dma=dk)
            it.reset()
        else:
            _, src, gath, dst, ri, rstride, nel = it
            cx.s.op("pool", lambda e, src=src, gath=gath: e.collective_compute(
                "AllGather", ALU.bypass, replica_groups=GROUPS, ins=[src.ap().opt()], outs=[gath.ap().opt()]),
                (), (gk,), dma=cck, inc=1)
            cx.s.op("pool", lambda e, dst=dst, gath=gath, ri=ri, rstride=rstride, nel=nel: e.dma_start(
                out=bass.AP(dst, 0, [[nel, 4], [1, nel]]), in_=bass.AP(gath, regs[ri], [[rstride, 4], [1, nel]])),
                (gk,), (dk,), dma=dk)
    cx.s.emit()


def build_fused(nc, upto=9):
    di = lambda name, shape, dt=F32: nc.dram_tensor(name, list(shape), dt, kind="ExternalInput").ap()
    x_d = di("x", [TOK, D])
    ident_d = di("ident", [128, 128], BF16)
    cid_d = di("cid", [1, 8], I32)
    g_attn0, g_attn1, g_mlp0, g_mlp1, g_kv, g_fin = [di(n, [128, D]) for n in
                                                     ("g_attn0", "g_attn1", "g_mlp0", "g_mlp1", "g_kv", "g_fin")]
    wqkv = di("wqkv", [D, 3 * D])
    wo0, wo1, wk, wv, wq = [di(n, [D, D]) for n in ("wo0", "wo1", "wk", "wv", "wq")]
    win0, win1 = di("win0", [D, DFF]), di("win1", [D, DFF])
    wout0, wout1 = di("wout0", [DFF, D]), di("wout1", [DFF, D])
    wf, bf = di("wf", [D, 16]), di("bf", [128, 16])
    aux0 = {"bias": di("bias", [128, 2, 640]), "subg": di("subg", [128, 256]), "lam": di("lam", [128, 4, 128])}
    trimask, cmats = di("trimask", [128, 128], BF16), di("cmats", [128, 3, 128])
    out_d = nc.dram_tensor("out", [TOK, D], F32, kind="ExternalOutput").ap()
    dt_ = lambda name, shape, dt=BF16: nc.dram_tensor(name, list(shape), dt)
    qT, kT, v = dt_("i_qT", [8192, 1024]), dt_("i_kT", [8192, 1024]), dt_("i_v", [4 * TOK, 512])
    Gq, Gk, Gv = dt_("i_Gq", [8192, TOK]), dt_("i_Gk", [8192, TOK]), dt_("i_Gv", [16 * TOK, 512])
    Lq, Lk, Lv = dt_("i_Lq", [8192, 1024]), dt_("i_Lk", [8192, 1024]), dt_("i_Lv", [4 * TOK, 512])
    o, Lo = dt_("i_o", [S, 512]), dt_("i_Lo", [4 * TOK, 512])
    h1 = dt_("i_h1", [TOK, D], F32)
    logf, Glogf, Llogf = dt_("i_logf", [4 * TOK, 4], F32), dt_("i_Glogf", [16 * TOK, 4], F32), dt_("i_Llogf", [4 * TOK, 4], F32)
    u3 = lambda t: t.ap().rearrange("(g rt ul d) t -> g rt ul d t", g=4, rt=4, ul=4)
    r4 = lambda t: t.ap().rearrange("(rt r ul d) t -> rt r ul d t", rt=4, r=4, ul=4)
    xq, xk_, xv = XItem(qT, Gq, Lq), XItem(kT, Gk, Lk), XItem(v, Gv, Lv)
    xo = XItem(o, Gv, Lo)
    c4 = lambda t: t.ap().rearrange("(c r i) v -> r c i v", c=4, r=4)
    g3 = lambda t: t.ap().rearrange("(g t) v -> g t v", g=4)
    LG = ("small", logf, Glogf, Llogf, 1, 4 * TOK * 4, TOK * 4)
    with ExitStack() as gstack:
        gsem = SemPool(gstack)
        if upto > 0:
            with ExitStack() as es:
                phase_a(nc, es, x_d, g_attn0, wqkv, ident_d, u3(qT), u3(kT), g3(v), gsem, "A_", (xq, xk_, xv))
        if upto > 1:
            with ExitStack() as es:
                comm_phase(nc, es, gsem, "X1_", cid_d, [xq, xk_, xv])
        if upto > 2:
            with ExitStack() as es:
                attention(nc, es, "diff", r4(Lq), r4(Lk), c4(Lv), o.ap(), aux0, gsem, "B1_", xo)
        if upto > 3:
            with ExitStack() as es:
                comm_phase(nc, es, gsem, "X2_", cid_d, [xo])
        if upto > 4:
            with ExitStack() as es:
                ex = {"kvg": g_kv, "qg": g_attn1, "wk": wk, "wv": wv, "wq": wq, "wf": wf, "bf": bf,
                      "h_out": h1.ap(), "q2T": u3(qT), "k2T": u3(kT), "v2": g3(v), "logf": g3(logf), "xis": (xq, xk_, xv)}
                row_phase(nc, es, 0, x_d, c4(Lo), ident_d, wo0, g_mlp0, win0, wout0, ex, gsem, "B2_")
        if upto > 5:
            with ExitStack() as es:
                comm_phase(nc, es, gsem, "X3_", cid_d, [xq, xk_, xv, LG])
        if upto > 6:
            with ExitStack() as es:
                aux1 = {"trimask": trimask, "cmats": cmats, "logf": g3(Llogf)}
                attention(nc, es, "fox", r4(Lq), r4(Lk), c4(Lv), o.ap(), aux1, gsem, "C1_", xo)
        if upto > 7:
            with ExitStack() as es:
                comm_phase(nc, es, gsem, "X4_", cid_d, [xo])
        if upto > 8:
            with ExitStack() as es:
                row_phase(nc, es, 1, h1.ap(), c4(Lo), ident_d, wo1, g_mlp1, win1, wout1,
                          {"fg": g_fin, "out": out_d}, gsem, "C2_")
        if upto < 9:
            with ExitStack() as es:
                cx = Ctx(nc, es, gsem, "Z_")
                k = Tk("z")
                cx.dma("sp", out_d, x_d, k, (), (k,))
                cx.s.emit()


def kernel(x, rel_bias_table, attn_norm_g, mlp_norm_g, w_qkv_a, lam_q1, lam_k1, lam_q2, lam_k2,
           subln_g, w_o_a, kv_norm_g, w_k_b, w_v_b, w_f_b, b_f_b, w_q_b, w_o_b, w_mlp_in, w_mlp_out,
           final_norm_g, _upto=9):
    f32 = lambda a: np.ascontiguousarray(np.asarray(a, np.float32))
    x = f32(x)
    kk = np.arange(128)[:, None]
    jj = np.arange(640)[None, :]
    bias_all = f32(rel_bias_table)[_t5_bucket_np(kk - jj)]
    tri = (np.arange(128)[None, :] >= np.arange(128)[:, None]).astype(np.float32)
    cm = np.zeros((128, 3, 128), np.float32)
    cm[:, 0, :] = tri
    cm[127, 1, :] = 1.0
    cm[0, 2, :] = 1.0
    shared = {
        "ident": IDENT, "g_attn0": _bc(attn_norm_g[0]), "g_attn1": _bc(attn_norm_g[1]),
        "g_mlp0": _bc(mlp_norm_g[0]), "g_mlp1": _bc(mlp_norm_g[1]), "g_kv": _bc(kv_norm_g),
        "g_fin": _bc(final_norm_g), "wqkv": f32(w_qkv_a[0]), "wo0": f32(w_o_a[0]), "wo1": f32(w_o_b[0]),
        "wk": f32(w_k_b), "wv": f32(w_v_b), "wq": f32(w_q_b[0]), "win0": f32(w_mlp_in[0]),
        "win1": f32(w_mlp_in[1]), "wout0": f32(w_mlp_out[0]), "wout1": f32(w_mlp_out[1]),
        "wf": f32(w_f_b), "bf": _bc(b_f_b), "subg": _bc(subln_g[0]),
        "lam": np.ascontiguousarray(np.broadcast_to(
            np.stack([f32(lam_q1[0]), f32(lam_k1[0]), f32(lam_q2[0]), f32(lam_k2[0])])[None], (128, 4, 128))),
        "trimask": tri.astype(ml_dtypes.bfloat16), "cmats": cm,
    }
    in_maps = []
    for c in range(NCORE):
        b, g = c // 4, c % 4
        m = dict(shared)
        m["x"] = np.ascontiguousarray(x[b, g * TOK:(g + 1) * TOK])
        m["bias"] = np.ascontiguousarray(bias_all[:, :, 2 * g:2 * g + 2].transpose(0, 2, 1))
        cid = np.zeros((1, 8), np.int32)
        cid[0, 0] = g * 16 * CHK
        for r in range(4):
            cid[0, 2 + r] = g * 16 * CHK + r * CHK
        cid[0, 1] = g * TOK * 4
        m["cid"] = cid
        in_maps.append(m)
    nc = bass.Bass("TRN2", target_bir_lowering=False)
    build_fused(nc, _upto)
    res = run_bass_kernel_spmd(nc, in_maps, core_ids=list(range(NCORE))).results
    out = np.empty((NB, S, D), np.float32)
    for c in range(NCORE):
        out[c // 4, (c % 4) * TOK:(c % 4 + 1) * TOK] = res[c]["out"]
    return out
```

```python
import math
from contextlib import ExitStack

import numpy as np
import ml_dtypes
import concourse.bass as bass
import concourse.mybir as mybir
from concourse.bass_utils import run_bass_kernel_spmd

F32 = mybir.dt.float32
BF16 = mybir.dt.bfloat16
AF = mybir.ActivationFunctionType
ALU = mybir.AluOpType
AX = mybir.AxisListType

D = 2048
S = 16384
NB = 2
DFF = 8192
NCORE = 8
TOK = 4096
CH = 2048
NEG = -30000.0
SCALE = 128 ** -0.5
EPS = 1e-6
SUBEPS = 1e-5
LAMBDA_INIT = 0.8 - 0.6 * math.exp(-0.3 * 0)


class Tk:
    __slots__ = ("name", "w", "r", "semcnt")

    def __init__(self, name):
        self.name = name
        self.w = None
        self.r = []
        self.semcnt = 0


class Op:
    __slots__ = ("eng", "fn", "deps", "needed", "is_dma", "tk", "val", "key", "inc", "nosig")


ENGS = ("pe", "act", "dve", "pool", "sp")
BLK = {"pe": "tensor", "act": "scalar", "dve": "vector", "pool": "gpsimd", "sp": "sync"}


class Sched:
    def __init__(self, nc, gsem=None, tag=""):
        self.nc = nc
        self.gsem = gsem
        self.tag = tag
        self.ops = {e: [] for e in ENGS}
        self.all = []

    def op(self, eng, fn, reads=(), writes=(), dma=None, inc=16):
        o = Op()
        o.inc = inc
        o.nosig = False
        o.eng = eng
        o.fn = fn
        o.needed = False
        o.is_dma = dma is not None
        o.tk = dma
        o.val = 0
        o.key = None
        deps = []
        for t in reads:
            if t.w is not None:
                deps.append(t.w)
        for t in writes:
            if t.w is not None:
                deps.append(t.w)
            deps.extend(t.r)
        o.deps = deps
        for d in deps:
            d.needed = True
        for t in reads:
            t.r.append(o)
        for t in writes:
            t.w = o
            t.r = []
        self.ops[eng].append(o)
        self.all.append(o)
        return o

    def emit(self):
        nc = self.nc
        engcnt = {e: 0 for e in ENGS}
        keys = {}
        for o in self.all:
            if o.is_dma:
                o.tk.semcnt += o.inc
                o.val = o.tk.semcnt
                o.key = ("t", id(o.tk))
                keys[o.key] = "d_" + o.tk.name
            elif o.needed:
                engcnt[o.eng] += 1
                o.val = engcnt[o.eng]
                o.key = ("e", o.eng)
                keys[o.key] = "e_" + o.eng
        if self.gsem is not None:
            for e in ENGS:
                for o in reversed(self.ops[e]):
                    if not o.is_dma and not o.nosig:
                        if not o.needed:
                            o.needed = True
                            engcnt[e] += 1
                            o.val = engcnt[e]
                            o.key = ("e", e)
                            keys[o.key] = "e_" + e
                        break
        final = {}
        for o in self.all:
            if o.key is not None:
                final[o.key] = max(final.get(o.key, 0), o.val)
        with ExitStack() as es:
            sems, base = {}, {}
            for k, n in keys.items():
                if self.gsem is None:
                    sems[k], base[k] = es.enter_context(nc.semaphore(self.tag + n)), 0
                elif self.gsem.free:
                    sems[k], base[k] = self.gsem.free.pop()
                else:
                    sems[k], base[k] = self.gsem.stack.enter_context(nc.semaphore(self.tag + n)), 0
            block = es.enter_context(nc.Block())
            for en in ENGS:
                def body(eng, en=en):
                    waited = {}
                    for o in self.ops[en]:
                        for d in o.deps:
                            if en == "pe" and d.eng == "pe" and not d.is_dma:
                                continue
                            if waited.get(d.key, 0) >= d.val:
                                continue
                            eng.wait_ge(sems[d.key], base[d.key] + d.val)
                            waited[d.key] = d.val
                        ins = o.fn(eng)
                        if o.is_dma:
                            ins.then_inc(sems[o.key], o.inc)
                        elif o.needed:
                            ins.then_inc(sems[o.key], 1)
                    if en == "sp" or self.gsem is not None:
                        for k, v in final.items():
                            if waited.get(k, 0) < v:
                                eng.wait_ge(sems[k], base[k] + v)
                getattr(block, BLK[en])(body)
            if self.gsem is not None:
                for k in keys:
                    self.gsem.free.append((sems[k], base[k] + final[k]))


class Ctx:
    def __init__(self, nc, es, gsem=None, tag=""):
        self.nc = nc
        self.es = es
        self.tag = tag
        self.s = Sched(nc, gsem, tag)
        self.rr = 0

    def sb(self, name, shape, dt):
        t = self.es.enter_context(self.nc.sbuf_tensor("sb_" + self.tag + name, list(shape), dt))
        return t, Tk(name)

    def ps(self, name, shape, dt=F32):
        t = self.es.enter_context(self.nc.psum_tensor("ps_" + self.tag + name, list(shape), dt))
        return t, Tk(name)

    def dma(self, q, out, in_, tk, reads=(), writes=()):
        return self.s.op(q, lambda e: e.dma_start(out=out, in_=in_), reads, writes, dma=tk)

    def mm(self, out, lhsT, rhs, start, stop, reads, writes):
        return self.s.op("pe", lambda e: e.matmul(out, lhsT, rhs, start=start, stop=stop),
                         reads, writes)

    def tr(self, out, in_, ident, reads, writes):
        return self.s.op("pe", lambda e: e.transpose(out, in_, ident), reads, writes)

    def act(self, out, in_, func, reads, writes, bias=0.0, scale=1.0, accum=None):
        if accum is None:
            return self.s.op("act", lambda e: e.activation(out, in_, func, bias=bias, scale=scale),
                             reads, writes)
        return self.s.op("act", lambda e: e.activation(out, in_, func, bias=bias, scale=scale,
                                                       accum_out=accum), reads, writes)

    def copy(self, out, in_, reads, writes, eng=None):
        if eng is None:
            self.rr ^= 1
            eng = "act" if self.rr else "dve"
        if eng == "act":
            return self.s.op("act", lambda e: e.copy(out, in_), reads, writes)
        return self.s.op(eng, lambda e: e.tensor_copy(out, in_), reads, writes)

    def ts(self, out, in0, s1, s2, op0, op1, reads, writes, eng="dve"):
        if op1 is None:
            return self.s.op(eng, lambda e: e.tensor_scalar(out, in0, s1, s2, op0), reads, writes)
        return self.s.op(eng, lambda e: e.tensor_scalar(out, in0, s1, s2, op0, op1), reads, writes)

    def stt(self, out, in0, scalar, in1, op0, op1, reads, writes, eng="dve"):
        return self.s.op(eng, lambda e: e.scalar_tensor_tensor(out, in0, scalar, in1, op0, op1),
                         reads, writes)

    def tt(self, out, in0, in1, op, reads, writes, eng="dve"):
        return self.s.op(eng, lambda e: e.tensor_tensor(out, in0, in1, op), reads, writes)

    def memset(self, ap, val, writes, eng="dve"):
        return self.s.op(eng, lambda e: e.memset(ap, val), (), writes)


class Slots:
    def __init__(self, cx, name, n, shape, dt, psum=False):
        self.items = []
        for i in range(n):
            self.items.append(cx.ps(f"{name}{i}", shape, dt) if psum else cx.sb(f"{name}{i}", shape, dt))
        self.i = 0

    def next(self):
        it = self.items[self.i % len(self.items)]
        self.i += 1
        return it


class WKeys:
    def __init__(self, tks, st):
        self.tks, self.st = tks, st

    def for_kc(self, kc):
        return self.tks[kc // self.st]


class RowKit:
    def __init__(self, cx, ident_d, nss=64, wslots=3, hnslots=2):
        self.cx = cx
        self.ident, self.ident_k = cx.sb("ident", [128, 128], BF16)
        cx.dma("sp", self.ident[:], ident_d, self.ident_k, (), (self.ident_k,))
        self.w = Slots(cx, "w", wslots, [128, 8192], BF16)
        self.pacc = Slots(cx, "pacc", 4, [128, 512], F32, psum=True)
        self.ptr = Slots(cx, "ptr", 2, [128, 1024], BF16, psum=True)
        self.hn = Slots(cx, "hn", hnslots, [128, 2048], BF16)
        self.ss, self.ss_k = cx.sb("ss", [128, nss], F32)
        self.rs, self.rs_k = cx.sb("rs", [128, nss], F32)
        cx.memset(self.ss[:], 0.0, (self.ss_k,))
        self.epsb, self.eps_k = cx.sb("epsb", [128, 2], F32)
        cx.memset(self.epsb[:, 0:1], EPS, (self.eps_k,))
        cx.memset(self.epsb[:, 1:2], SUBEPS, (self.eps_k,))
        self.eps_main = self.epsb[:, 0:1]
        self.eps_sub = self.epsb[:, 1:2]
        self.nss = 0
        self.wsub = {}
        self.pending = []
        self.per_load = 1

    def drain(self, n=None):
        while self.pending and (n is None or n > 0):
            self.pending.pop(0)()
            if n is not None:
                n -= 1

    def norm_stats(self, x_ap, x_k, junk=None, junk_k=None):
        cx = self.cx
        j = self.nss
        self.nss += 1
        ss = self.ss[:, j:j + 1]
        rs = self.rs[:, j:j + 1]
        if junk is None:
            junk, junk_k = self.hn.next()
        cx.act(junk[:], x_ap, AF.Square, (x_k,), (junk_k, self.ss_k), accum=ss)
        cx.act(rs, ss, AF.Ln, (self.ss_k, self.eps_k), (self.rs_k,), bias=self.eps_main, scale=1.0 / D)
        cx.act(rs, rs, AF.Exp, (self.rs_k,), (self.rs_k,), scale=-0.5)
        return rs

    def norm_T(self, x_ap, x_k, g, g_k, dstT, dstT_k, tcol):
        cx = self.cx
        hn, hn_k = self.hn.next()
        rs = self.norm_stats(x_ap, x_k, hn, hn_k)
        cx.stt(hn[:], x_ap, rs, g[:], ALU.mult, ALU.mult, (x_k, self.rs_k, g_k), (hn_k,))
        self.transpose_in(hn, hn_k, dstT, dstT_k, tcol)

    def transpose_in(self, src, src_k, dstT, dstT_k, tcol, nchunk=16, c0=0):
        cx = self.cx
        for half in range(0, nchunk, 8):
            n = min(8, nchunk - half)
            pt, pt_k = self.ptr.next()
            for i in range(n):
                cx.tr(pt[:, i * 128:(i + 1) * 128], src[:, (half + i) * 128:(half + i + 1) * 128],
                      self.ident[:], (src_k, self.ident_k), (pt_k,))
            cx.copy(dstT[:, c0 + half:c0 + half + n, tcol:tcol + 128],
                    pt[:, 0:n * 128].rearrange("p (c t) -> p c t", t=128), (pt_k,), (dstT_k,))

    def load_w(self, w_d, c0, ncols, nk=16, k0=0):
        cx = self.cx
        wt, wk = self.w.next()
        if id(wk) not in self.wsub:
            self.wsub[id(wk)] = [Tk(wk.name + f"_{i}") for i in range(4)]
        subs = self.wsub[id(wk)]
        wv_s = wt[:, 0:nk * ncols].rearrange("p (k n) -> p k n", n=ncols)
        wv = w_d.rearrange("(kc p) n -> p kc n", p=128)
        st = max(1, nk // 4)
        for i, q in enumerate(range(0, nk, st)):
            cx.dma("pool", wv_s[:, q:q + st, :], wv[:, k0 + q:k0 + q + st, c0:c0 + ncols], subs[i], (), (subs[i],))
        self.drain(self.per_load)
        return wv_s, WKeys(subs, st)

    def lin_fm(self, srcT, srcT_k, ntok, wt, wk, ncols, cb, nk=16):
        cx = self.cx
        for oc in range(ncols // 128):
            for tt in range(ntok // 512):
                pa, pa_k = self.pacc.next()
                for kc in range(nk):
                    cx.mm(pa[:], wt[:, kc, oc * 128:(oc + 1) * 128], srcT[:, kc, tt * 512:(tt + 1) * 512],
                          kc == 0, kc == nk - 1, (wk.for_kc(kc), srcT_k), (pa_k,))
                cb(oc, tt, pa, pa_k)

    def lin_tm(self, srcT, srcT_k, ntok, wt, wk, ncols, cb, nk=16):
        cx = self.cx
        for tb in range(ntok // 128):
            pa, pa_k = self.pacc.next()
            for kc in range(nk):
                cx.mm(pa[:, 0:ncols], srcT[:, kc, tb * 128:(tb + 1) * 128], wt[:, kc, 0:ncols],
                      kc == 0, kc == nk - 1, (wk.for_kc(kc), srcT_k), (pa_k,))
            cb(tb, pa, pa_k)


class XItem:
    def __init__(self, src, gall, dst):
        self.src, self.gall, self.dst = src, gall, dst
        self.issued = set()
        self.tks = {}

    def reset(self):
        self.issued = set()
        self.tks = {}

    def tk(self, g, rt):
        if (g, rt) not in self.tks:
            self.tks[(g, rt)] = Tk(f"x{g}{rt}")
        return self.tks[(g, rt)]

    def issue(self, cx, cck, rts, gs=(0, 1, 2, 3), defer=None):
        for rt in rts:
            for g in gs:
                ci = 4 * g + rt
                if ci in self.issued:
                    continue
                self.issued.add(ci)
                src, gall = self.src, self.gall

                def rec(src=src, gall=gall, ci=ci, tk=self.tk(g, rt)):
                    cx.s.op("pool", lambda e: e.collective_compute(
                        "AllGather", ALU.bypass, replica_groups=GROUPS,
                        ins=[bass.AP(src, ci * CHK, [[8192, CHK // 8192], [1, 8192]])],
                        outs=[bass.AP(gall, ci * 4 * CHK, [[8192, 4 * CHK // 8192], [1, 8192]])]),
                        (), (tk,), dma=cck, inc=1)
                if defer is None:
                    rec()
                else:
                    defer.pending.append(rec)


def proj_qkv(cx, rk, hT, hT_k, ntok, t0, ost, specs, after_load=None):
    for si, (kind, w_d, col0, dst, xi) in enumerate(specs):
        for wb in range(4):
            wt, wk = rk.load_w(w_d, col0 + wb * 512, 512)
            if after_load is not None:
                after_load(si, wb)
            if kind == "fm":
                def cb(oc, tt, pa, pa_k, dst=dst, wb=wb, xi=xi):
                    o, ok = ost.next()
                    cx.copy(o[:], pa[:], (pa_k,), (ok,))
                    tok = t0 + tt * 512
                    rds = (ok,) if xi is None else (ok, xi.tk(wb, tok // 1024))
                    cx.dma("sp", dst[wb, tok // 1024, oc, :, tok % 1024:tok % 1024 + 512], o[:], ok, rds, ())
                rk.lin_fm(hT, hT_k, ntok, wt, wk, 512, cb)
            else:
                def cb(tb, pa, pa_k, dst=dst, wb=wb, xi=xi):
                    o, ok = ost.next()
                    cx.copy(o[:], pa[:], (pa_k,), (ok,))
                    tok = t0 + tb * 128
                    rds = (ok,) if xi is None else (ok, xi.tk(wb, tok // 1024))
                    cx.dma("sp", dst[wb, tok:tok + 128, :], o[:], ok, rds, ())
                rk.lin_tm(hT, hT_k, ntok, wt, wk, 512, cb)


def phase_a(nc, es, x_d, g_d, wqkv_d, ident_d, qT_d, kT_d, v_d, gsem=None, tag="", xis=(None, None, None)):
    cx = Ctx(nc, es, gsem, tag)
    rk = RowKit(cx, ident_d)
    rk.per_load = 2
    g, g_k = cx.sb("g", [128, 2048], F32)
    cx.dma("sp", g[:], g_d, g_k, (), (g_k,))
    xs = Slots(cx, "x", 2, [128, 2048], F32)
    hT, hT_k = cx.sb("hT", [128, 16, CH], BF16)
    ost = Slots(cx, "ost", 4, [128, 512], BF16)
    cck = Tk("cc")
    for half in range(2):
        t0 = half * CH
        if half == 1 and xis[0] is not None:
            for xi in xis:
                xi.issue(cx, cck, (0, 1), defer=rk)
        for tb in range(CH // 128):
            xt, xk = xs.next()
            cx.dma("sp", xt[:], x_d[t0 + tb * 128:t0 + (tb + 1) * 128, :], xk, (), (xk,))
            rk.norm_T(xt[:], xk, g, g_k, hT, hT_k, tb * 128)
        def early(si, wb, half=half):
            if half == 1 and xis[0] is not None and wb == 1 and si in (1, 2):
                xis[si - 1].issue(cx, cck, (2, 3), defer=rk)
        proj_qkv(cx, rk, hT, hT_k, CH, t0, ost,
                 [("fm", wqkv_d, 0, qT_d, xis[0]), ("fm", wqkv_d, 2048, kT_d, xis[1]),
                  ("tm", wqkv_d, 4096, v_d, xis[2])], early)
    rk.drain()
    cx.s.emit()


def attention(nc, es, mode, qT_d, kT_d, v_d, o_d, aux, gsem=None, tag="", xo=None):
    cx = Ctx(nc, es, gsem, tag)
    cck = Tk("cc")
    diff = mode == "diff"
    VD = 256 if diff else 128
    nheads = 2 if diff else 4
    nmaps = 2 if diff else 1
    NKB = S // 128
    KT = Slots(cx, "KT", 2, [128, S], BF16)
    Vt = Slots(cx, "Vt", 1 if diff else 2, [128, NKB, VD + 1], BF16)
    vkeys = {}
    for vt, vk in Vt.items:
        vkeys[id(vk)] = [Tk(vk.name + f"_{i}") for i in range(16)]
        cx.memset(vt[:, :, VD:VD + 1], 1.0, tuple(vkeys[id(vk)]), eng="pool")
    kkeys = {id(kk_): [Tk(kk_.name + f"_{i}") for i in range(16)] for _, kk_ in KT.items}
    QT = Slots(cx, "QT", 2, [128, nmaps, 512], BF16)
    if diff:
        PT = Slots(cx, "PT", 3, [128, 1024], BF16)
        psS = Slots(cx, "psS", 2, [128, 1024], F32, psum=True)
    else:
        PT = Slots(cx, "PT", 5, [128, 512], BF16)
        psS = Slots(cx, "psS", 4, [128, 512], F32, psum=True)
    psO = Slots(cx, "psO", 4, [128, 512], F32, psum=True)
    ostg = Slots(cx, "ostg", 2, [128, 4, VD], BF16)
    sm, sm_k = cx.sb("sm", [128, 16], F32)
    rec = Slots(cx, "rec", 4, [128, 2], F32)
    nssq = 0
    if diff:
        bw, bw_k = cx.sb("bw", [128, 2, 640], F32)
        cx.dma("sp", bw[:], aux["bias"], bw_k, (), (bw_k,))
        cx.memset(bw[64:128, :, 0:64], NEG, (bw_k,))
        tmpS = Slots(cx, "tmpS", 2, [128, 512], F32)
        A0, A0_k = cx.sb("A0", [128, 4, 256], F32)
        comb = Slots(cx, "comb", 2, [128, 256], F32)
        junk, junk_k = cx.sb("junk", [128, 256], BF16)
        gs, gs_k = cx.sb("gs", [128, 256], F32)
        cx.dma("sp", gs[:], aux["subg"], gs_k, (), (gs_k,))
        cx.ts(gs[:], gs[:], 1.0 - LAMBDA_INIT, None, ALU.mult, None, (gs_k,), (gs_k,))
        lamb, lamb_k = cx.sb("lamb", [128, 4, 128], F32)
        cx.dma("sp", lamb[:], aux["lam"], lamb_k, (), (lamb_k,))
        lp, lp_k = cx.sb("lp", [128, 2, 128], F32)
        cx.tt(lp[:, 0, :], lamb[:, 0, :], lamb[:, 1, :], ALU.mult, (lamb_k,), (lp_k,))
        cx.tt(lp[:, 1, :], lamb[:, 2, :], lamb[:, 3, :], ALU.mult, (lamb_k,), (lp_k,))
        cx.s.op("dve", lambda e: e.tensor_reduce(sm[:, 0:2], lp[:], AX.X, ALU.add), (lp_k,), (sm_k,))
        cx.act(sm[:, 2:4], sm[:, 0:2], AF.Exp, (sm_k,), (sm_k,))
        cx.tt(sm[:, 4:5], sm[:, 3:4], sm[:, 2:3], ALU.subtract, (sm_k,), (sm_k,))
        cx.ts(sm[:, 5:6], sm[:, 4:5], -LAMBDA_INIT, None, ALU.add, None, (sm_k,), (sm_k,))
        nlam = sm[:, 5:6]
        cx.memset(sm[:, 6:7], SUBEPS, (sm_k,))
        epsb = sm[:, 6:7]
        ssq, ssq_k = cx.sb("ssq", [128, 2 * 32 * 4], F32)
        cx.memset(ssq[:], 0.0, (ssq_k,))
    else:
        tri, tri_k = cx.sb("tri", [128, 128], BF16)
        cx.dma("sp", tri[:], aux["trimask"], tri_k, (), (tri_k,))
        cm, cm_k = cx.sb("cm", [128, 3, 128], F32)
        cx.dma("sp", cm[:], aux["cmats"], cm_k, (), (cm_k,))
        X, X_k = cx.sb("X", [128, NKB, 4], F32)
        for r in range(4):
            cx.dma("sp", X[:, r * 32:(r + 1) * 32, :],
                   aux["logf"][r].rearrange("(kb p) h -> p kb h", p=128), X_k, (), (X_k,))
        pre, pre_k = cx.sb("pre", [128, NKB * 4], F32)
        sA, sA_k = cx.sb("sA", [128, NKB * 4], F32)
        sB, sB_k = cx.sb("sB", [128, NKB * 4], F32)
        cT, cT_k = cx.sb("cT", [128, NKB, 4], F32)
        crefb, crefb_k = cx.sb("crefb", [128, NKB, 4], F32)
        Xf = X[:].rearrange("p k h -> p (k h)")
        pc, pc_k = psS.next()
        cx.mm(pc[:], cm[:, 0, :], Xf, True, True, (cm_k, X_k), (pc_k,))
        cx.copy(pre[:], pc[:], (pc_k,), (pre_k,), eng="dve")
        pc, pc_k = psS.next()
        cx.mm(pc[:], cm[:, 1, :], pre[:], True, True, (cm_k, pre_k), (pc_k,))
        cx.copy(sA[:], pc[:], (pc_k,), (sA_k,), eng="dve")
        cx.tt(pre[:], pre[:], sA[:], ALU.subtract, (pre_k, sA_k), (pre_k,))
        a, ak, b, bk = sA, sA_k, sB, sB_k
        sft = 4
        while sft < NKB * 4:
            cx.copy(b[:, 0:sft], a[:, 0:sft], (ak,), (bk,), eng="dve")
            cx.tt(b[:, sft:], a[:, sft:], a[:, 0:NKB * 4 - sft], ALU.add, (ak,), (bk,))
            a, ak, b, bk = b, bk, a, ak
            sft *= 2
        cx.tt(cT[:].rearrange("p k h -> p (k h)"), pre[:], a[:], ALU.add, (pre_k, ak), (cT_k,))
        pc, pc_k = psS.next()
        cx.mm(pc[:], cm[:, 2, :], cT[:].rearrange("p k h -> p (k h)"), True, True, (cm_k, cT_k), (pc_k,))
        cx.copy(crefb[:].rearrange("p k h -> p (k h)"), pc[:], (pc_k,), (crefb_k,), eng="dve")
        bcol = Slots(cx, "bcol", 3, [128, 2, NKB], F32)

    for hl in range(nheads):
        vt, vk = Vt.next()
        vk = vkeys[id(vk)]
        for r in range(4):
            for cl in range(4):
                src = v_d[r, cl].rearrange("(kb p) v -> p kb v", p=128)
                cx.dma("sp", vt[:, r * 32 + cl * 8:r * 32 + cl * 8 + 8, 0:VD],
                       src[:, :, hl * VD:(hl + 1) * VD], vk[r * 4 + cl], (), (vk[r * 4 + cl],))
        kts = []
        for c in range(nmaps):
            u = hl * nmaps + c
            kt, kk = KT.next()
            kk = kkeys[id(kk)]
            for r in range(4):
                for rt in range(4):
                    cx.dma("pool", kt[:, r * 4096 + rt * 1024:r * 4096 + (rt + 1) * 1024], kT_d[rt, r, u],
                           kk[r * 4 + rt], (), (kk[r * 4 + rt],))
            kts.append((kt, kk))
        tasks = []
        for qt in range(S // 512):
            for c in range(nmaps):
                kb, nfar = 0, (max(0, 4 * qt - 1) if diff else 0)
                while kb < 4 * qt + 4:
                    if kb + 1 < nfar:
                        tasks.append((qt, c, (kb, kb + 1)))
                        kb += 2
                    else:
                        tasks.append((qt, c, (kb,)))
                        kb += 1
        qtiles, ptiles, groups, ostage, bcols = {}, {}, {}, {}, {}

        def stage_s(i):
            nonlocal nssq
            qt, c, kbs = tasks[i]
            if qt not in qtiles:
                r, tq = qt // 8, (qt % 8) * 512
                q, qk = QT.next()
                for cc in range(nmaps):
                    cx.dma("sp", q[:, cc, :], qT_d[tq // 1024, r, hl * nmaps + cc, :, tq % 1024:tq % 1024 + 512],
                           qk, (), (qk,))
                qtiles[qt] = (q, qk)
                if not diff:
                    bc, bc_k = bcol.next()
                    nkb = 4 * qt + 4
                    for qh in range(FOXH):
                        refblk = 4 * qt + (2 if FOXH == 1 else 1 + 2 * qh)
                        cx.ts(bc[:, qh, 0:nkb], cT[:, 0:nkb, hl], -1.0, crefb[:, refblk, hl:hl + 1],
                              ALU.mult, ALU.add, (cT_k, crefb_k), (bc_k,))
                    bcols[qt] = (bc, bc_k)
            q, qk = qtiles[qt]
            kt, kk = kts[c]
            ps, ps_k = psS.next()
            p, pk = PT.next()
            for j, kb in enumerate(kbs):
                t = kb - 4 * qt
                ql = max(0, 128 * t)
                cx.mm(ps[:, j * 512 + ql:(j + 1) * 512], kt[:, kb * 128:(kb + 1) * 128], q[:, c, ql:512],
                      True, True, (kk[kb // 8], qk), (ps_k,))
            if diff:
                if t <= -2:
                    w_ = 512 * len(kbs)
                    cx.act(p[:, 0:w_], ps[:, 0:w_], AF.Exp, (ps_k, bw_k), (pk,), bias=bw[:, hl, 639:640], scale=SCALE)
                else:
                    tm, tmk = tmpS.next()
                    cx.stt(tm[:, ql:512], ps[:, ql:512], SCALE, bw[:, hl, ql - 128 * t:512 - 128 * t],
                           ALU.mult, ALU.add, (ps_k, bw_k), (tmk,))
                    cx.act(p[:, ql:512], tm[:, ql:512], AF.Exp, (tmk,), (pk,))
            else:
                bc, bc_k = bcols[qt]
                for qh in range(FOXH):
                    w_ = 512 // FOXH
                    lo, hi = max(ql, w_ * qh), w_ * (qh + 1)
                    if lo < hi:
                        cx.act(p[:, lo:hi], ps[:, lo:hi], AF.Exp, (ps_k, bc_k), (pk,),
                               bias=bc[:, qh, kb:kb + 1], scale=SCALE)
                if t >= 0:
                    cx.tt(p[:, ql:ql + 128], p[:, ql:ql + 128], tri[:], ALU.mult, (pk, tri_k), (pk,))
            ptiles[i] = (p, pk)

        def stage_av(i):
            nonlocal nssq
            qt, c, kbs = tasks[i]
            p, pk = ptiles.pop(i)
            if (qt, c) not in groups:
                groups[(qt, c)] = [psO.next() for _ in range(4)]
            Os = groups[(qt, c)]
            if qt not in ostage:
                ostage[qt] = ostg.next()
            os_, os_k = ostage[qt]
            for j, kb in enumerate(kbs):
                t = kb - 4 * qt
                for qs in range(max(t, 0), 4):
                    O, Ok = Os[qs]
                    cx.mm(O[:, 0:VD + 1], p[:, j * 512 + qs * 128:j * 512 + (qs + 1) * 128], vt[:, kb, :],
                          kb == 0, kb == 4 * qt + qs, (pk, vk[kb // 8]), (Ok,))
            if kb != 4 * qt + 3:
                return
            for qs in range(4):
                O, Ok = Os[qs]
                rc, rck = rec.next()
                cx.s.op("dve", lambda e, rc=rc, O=O: e.reciprocal(rc[:, 0:1], O[:, VD:VD + 1]), (Ok,), (rck,))
                if not diff:
                    cx.ts(os_[:, qs, :], O[:, 0:VD], rc[:, 0:1], None, ALU.mult, None, (Ok, rck), (os_k,))
                elif c == 0:
                    cx.ts(A0[:, qs, :], O[:, 0:VD], rc[:, 0:1], None, ALU.mult, None, (Ok, rck), (A0_k,))
                else:
                    cx.tt(rc[:, 1:2], rc[:, 0:1], nlam, ALU.mult, (rck, sm_k), (rck,))
                    cb_, cbk = comb.next()
                    cx.stt(cb_[:], O[:, 0:VD], rc[:, 1:2], A0[:, qs, :], ALU.mult, ALU.add,
                           (Ok, rck, A0_k), (cbk,))
                    j = nssq
                    nssq += 1
                    cx.act(junk[:], cb_[:], AF.Square, (cbk,), (junk_k, ssq_k), accum=ssq[:, j:j + 1])
                    cx.act(ssq[:, j:j + 1], ssq[:, j:j + 1], AF.Ln, (ssq_k, sm_k), (ssq_k,),
                           bias=epsb, scale=1.0 / 256)
                    cx.act(ssq[:, j:j + 1], ssq[:, j:j + 1], AF.Exp, (ssq_k,), (ssq_k,), scale=-0.5)
                    cx.stt(os_[:, qs, :], cb_[:], ssq[:, j:j + 1], gs[:], ALU.mult, ALU.mult,
                           (cbk, ssq_k, gs_k), (os_k,))
            if c == nmaps - 1:
                ci = qt // 2
                rds = (os_k,) if xo is None else (os_k, xo.tk(ci // 4, ci % 4))
                cx.dma("sp", o_d[qt * 512:(qt + 1) * 512, hl * VD:(hl + 1) * VD].rearrange("(qs p) v -> p qs v", p=128),
                       os_[:], os_k, rds, ())
                if xo is not None and hl == nheads - 1 and qt % 2 == 1:
                    xo.issue(cx, cck, (ci % 4,), gs=(ci // 4,))
                del qtiles[qt]

        LOOK = 1 if diff else 3
        for i in range(min(LOOK, len(tasks))):
            stage_s(i)
        for i in range(len(tasks)):
            if i + LOOK < len(tasks):
                stage_s(i + LOOK)
            stage_av(i)
    cx.s.emit()


TT = 1024


def row_phase(nc, es, layer, x_d, o_d, ident_d, wo_d, gm_d, win_d, wout_d, ex, gsem=None, tag=""):
    cx = Ctx(nc, es, gsem, tag)
    rk = RowKit(cx, ident_d, nss=128, wslots=2, hnslots=2)
    h, _ = cx.sb("h", [128, 8, 2048], F32)
    hk = [Tk(f"h{i}") for i in range(8)]
    aT, aT_k = cx.sb("aT", [128, 16, TT], BF16)
    hid = Slots(cx, "hid", 2, [128, 4, TT], BF16)
    g, g_k = cx.sb("g", [128, 2048], F32)
    obs = Slots(cx, "ob", 2, [128, 2048], BF16)
    rr = Slots(cx, "rr", 2, [128, 512], BF16)
    ost = Slots(cx, "ost", 4, [128, 512], BF16)
    if layer == 0:
        wf, wf_k = cx.sb("wf", [128, 16, 16], BF16)
        cx.dma("pool", wf[:], ex["wf"].rearrange("(kc p) n -> p kc n", p=128), wf_k, (), (wf_k,))
        bfb, bfb_k = cx.sb("bfb", [128, 16], F32)
        cx.dma("sp", bfb[:], ex["bf"], bfb_k, (), (bfb_k,))
        zs = Slots(cx, "zs", 2, [128, 16], F32)
        one, one_k = cx.sb("one", [128, 1], F32)
        cx.memset(one[:], 1.0, (one_k,))
    else:
        fin = Slots(cx, "fin", 1, [128, 2048], F32)

    cck = Tk("cc")
    xis_ = ex.get("xis") or (None, None, None)
    for rt in range(TOK // TT):
        r0 = rt * TT
        for tb in range(8):
            cx.dma("sp", h[:, tb, :], x_d[r0 + tb * 128:r0 + (tb + 1) * 128, :], hk[tb], (), (hk[tb],))
        for tb in range(8):
            ob, ob_k = obs.next()
            rw = r0 + tb * 128
            cx.dma("sp", ob[:, :].rearrange("p (g v) -> p g v", g=4),
                   o_d[:, rw // 1024, rw % 1024:rw % 1024 + 128, :].rearrange("g p v -> p g v"), ob_k, (), (ob_k,))
            rk.transpose_in(ob, ob_k, aT, aT_k, tb * 128)
        for wb in range(4):
            wt, wk = rk.load_w(wo_d, wb * 512, 512)
            if layer == 0 and wb == 1 and rt > 0 and ex.get("xis") is not None:
                for xi in ex["xis"]:
                    xi.issue(cx, cck, (rt - 1,), defer=rk)

            def cb(tb, pa, pa_k, wb=wb):
                hs = h[:, tb, wb * 512:(wb + 1) * 512]
                cx.tt(hs, hs, pa[:], ALU.add, (pa_k, hk[tb]), (hk[tb],))
            rk.lin_tm(aT, aT_k, TT, wt, wk, 512, cb)
        cx.dma("sp", g[:], gm_d, g_k, (), (g_k,))
        for tb in range(8):
            rk.norm_T(h[:, tb, :], hk[tb], g, g_k, aT, aT_k, tb * 128)
        for hb in range(DFF // 512):
            wt, wk = rk.load_w(win_d, hb * 512, 512)
            hd, hd_k = hid.next()

            def cb(oc, tt, pa, pa_k, hd=hd, hd_k=hd_k):
                r_, rk_ = rr.next()
                cx.act(r_[:], pa[:], AF.Relu, (pa_k,), (rk_,))
                cx.tt(hd[:, oc, tt * 512:(tt + 1) * 512], r_[:], r_[:], ALU.mult, (rk_,), (hd_k,))
            rk.lin_fm(aT, aT_k, TT, wt, wk, 512, cb)
            wt, wk = rk.load_w(wout_d, 0, 2048, nk=4, k0=hb * 4)
            wv = wt
            for tb in range(8):
                for cg in range(4):
                    pa, pa_k = rk.pacc.next()
                    for kc in range(4):
                        cx.mm(pa[:], hd[:, kc, tb * 128:(tb + 1) * 128], wv[:, kc, cg * 512:(cg + 1) * 512],
                              kc == 0, kc == 3, (hd_k, wk.for_kc(kc)), (pa_k,))
                    hs = h[:, tb, cg * 512:(cg + 1) * 512]
                    cx.tt(hs, hs, pa[:], ALU.add, (pa_k, hk[tb]), (hk[tb],))
        if layer == 0:
            for tb in range(8):
                cx.dma("sp", ex["h_out"][r0 + tb * 128:r0 + (tb + 1) * 128, :], h[:, tb, :], hk[tb], (hk[tb],), ())
            cx.dma("sp", g[:], ex["kvg"], g_k, (), (g_k,))
            for tb in range(8):
                rk.norm_T(h[:, tb, :], hk[tb], g, g_k, aT, aT_k, tb * 128)
            for tb in range(8):
                pa, pa_k = rk.pacc.next()
                for kc in range(16):
                    cx.mm(pa[:, 0:16], aT[:, kc, tb * 128:(tb + 1) * 128], wf[:, kc, :], kc == 0, kc == 15,
                          (aT_k, wf_k), (pa_k,))
                z, zk = zs.next()
                cx.tt(z[:], pa[:, 0:16], bfb[:], ALU.add, (pa_k, bfb_k), (zk,))
                cx.act(z[:], z[:], AF.Exp, (zk,), (zk,), scale=-1.0)
                cx.act(z[:], z[:], AF.Ln, (zk, one_k), (zk,), bias=one[:, 0:1])
                cx.ts(z[:], z[:], -1.0, None, ALU.mult, None, (zk,), (zk,))
                for gg in range(4):
                    cx.dma("sp", ex["logf"][gg, r0 + tb * 128:r0 + (tb + 1) * 128, :], z[:, gg * 4:(gg + 1) * 4],
                           zk, (zk,), ())
            def early_kv(si, wb, rt=rt):
                if rt == 3 and xis_[0] is not None and si == 1 and wb == 1:
                    xis_[1].issue(cx, cck, (3,), defer=rk)

            def early_q(si, wb, rt=rt):
                if rt == 3 and xis_[0] is not None and wb == 1:
                    xis_[2].issue(cx, cck, (3,), defer=rk)
            proj_qkv(cx, rk, aT, aT_k, TT, r0, ost,
                     [("fm", ex["wk"], 0, ex["k2T"], xis_[1]), ("tm", ex["wv"], 0, ex["v2"], xis_[2])], early_kv)
            cx.dma("sp", g[:], ex["qg"], g_k, (), (g_k,))
            for tb in range(8):
                rk.norm_T(h[:, tb, :], hk[tb], g, g_k, aT, aT_k, tb * 128)
            proj_qkv(cx, rk, aT, aT_k, TT, r0, ost, [("fm", ex["wq"], 0, ex["q2T"], xis_[0])], early_q)
        else:
            cx.dma("sp", g[:], ex["fg"], g_k, (), (g_k,))
            for tb in range(8):
                f_, fk = fin.next()
                rs = rk.norm_stats(h[:, tb, :], hk[tb])
                cx.stt(f_[:], h[:, tb, :], rs, g[:], ALU.mult, ALU.mult, (hk[tb], rk.rs_k, g_k), (fk,))
                cx.dma("sp", ex["out"][r0 + tb * 128:r0 + (tb + 1) * 128, :], f_[:], fk, (fk,), ())
    rk.drain()
    cx.s.emit()


def _t5_bucket_np(rel):
    half, max_exact = 16, 8
    ret = np.where(rel > 0, half, 0)
    n = np.abs(rel)
    nf = np.maximum(n, 1).astype(np.float32)
    large = max_exact + (np.log(nf / np.float32(max_exact)) / np.float32(math.log(128 / max_exact))
                         * np.float32(half - max_exact)).astype(np.int32)
    large = np.minimum(large, half - 1)
    return ret + np.where(n < max_exact, n, large)


def _bc(v, n=128):
    v = np.asarray(v, np.float32).reshape(1, -1)
    return np.ascontiguousarray(np.broadcast_to(v, (n, v.shape[1])))


def _a2a(outs, key, b):
    return [np.ascontiguousarray(np.stack([outs[b * 4 + r][key][g] for r in range(4)])) for g in range(4)]


def _launch(build, in_maps):
    nc = bass.Bass("TRN2", target_bir_lowering=False)
    with ExitStack() as es:
        build(nc, es)
    res = run_bass_kernel_spmd(nc, in_maps, core_ids=list(range(NCORE)))
    return res.results


def _din(nc, name, shape, dt=F32):
    return nc.dram_tensor(name, list(shape), dt, kind="ExternalInput").ap()


def _dout(nc, name, shape, dt=F32):
    return nc.dram_tensor(name, list(shape), dt, kind="ExternalOutput").ap()


IDENT = np.eye(128, dtype=np.float32).astype(ml_dtypes.bfloat16)


def build_a(nc, es):
    phase_a(nc, es, _din(nc, "x", [TOK, D]), _din(nc, "g", [128, D]), _din(nc, "wqkv", [D, 3 * D]),
            _din(nc, "ident", [128, 128], BF16), _dout(nc, "qT", [16, 128, TOK], BF16),
            _dout(nc, "kT", [16, 128, TOK], BF16), _dout(nc, "v", [4, TOK, 512], BF16))


def build_attn(mode):
    def build(nc, es):
        aux = {}
        if mode == "diff":
            aux["bias"] = _din(nc, "bias", [128, 2, 640])
            aux["subg"] = _din(nc, "subg", [128, 256])
            aux["lam"] = _din(nc, "lam", [128, 4, 128])
        else:
            aux["trimask"] = _din(nc, "trimask", [128, 128], BF16)
            aux["cmats"] = _din(nc, "cmats", [128, 3, 128])
            aux["logf"] = _din(nc, "logf", [4, TOK, 4])
        attention(nc, es, mode, _din(nc, "qT", [4, 4, 128, TOK], BF16), _din(nc, "kT", [4, 4, 128, TOK], BF16),
                  _din(nc, "v", [4, TOK, 512], BF16), _dout(nc, "o", [S, 512], BF16), aux)
    return build


def build_row(layer):
    def build(nc, es):
        ex = {}
        if layer == 0:
            for nm in ("kvg", "qg"):
                ex[nm] = _din(nc, nm, [128, D])
            for nm in ("wk", "wv", "wq"):
                ex[nm] = _din(nc, nm, [D, D])
            ex["wf"] = _din(nc, "wf", [D, 16])
            ex["bf"] = _din(nc, "bf", [128, 16])
            ex["h_out"] = _dout(nc, "h_out", [TOK, D])
            ex["q2T"] = _dout(nc, "q2T", [16, 128, TOK], BF16)
            ex["k2T"] = _dout(nc, "k2T", [16, 128, TOK], BF16)
            ex["v2"] = _dout(nc, "v2", [4, TOK, 512], BF16)
            ex["logf"] = _dout(nc, "logf", [4, TOK, 4])
        else:
            ex["fg"] = _din(nc, "fg", [128, D])
            ex["out"] = _dout(nc, "out", [TOK, D])
        row_phase(nc, es, layer, _din(nc, "x", [TOK, D]), _din(nc, "o", [4, TOK, 512], BF16),
                  _din(nc, "ident", [128, 128], BF16), _din(nc, "wo", [D, D]), _din(nc, "gm", [128, D]),
                  _din(nc, "win", [D, DFF]), _din(nc, "wout", [DFF, D]), ex)
    return build


I32 = mybir.dt.int32


class SemPool:
    def __init__(self, stack):
        self.stack = stack
        self.free = []
        self.regs = []


GROUPS = [[0, 1, 2, 3], [4, 5, 6, 7]]


FOXH = 2
CHK = 524288


def comm_phase(nc, es, gsem, tag, cid_d, items):
    cx = Ctx(nc, es, gsem, tag)
    cid, cid_k = cx.sb("cid", [1, 8], I32)
    cx.dma("sp", cid[:], cid_d, cid_k, (), (cid_k,))
    if not gsem.regs:
        gsem.regs = [gsem.stack.enter_context(nc.gpsimd.register(f"cr{i}")) for i in range(6)]
        for i in range(6):
            o = cx.s.op("pool", lambda e, i=i: e.reg_load(gsem.regs[i], cid[:1, i:i + 1]), (cid_k,), ())
            o.nosig = True
    regs = gsem.regs
    cck = Tk("cc")
    inner = [[8192, CHK // 8192], [1, 8192]]
    for n, it in enumerate(items):
        gk, dk = Tk(f"g{n}"), Tk(f"l{n}")
        if isinstance(it, XItem):
            it.tks = {}
            it.issue(cx, cck, (0, 1, 2, 3))
            gall, dst = it.gall, it.dst
            cx.s.op("pool", lambda e, dst=dst, gall=gall: e.dma_start(
                out=bass.AP(dst, 0, [[32768, 16 * CHK // 32768], [1, 32768]]),
                in_=bass.AP(gall, regs[0], [[32768, 16 * CHK // 32768], [1, 32768]])),
                tuple(it.tks.values()), (dk,), dma=dk)
            it.reset()
        else:
            _, src, gath, dst, ri, rstride, nel = it
            cx.s.op("pool", lambda e, src=src, gath=gath: e.collective_compute(
                "AllGather", ALU.bypass, replica_groups=GROUPS, ins=[src.ap().opt()], outs=[gath.ap().opt()]),
                (), (gk,), dma=cck, inc=1)
            cx.s.op("pool", lambda e, dst=dst, gath=gath, ri=ri, rstride=rstride, nel=nel: e.dma_start(
                out=bass.AP(dst, 0, [[nel, 4], [1, nel]]), in_=bass.AP(gath, regs[ri], [[rstride, 4], [1, nel]])),
                (gk,), (dk,), dma=dk)
    cx.s.emit()


def build_fused(nc, upto=9):
    di = lambda name, shape, dt=F32: nc.dram_tensor(name, list(shape), dt, kind="ExternalInput").ap()
    x_d = di("x", [TOK, D])
    ident_d = di("ident", [128, 128], BF16)
    cid_d = di("cid", [1, 8], I32)
    g_attn0, g_attn1, g_mlp0, g_mlp1, g_kv, g_fin = [di(n, [128, D]) for n in
                                                     ("g_attn0", "g_attn1", "g_mlp0", "g_mlp1", "g_kv", "g_fin")]
    wqkv = di("wqkv", [D, 3 * D])
    wo0, wo1, wk, wv, wq = [di(n, [D, D]) for n in ("wo0", "wo1", "wk", "wv", "wq")]
    win0, win1 = di("win0", [D, DFF]), di("win1", [D, DFF])
    wout0, wout1 = di("wout0", [DFF, D]), di("wout1", [DFF, D])
    wf, bf = di("wf", [D, 16]), di("bf", [128, 16])
    aux0 = {"bias": di("bias", [128, 2, 640]), "subg": di("subg", [128, 256]), "lam": di("lam", [128, 4, 128])}
    trimask, cmats = di("trimask", [128, 128], BF16), di("cmats", [128, 3, 128])
    out_d = nc.dram_tensor("out", [TOK, D], F32, kind="ExternalOutput").ap()
    dt_ = lambda name, shape, dt=BF16: nc.dram_tensor(name, list(shape), dt)
    qT, kT, v = dt_("i_qT", [8192, 1024]), dt_("i_kT", [8192, 1024]), dt_("i_v", [4 * TOK, 512])
    Gq, Gk, Gv = dt_("i_Gq", [8192, TOK]), dt_("i_Gk", [8192, TOK]), dt_("i_Gv", [16 * TOK, 512])
    Lq, Lk, Lv = dt_("i_Lq", [8192, 1024]), dt_("i_Lk", [8192, 1024]), dt_("i_Lv", [4 * TOK, 512])
    o, Lo = dt_("i_o", [S, 512]), dt_("i_Lo", [4 * TOK, 512])
    h1 = dt_("i_h1", [TOK, D], F32)
    logf, Glogf, Llogf = dt_("i_logf", [4 * TOK, 4], F32), dt_("i_Glogf", [16 * TOK, 4], F32), dt_("i_Llogf", [4 * TOK, 4], F32)
    u3 = lambda t: t.ap().rearrange("(g rt ul d) t -> g rt ul d t", g=4, rt=4, ul=4)
    r4 = lambda t: t.ap().rearrange("(rt r ul d) t -> rt r ul d t", rt=4, r=4, ul=4)
    xq, xk_, xv = XItem(qT, Gq, Lq), XItem(kT, Gk, Lk), XItem(v, Gv, Lv)
    xo = XItem(o, Gv, Lo)
    c4 = lambda t: t.ap().rearrange("(c r i) v -> r c i v", c=4, r=4)
    g3 = lambda t: t.ap().rearrange("(g t) v -> g t v", g=4)
    LG = ("small", logf, Glogf, Llogf, 1, 4 * TOK * 4, TOK * 4)
    with ExitStack() as gstack:
        gsem = SemPool(gstack)
        if upto > 0:
            with ExitStack() as es:
                phase_a(nc, es, x_d, g_attn0, wqkv, ident_d, u3(qT), u3(kT), g3(v), gsem, "A_", (xq, xk_, xv))
        if upto > 1:
            with ExitStack() as es:
                comm_phase(nc, es, gsem, "X1_", cid_d, [xq, xk_, xv])
        if upto > 2:
            with ExitStack() as es:
                attention(nc, es, "diff", r4(Lq), r4(Lk), c4(Lv), o.ap(), aux0, gsem, "B1_", xo)
        if upto > 3:
            with ExitStack() as es:
                comm_phase(nc, es, gsem, "X2_", cid_d, [xo])
        if upto > 4:
            with ExitStack() as es:
                ex = {"kvg": g_kv, "qg": g_attn1, "wk": wk, "wv": wv, "wq": wq, "wf": wf, "bf": bf,
                      "h_out": h1.ap(), "q2T": u3(qT), "k2T": u3(kT), "v2": g3(v), "logf": g3(logf), "xis": (xq, xk_, xv)}
                row_phase(nc, es, 0, x_d, c4(Lo), ident_d, wo0, g_mlp0, win0, wout0, ex, gsem, "B2_")
        if upto > 5:
            with ExitStack() as es:
                comm_phase(nc, es, gsem, "X3_", cid_d, [xq, xk_, xv, LG])
        if upto > 6:
            with ExitStack() as es:
                aux1 = {"trimask": trimask, "cmats": cmats, "logf": g3(Llogf)}
                attention(nc, es, "fox", r4(Lq), r4(Lk), c4(Lv), o.ap(), aux1, gsem, "C1_", xo)
        if upto > 7:
            with ExitStack() as es:
                comm_phase(nc, es, gsem, "X4_", cid_d, [xo])
        if upto > 8:
            with ExitStack() as es:
                row_phase(nc, es, 1, h1.ap(), c4(Lo), ident_d, wo1, g_mlp1, win1, wout1,
                          {"fg": g_fin, "out": out_d}, gsem, "C2_")
        if upto < 9:
            with ExitStack() as es:
                cx = Ctx(nc, es, gsem, "Z_")
                k = Tk("z")
                cx.dma("sp", out_d, x_d, k, (), (k,))
                cx.s.emit()


def kernel(x, rel_bias_table, attn_norm_g, mlp_norm_g, w_qkv_a, lam_q1, lam_k1, lam_q2, lam_k2,
           subln_g, w_o_a, kv_norm_g, w_k_b, w_v_b, w_f_b, b_f_b, w_q_b, w_o_b, w_mlp_in, w_mlp_out,
           final_norm_g, _upto=9):
    f32 = lambda a: np.ascontiguousarray(np.asarray(a, np.float32))
    x = f32(x)
    kk = np.arange(128)[:, None]
    jj = np.arange(640)[None, :]
    bias_all = f32(rel_bias_table)[_t5_bucket_np(kk - jj)]
    tri = (np.arange(128)[None, :] >= np.arange(128)[:, None]).astype(np.float32)
    cm = np.zeros((128, 3, 128), np.float32)
    cm[:, 0, :] = tri
    cm[127, 1, :] = 1.0
    cm[0, 2, :] = 1.0
    shared = {
        "ident": IDENT, "g_attn0": _bc(attn_norm_g[0]), "g_attn1": _bc(attn_norm_g[1]),
        "g_mlp0": _bc(mlp_norm_g[0]), "g_mlp1": _bc(mlp_norm_g[1]), "g_kv": _bc(kv_norm_g),
        "g_fin": _bc(final_norm_g), "wqkv": f32(w_qkv_a[0]), "wo0": f32(w_o_a[0]), "wo1": f32(w_o_b[0]),
        "wk": f32(w_k_b), "wv": f32(w_v_b), "wq": f32(w_q_b[0]), "win0": f32(w_mlp_in[0]),
        "win1": f32(w_mlp_in[1]), "wout0": f32(w_mlp_out[0]), "wout1": f32(w_mlp_out[1]),
        "wf": f32(w_f_b), "bf": _bc(b_f_b), "subg": _bc(subln_g[0]),
        "lam": np.ascontiguousarray(np.broadcast_to(
            np.stack([f32(lam_q1[0]), f32(lam_k1[0]), f32(lam_q2[0]), f32(lam_k2[0])])[None], (128, 4, 128))),
        "trimask": tri.astype(ml_dtypes.bfloat16), "cmats": cm,
    }
    in_maps = []
    for c in range(NCORE):
        b, g = c // 4, c % 4
        m = dict(shared)
        m["x"] = np.ascontiguousarray(x[b, g * TOK:(g + 1) * TOK])
        m["bias"] = np.ascontiguousarray(bias_all[:, :, 2 * g:2 * g + 2].transpose(0, 2, 1))
        cid = np.zeros((1, 8), np.int32)
        cid[0, 0] = g * 16 * CHK
        for r in range(4):
            cid[0, 2 + r] = g * 16 * CHK + r * CHK
        cid[0, 1] = g * TOK * 4
        m["cid"] = cid
        in_maps.append(m)
    nc = bass.Bass("TRN2", target_bir_lowering=False)
    build_fused(nc, _upto)
    res = run_bass_kernel_spmd(nc, in_maps, core_ids=list(range(NCORE))).results
    out = np.empty((NB, S, D), np.float32)
    for c in range(NCORE):
        out[c // 4, (c % 4) * TOK:(c % 4 + 1) * TOK] = res[c]["out"]
    return out
```

```python
import math
from contextlib import ExitStack

import numpy as np
import ml_dtypes
import concourse.bass as bass
import concourse.mybir as mybir
from concourse.bass_utils import run_bass_kernel_spmd

F32 = mybir.dt.float32
BF16 = mybir.dt.bfloat16
AF = mybir.ActivationFunctionType
ALU = mybir.AluOpType
AX = mybir.AxisListType

D = 2048
S = 16384
NB = 2
DFF = 8192
NCORE = 8
TOK = 4096
CH = 2048
NEG = -30000.0
SCALE = 128 ** -0.5
EPS = 1e-6
SUBEPS = 1e-5
LAMBDA_INIT = 0.8 - 0.6 * math.exp(-0.3 * 0)


class Tk:
    __slots__ = ("name", "w", "r", "semcnt")

    def __init__(self, name):
        self.name = name
        self.w = None
        self.r = []
        self.semcnt = 0


class Op:
    __slots__ = ("eng", "fn", "deps", "needed", "is_dma", "tk", "val", "key", "inc", "nosig")


ENGS = ("pe", "act", "dve", "pool", "sp")
BLK = {"pe": "tensor", "act": "scalar", "dve": "vector", "pool": "gpsimd", "sp": "sync"}


class Sched:
    def __init__(self, nc, gsem=None, tag=""):
        self.nc = nc
        self.gsem = gsem
        self.tag = tag
        self.ops = {e: [] for e in ENGS}
        self.all = []

    def op(self, eng, fn, reads=(), writes=(), dma=None, inc=16):
        o = Op()
        o.inc = inc
        o.nosig = False
        o.eng = eng
        o.fn = fn
        o.needed = False
        o.is_dma = dma is not None
        o.tk = dma
        o.val = 0
        o.key = None
        deps = []
        for t in reads:
            if t.w is not None:
                deps.append(t.w)
        for t in writes:
            if t.w is not None:
                deps.append(t.w)
            deps.extend(t.r)
        o.deps = deps
        for d in deps:
            d.needed = True
        for t in reads:
            t.r.append(o)
        for t in writes:
            t.w = o
            t.r = []
        self.ops[eng].append(o)
        self.all.append(o)
        return o

    def emit(self):
        nc = self.nc
        engcnt = {e: 0 for e in ENGS}
        keys = {}
        for o in self.all:
            if o.is_dma:
                o.tk.semcnt += o.inc
                o.val = o.tk.semcnt
                o.key = ("t", id(o.tk))
                keys[o.key] = "d_" + o.tk.name
            elif o.needed:
                engcnt[o.eng] += 1
                o.val = engcnt[o.eng]
                o.key = ("e", o.eng)
                keys[o.key] = "e_" + o.eng
        if self.gsem is not None:
            for e in ENGS:
                for o in reversed(self.ops[e]):
                    if not o.is_dma and not o.nosig:
                        if not o.needed:
                            o.needed = True
                            engcnt[e] += 1
                            o.val = engcnt[e]
                            o.key = ("e", e)
                            keys[o.key] = "e_" + e
                        break
        final = {}
        for o in self.all:
            if o.key is not None:
                final[o.key] = max(final.get(o.key, 0), o.val)
        with ExitStack() as es:
            sems, base = {}, {}
            for k, n in keys.items():
                if self.gsem is None:
                    sems[k], base[k] = es.enter_context(nc.semaphore(self.tag + n)), 0
                elif self.gsem.free:
                    sems[k], base[k] = self.gsem.free.pop()
                else:
                    sems[k], base[k] = self.gsem.stack.enter_context(nc.semaphore(self.tag + n)), 0
            block = es.enter_context(nc.Block())
            for en in ENGS:
                def body(eng, en=en):
                    waited = {}
                    for o in self.ops[en]:
                        for d in o.deps:
                            if en == "pe" and d.eng == "pe" and not d.is_dma:
                                continue
                            if waited.get(d.key, 0) >= d.val:
                                continue
                            eng.wait_ge(sems[d.key], base[d.key] + d.val)
                            waited[d.key] = d.val
                        ins = o.fn(eng)
                        if o.is_dma:
                            ins.then_inc(sems[o.key], o.inc)
                        elif o.needed:
                            ins.then_inc(sems[o.key], 1)
                    if en == "sp" or self.gsem is not None:
                        for k, v in final.items():
                            if waited.get(k, 0) < v:
                                eng.wait_ge(sems[k], base[k] + v)
                getattr(block, BLK[en])(body)
            if self.gsem is not None:
                for k in keys:
                    self.gsem.free.append((sems[k], base[k] + final[k]))


class Ctx:
    def __init__(self, nc, es, gsem=None, tag=""):
        self.nc = nc
        self.es = es
        self.tag = tag
        self.s = Sched(nc, gsem, tag)
        self.rr = 0

    def sb(self, name, shape, dt):
        t = self.es.enter_context(self.nc.sbuf_tensor("sb_" + self.tag + name, list(shape), dt))
        return t, Tk(name)

    def ps(self, name, shape, dt=F32):
        t = self.es.enter_context(self.nc.psum_tensor("ps_" + self.tag + name, list(shape), dt))
        return t, Tk(name)

    def dma(self, q, out, in_, tk, reads=(), writes=()):
        return self.s.op(q, lambda e: e.dma_start(out=out, in_=in_), reads, writes, dma=tk)

    def mm(self, out, lhsT, rhs, start, stop, reads, writes):
        return self.s.op("pe", lambda e: e.matmul(out, lhsT, rhs, start=start, stop=stop),
                         reads, writes)

    def tr(self, out, in_, ident, reads, writes):
        return self.s.op("pe", lambda e: e.transpose(out, in_, ident), reads, writes)

    def act(self, out, in_, func, reads, writes, bias=0.0, scale=1.0, accum=None):
        if accum is None:
            return self.s.op("act", lambda e: e.activation(out, in_, func, bias=bias, scale=scale),
                             reads, writes)
        return self.s.op("act", lambda e: e.activation(out, in_, func, bias=bias, scale=scale,
                                                       accum_out=accum), reads, writes)

    def copy(self, out, in_, reads, writes, eng=None):
        if eng is None:
            self.rr ^= 1
            eng = "act" if self.rr else "dve"
        if eng == "act":
            return self.s.op("act", lambda e: e.copy(out, in_), reads, writes)
        return self.s.op(eng, lambda e: e.tensor_copy(out, in_), reads, writes)

    def ts(self, out, in0, s1, s2, op0, op1, reads, writes, eng="dve"):
        if op1 is None:
            return self.s.op(eng, lambda e: e.tensor_scalar(out, in0, s1, s2, op0), reads, writes)
        return self.s.op(eng, lambda e: e.tensor_scalar(out, in0, s1, s2, op0, op1), reads, writes)

    def stt(self, out, in0, scalar, in1, op0, op1, reads, writes, eng="dve"):
        return self.s.op(eng, lambda e: e.scalar_tensor_tensor(out, in0, scalar, in1, op0, op1),
                         reads, writes)

    def tt(self, out, in0, in1, op, reads, writes, eng="dve"):
        return self.s.op(eng, lambda e: e.tensor_tensor(out, in0, in1, op), reads, writes)

    def memset(self, ap, val, writes, eng="dve"):
        return self.s.op(eng, lambda e: e.memset(ap, val), (), writes)


class Slots:
    def __init__(self, cx, name, n, shape, dt, psum=False):
        self.items = []
        for i in range(n):
            self.items.append(cx.ps(f"{name}{i}", shape, dt) if psum else cx.sb(f"{name}{i}", shape, dt))
        self.i = 0

    def next(self):
        it = self.items[self.i % len(self.items)]
        self.i += 1
        return it


class WKeys:
    def __init__(self, tks, st):
        self.tks, self.st = tks, st

    def for_kc(self, kc):
        return self.tks[kc // self.st]


class RowKit:
    def __init__(self, cx, ident_d, nss=64, wslots=3, hnslots=2):
        self.cx = cx
        self.ident, self.ident_k = cx.sb("ident", [128, 128], BF16)
        cx.dma("sp", self.ident[:], ident_d, self.ident_k, (), (self.ident_k,))
        self.w = Slots(cx, "w", wslots, [128, 8192], BF16)
        self.pacc = Slots(cx, "pacc", 4, [128, 512], F32, psum=True)
        self.ptr = Slots(cx, "ptr", 2, [128, 1024], BF16, psum=True)
        self.hn = Slots(cx, "hn", hnslots, [128, 2048], BF16)
        self.ss, self.ss_k = cx.sb("ss", [128, nss], F32)
        self.rs, self.rs_k = cx.sb("rs", [128, nss], F32)
        cx.memset(self.ss[:], 0.0, (self.ss_k,))
        self.epsb, self.eps_k = cx.sb("epsb", [128, 2], F32)
        cx.memset(self.epsb[:, 0:1], EPS, (self.eps_k,))
        cx.memset(self.epsb[:, 1:2], SUBEPS, (self.eps_k,))
        self.eps_main = self.epsb[:, 0:1]
        self.eps_sub = self.epsb[:, 1:2]
        self.nss = 0
        self.wsub = {}
        self.pending = []
        self.per_load = 1

    def drain(self, n=None):
        while self.pending and (n is None or n > 0):
            self.pending.pop(0)()
            if n is not None:
                n -= 1

    def norm_stats(self, x_ap, x_k, junk=None, junk_k=None):
        cx = self.cx
        j = self.nss
        self.nss += 1
        ss = self.ss[:, j:j + 1]
        rs = self.rs[:, j:j + 1]
        if junk is None:
            junk, junk_k = self.hn.next()
        cx.act(junk[:], x_ap, AF.Square, (x_k,), (junk_k, self.ss_k), accum=ss)
        cx.act(rs, ss, AF.Ln, (self.ss_k, self.eps_k), (self.rs_k,), bias=self.eps_main, scale=1.0 / D)
        cx.act(rs, rs, AF.Exp, (self.rs_k,), (self.rs_k,), scale=-0.5)
        return rs

    def norm_T(self, x_ap, x_k, g, g_k, dstT, dstT_k, tcol):
        cx = self.cx
        hn, hn_k = self.hn.next()
        rs = self.norm_stats(x_ap, x_k, hn, hn_k)
        cx.stt(hn[:], x_ap, rs, g[:], ALU.mult, ALU.mult, (x_k, self.rs_k, g_k), (hn_k,))
        self.transpose_in(hn, hn_k, dstT, dstT_k, tcol)

    def transpose_in(self, src, src_k, dstT, dstT_k, tcol, nchunk=16, c0=0):
        cx = self.cx
        for half in range(0, nchunk, 8):
            n = min(8, nchunk - half)
            pt, pt_k = self.ptr.next()
            for i in range(n):
                cx.tr(pt[:, i * 128:(i + 1) * 128], src[:, (half + i) * 128:(half + i + 1) * 128],
                      self.ident[:], (src_k, self.ident_k), (pt_k,))
            cx.copy(dstT[:, c0 + half:c0 + half + n, tcol:tcol + 128],
                    pt[:, 0:n * 128].rearrange("p (c t) -> p c t", t=128), (pt_k,), (dstT_k,))

    def load_w(self, w_d, c0, ncols, nk=16, k0=0):
        cx = self.cx
        wt, wk = self.w.next()
        if id(wk) not in self.wsub:
            self.wsub[id(wk)] = [Tk(wk.name + f"_{i}") for i in range(4)]
        subs = self.wsub[id(wk)]
        wv_s = wt[:, 0:nk * ncols].rearrange("p (k n) -> p k n", n=ncols)
        wv = w_d.rearrange("(kc p) n -> p kc n", p=128)
        st = max(1, nk // 4)
        for i, q in enumerate(range(0, nk, st)):
            cx.dma("pool", wv_s[:, q:q + st, :], wv[:, k0 + q:k0 + q + st, c0:c0 + ncols], subs[i], (), (subs[i],))
        self.drain(self.per_load)
        return wv_s, WKeys(subs, st)

    def lin_fm(self, srcT, srcT_k, ntok, wt, wk, ncols, cb, nk=16):
        cx = self.cx
        for oc in range(ncols // 128):
            for tt in range(ntok // 512):
                pa, pa_k = self.pacc.next()
                for kc in range(nk):
                    cx.mm(pa[:], wt[:, kc, oc * 128:(oc + 1) * 128], srcT[:, kc, tt * 512:(tt + 1) * 512],
                          kc == 0, kc == nk - 1, (wk.for_kc(kc), srcT_k), (pa_k,))
                cb(oc, tt, pa, pa_k)

    def lin_tm(self, srcT, srcT_k, ntok, wt, wk, ncols, cb, nk=16):
        cx = self.cx
        for tb in range(ntok // 128):
            pa, pa_k = self.pacc.next()
            for kc in range(nk):
                cx.mm(pa[:, 0:ncols], srcT[:, kc, tb * 128:(tb + 1) * 128], wt[:, kc, 0:ncols],
                      kc == 0, kc == nk - 1, (wk.for_kc(kc), srcT_k), (pa_k,))
            cb(tb, pa, pa_k)


class XItem:
    def __init__(self, src, gall, dst):
        self.src, self.gall, self.dst = src, gall, dst
        self.issued = set()
        self.tks = {}

    def reset(self):
        self.issued = set()
        self.tks = {}

    def tk(self, g, rt):
        if (g, rt) not in self.tks:
            self.tks[(g, rt)] = Tk(f"x{g}{rt}")
        return self.tks[(g, rt)]

    def issue(self, cx, cck, rts, gs=(0, 1, 2, 3), defer=None):
        for rt in rts:
            for g in gs:
                ci = 4 * g + rt
                if ci in self.issued:
                    continue
                self.issued.add(ci)
                src, gall = self.src, self.gall

                def rec(src=src, gall=gall, ci=ci, tk=self.tk(g, rt)):
                    cx.s.op("pool", lambda e: e.collective_compute(
                        "AllGather", ALU.bypass, replica_groups=GROUPS,
                        ins=[bass.AP(src, ci * CHK, [[8192, CHK // 8192], [1, 8192]])],
                        outs=[bass.AP(gall, ci * 4 * CHK, [[8192, 4 * CHK // 8192], [1, 8192]])]),
                        (), (tk,), dma=cck, inc=1)
                if defer is None:
                    rec()
                else:
                    defer.pending.append(rec)


def proj_qkv(cx, rk, hT, hT_k, ntok, t0, ost, specs, after_load=None):
    for si, (kind, w_d, col0, dst, xi) in enumerate(specs):
        for wb in range(4):
            wt, wk = rk.load_w(w_d, col0 + wb * 512, 512)
            if after_load is not None:
                after_load(si, wb)
            if kind == "fm":
                def cb(oc, tt, pa, pa_k, dst=dst, wb=wb, xi=xi):
                    o, ok = ost.next()
                    cx.copy(o[:], pa[:], (pa_k,), (ok,))
                    tok = t0 + tt * 512
                    rds = (ok,) if xi is None else (ok, xi.tk(wb, tok // 1024))
                    cx.dma("sp", dst[wb, tok // 1024, oc, :, tok % 1024:tok % 1024 + 512], o[:], ok, rds, ())
                rk.lin_fm(hT, hT_k, ntok, wt, wk, 512, cb)
            else:
                def cb(tb, pa, pa_k, dst=dst, wb=wb, xi=xi):
                    o, ok = ost.next()
                    cx.copy(o[:], pa[:], (pa_k,), (ok,))
                    tok = t0 + tb * 128
                    rds = (ok,) if xi is None else (ok, xi.tk(wb, tok // 1024))
                    cx.dma("sp", dst[wb, tok:tok + 128, :], o[:], ok, rds, ())
                rk.lin_tm(hT, hT_k, ntok, wt, wk, 512, cb)


def phase_a(nc, es, x_d, g_d, wqkv_d, ident_d, qT_d, kT_d, v_d, gsem=None, tag="", xis=(None, None, None)):
    cx = Ctx(nc, es, gsem, tag)
    rk = RowKit(cx, ident_d)
    rk.per_load = 2
    g, g_k = cx.sb("g", [128, 2048], F32)
    cx.dma("sp", g[:], g_d, g_k, (), (g_k,))
    xs = Slots(cx, "x", 2, [128, 2048], F32)
    hT, hT_k = cx.sb("hT", [128, 16, CH], BF16)
    ost = Slots(cx, "ost", 4, [128, 512], BF16)
    cck = Tk("cc")
    for half in range(2):
        t0 = half * CH
        if half == 1 and xis[0] is not None:
            for xi in xis:
                xi.issue(cx, cck, (0, 1), defer=rk)
        for tb in range(CH // 128):
            xt, xk = xs.next()
            cx.dma("sp", xt[:], x_d[t0 + tb * 128:t0 + (tb + 1) * 128, :], xk, (), (xk,))
            rk.norm_T(xt[:], xk, g, g_k, hT, hT_k, tb * 128)
        def early(si, wb, half=half):
            if half == 1 and xis[0] is not None and wb == 1 and si in (1, 2):
                xis[si - 1].issue(cx, cck, (2, 3), defer=rk)
        proj_qkv(cx, rk, hT, hT_k, CH, t0, ost,
                 [("fm", wqkv_d, 0, qT_d, xis[0]), ("fm", wqkv_d, 2048, kT_d, xis[1]),
                  ("tm", wqkv_d, 4096, v_d, xis[2])], early)
    rk.drain()
    cx.s.emit()


def attention(nc, es, mode, qT_d, kT_d, v_d, o_d, aux, gsem=None, tag="", xo=None):
    cx = Ctx(nc, es, gsem, tag)
    cck = Tk("cc")
    diff = mode == "diff"
    VD = 256 if diff else 128
    nheads = 2 if diff else 4
    nmaps = 2 if diff else 1
    NKB = S // 128
    KT = Slots(cx, "KT", 2, [128, S], BF16)
    Vt = Slots(cx, "Vt", 1 if diff else 2, [128, NKB, VD + 1], BF16)
    vkeys = {}
    for vt, vk in Vt.items:
        vkeys[id(vk)] = [Tk(vk.name + f"_{i}") for i in range(16)]
        cx.memset(vt[:, :, VD:VD + 1], 1.0, tuple(vkeys[id(vk)]), eng="pool")
    kkeys = {id(kk_): [Tk(kk_.name + f"_{i}") for i in range(16)] for _, kk_ in KT.items}
    QT = Slots(cx, "QT", 2, [128, nmaps, 512], BF16)
    if diff and PAIR:
        PT = Slots(cx, "PT", 3, [128, 1024], BF16)
        psS = Slots(cx, "psS", 2, [128, 1024], F32, psum=True)
    else:
        PT = Slots(cx, "PT", 5, [128, 512], BF16)
        psS = Slots(cx, "psS", 4, [128, 512], F32, psum=True)
    psO = Slots(cx, "psO", 4, [128, 512], F32, psum=True)
    ostg = Slots(cx, "ostg", 2, [128, 4, VD], BF16)
    sm, sm_k = cx.sb("sm", [128, 16], F32)
    rec = Slots(cx, "rec", 4, [128, 2], F32)
    nssq = 0
    if diff:
        bw, bw_k = cx.sb("bw", [128, 2, 640], F32)
        cx.dma("sp", bw[:], aux["bias"], bw_k, (), (bw_k,))
        cx.memset(bw[64:128, :, 0:64], NEG, (bw_k,))
        tmpS = Slots(cx, "tmpS", 2, [128, 512], F32)
        A0, A0_k = cx.sb("A0", [128, 4, 256], F32)
        comb = Slots(cx, "comb", 2, [128, 256], F32)
        junk, junk_k = cx.sb("junk", [128, 256], BF16)
        gs, gs_k = cx.sb("gs", [128, 256], F32)
        cx.dma("sp", gs[:], aux["subg"], gs_k, (), (gs_k,))
        cx.ts(gs[:], gs[:], 1.0 - LAMBDA_INIT, None, ALU.mult, None, (gs_k,), (gs_k,))
        lamb, lamb_k = cx.sb("lamb", [128, 4, 128], F32)
        cx.dma("sp", lamb[:], aux["lam"], lamb_k, (), (lamb_k,))
        lp, lp_k = cx.sb("lp", [128, 2, 128], F32)
        cx.tt(lp[:, 0, :], lamb[:, 0, :], lamb[:, 1, :], ALU.mult, (lamb_k,), (lp_k,))
        cx.tt(lp[:, 1, :], lamb[:, 2, :], lamb[:, 3, :], ALU.mult, (lamb_k,), (lp_k,))
        cx.s.op("dve", lambda e: e.tensor_reduce(sm[:, 0:2], lp[:], AX.X, ALU.add), (lp_k,), (sm_k,))
        cx.act(sm[:, 2:4], sm[:, 0:2], AF.Exp, (sm_k,), (sm_k,))
        cx.tt(sm[:, 4:5], sm[:, 3:4], sm[:, 2:3], ALU.subtract, (sm_k,), (sm_k,))
        cx.ts(sm[:, 5:6], sm[:, 4:5], -LAMBDA_INIT, None, ALU.add, None, (sm_k,), (sm_k,))
        nlam = sm[:, 5:6]
        cx.memset(sm[:, 6:7], SUBEPS, (sm_k,))
        epsb = sm[:, 6:7]
        ssq, ssq_k = cx.sb("ssq", [128, 2 * 32 * 4], F32)
        cx.memset(ssq[:], 0.0, (ssq_k,))
    else:
        tri, tri_k = cx.sb("tri", [128, 128], BF16)
        cx.dma("sp", tri[:], aux["trimask"], tri_k, (), (tri_k,))
        cm, cm_k = cx.sb("cm", [128, 3, 128], F32)
        cx.dma("sp", cm[:], aux["cmats"], cm_k, (), (cm_k,))
        X, X_k = cx.sb("X", [128, NKB, 4], F32)
        for r in range(4):
            cx.dma("sp", X[:, r * 32:(r + 1) * 32, :],
                   aux["logf"][r].rearrange("(kb p) h -> p kb h", p=128), X_k, (), (X_k,))
        pre, pre_k = cx.sb("pre", [128, NKB * 4], F32)
        sA, sA_k = cx.sb("sA", [128, NKB * 4], F32)
        sB, sB_k = cx.sb("sB", [128, NKB * 4], F32)
        cT, cT_k = cx.sb("cT", [128, NKB, 4], F32)
        crefb, crefb_k = cx.sb("crefb", [128, NKB, 4], F32)
        Xf = X[:].rearrange("p k h -> p (k h)")
        pc, pc_k = psS.next()
        cx.mm(pc[:], cm[:, 0, :], Xf, True, True, (cm_k, X_k), (pc_k,))
        cx.copy(pre[:], pc[:], (pc_k,), (pre_k,), eng="dve")
        pc, pc_k = psS.next()
        cx.mm(pc[:], cm[:, 1, :], pre[:], True, True, (cm_k, pre_k), (pc_k,))
        cx.copy(sA[:], pc[:], (pc_k,), (sA_k,), eng="dve")
        cx.tt(pre[:], pre[:], sA[:], ALU.subtract, (pre_k, sA_k), (pre_k,))
        a, ak, b, bk = sA, sA_k, sB, sB_k
        sft = 4
        while sft < NKB * 4:
            cx.copy(b[:, 0:sft], a[:, 0:sft], (ak,), (bk,), eng="dve")
            cx.tt(b[:, sft:], a[:, sft:], a[:, 0:NKB * 4 - sft], ALU.add, (ak,), (bk,))
            a, ak, b, bk = b, bk, a, ak
            sft *= 2
        cx.tt(cT[:].rearrange("p k h -> p (k h)"), pre[:], a[:], ALU.add, (pre_k, ak), (cT_k,))
        pc, pc_k = psS.next()
        cx.mm(pc[:], cm[:, 2, :], cT[:].rearrange("p k h -> p (k h)"), True, True, (cm_k, cT_k), (pc_k,))
        cx.copy(crefb[:].rearrange("p k h -> p (k h)"), pc[:], (pc_k,), (crefb_k,), eng="dve")
        bcol = Slots(cx, "bcol", 3, [128, 2, NKB], F32)

    for hl in range(nheads):
        vt, vk = Vt.next()
        vk = vkeys[id(vk)]
        for r in range(4):
            for cl in range(4):
                src = v_d[r, cl].rearrange("(kb p) v -> p kb v", p=128)
                cx.dma("sp", vt[:, r * 32 + cl * 8:r * 32 + cl * 8 + 8, 0:VD],
                       src[:, :, hl * VD:(hl + 1) * VD], vk[r * 4 + cl], (), (vk[r * 4 + cl],))
        kts = []
        for c in range(nmaps):
            u = hl * nmaps + c
            kt, kk = KT.next()
            kk = kkeys[id(kk)]
            for r in range(4):
                for rt in range(4):
                    cx.dma("pool", kt[:, r * 4096 + rt * 1024:r * 4096 + (rt + 1) * 1024], kT_d[rt, r, u],
                           kk[r * 4 + rt], (), (kk[r * 4 + rt],))
            kts.append((kt, kk))
        tasks = []
        for qt in range(S // 512):
            for c in range(nmaps):
                kb, nfar = 0, (max(0, 4 * qt - 1) if (diff and PAIR) else 0)
                while kb < 4 * qt + 4:
                    if kb + 1 < nfar:
                        tasks.append((qt, c, (kb, kb + 1)))
                        kb += 2
                    else:
                        tasks.append((qt, c, (kb,)))
                        kb += 1
        qtiles, ptiles, groups, ostage, bcols = {}, {}, {}, {}, {}

        def stage_s(i):
            nonlocal nssq
            qt, c, kbs = tasks[i]
            if qt not in qtiles:
                r, tq = qt // 8, (qt % 8) * 512
                q, qk = QT.next()
                for cc in range(nmaps):
                    cx.dma("sp", q[:, cc, :], qT_d[tq // 1024, r, hl * nmaps + cc, :, tq % 1024:tq % 1024 + 512],
                           qk, (), (qk,))
                qtiles[qt] = (q, qk)
                if not diff:
                    bc, bc_k = bcol.next()
                    nkb = 4 * qt + 4
                    for qh in range(FOXH):
                        refblk = 4 * qt + (2 if FOXH == 1 else 1 + 2 * qh)
                        cx.ts(bc[:, qh, 0:nkb], cT[:, 0:nkb, hl], -1.0, crefb[:, refblk, hl:hl + 1],
                              ALU.mult, ALU.add, (cT_k, crefb_k), (bc_k,))
                    bcols[qt] = (bc, bc_k)
            q, qk = qtiles[qt]
            kt, kk = kts[c]
            ps, ps_k = psS.next()
            p, pk = PT.next()
            for j, kb in enumerate(kbs):
                t = kb - 4 * qt
                ql = max(0, 128 * t)
                cx.mm(ps[:, j * 512 + ql:(j + 1) * 512], kt[:, kb * 128:(kb + 1) * 128], q[:, c, ql:512],
                      True, True, (kk[kb // 8], qk), (ps_k,))
            if diff:
                if t <= -2:
                    w_ = 512 * len(kbs)
                    cx.act(p[:, 0:w_], ps[:, 0:w_], AF.Exp, (ps_k, bw_k), (pk,), bias=bw[:, hl, 639:640], scale=SCALE)
                else:
                    tm, tmk = tmpS.next()
                    cx.stt(tm[:, ql:512], ps[:, ql:512], SCALE, bw[:, hl, ql - 128 * t:512 - 128 * t],
                           ALU.mult, ALU.add, (ps_k, bw_k), (tmk,))
                    cx.act(p[:, ql:512], tm[:, ql:512], AF.Exp, (tmk,), (pk,))
            else:
                bc, bc_k = bcols[qt]
                for qh in range(FOXH):
                    w_ = 512 // FOXH
                    lo, hi = max(ql, w_ * qh), w_ * (qh + 1)
                    if lo < hi:
                        cx.act(p[:, lo:hi], ps[:, lo:hi], AF.Exp, (ps_k, bc_k), (pk,),
                               bias=bc[:, qh, kb:kb + 1], scale=SCALE)
                if t >= 0:
                    cx.tt(p[:, ql:ql + 128], p[:, ql:ql + 128], tri[:], ALU.mult, (pk, tri_k), (pk,))
            ptiles[i] = (p, pk)

        def stage_av(i):
            nonlocal nssq
            qt, c, kbs = tasks[i]
            p, pk = ptiles.pop(i)
            if (qt, c) not in groups:
                groups[(qt, c)] = [psO.next() for _ in range(4)]
            Os = groups[(qt, c)]
            if qt not in ostage:
                ostage[qt] = ostg.next()
            os_, os_k = ostage[qt]
            for j, kb in enumerate(kbs):
                t = kb - 4 * qt
                for qs in range(max(t, 0), 4):
                    O, Ok = Os[qs]
                    cx.mm(O[:, 0:VD + 1], p[:, j * 512 + qs * 128:j * 512 + (qs + 1) * 128], vt[:, kb, :],
                          kb == 0, kb == 4 * qt + qs, (pk, vk[kb // 8]), (Ok,))
            if kb != 4 * qt + 3:
                return
            for qs in range(4):
                O, Ok = Os[qs]
                rc, rck = rec.next()
                cx.s.op("dve", lambda e, rc=rc, O=O: e.reciprocal(rc[:, 0:1], O[:, VD:VD + 1]), (Ok,), (rck,))
                if not diff:
                    cx.ts(os_[:, qs, :], O[:, 0:VD], rc[:, 0:1], None, ALU.mult, None, (Ok, rck), (os_k,))
                elif c == 0:
                    cx.ts(A0[:, qs, :], O[:, 0:VD], rc[:, 0:1], None, ALU.mult, None, (Ok, rck), (A0_k,))
                else:
                    cx.tt(rc[:, 1:2], rc[:, 0:1], nlam, ALU.mult, (rck, sm_k), (rck,))
                    cb_, cbk = comb.next()
                    cx.stt(cb_[:], O[:, 0:VD], rc[:, 1:2], A0[:, qs, :], ALU.mult, ALU.add,
                           (Ok, rck, A0_k), (cbk,))
                    j = nssq
                    nssq += 1
                    cx.act(junk[:], cb_[:], AF.Square, (cbk,), (junk_k, ssq_k), accum=ssq[:, j:j + 1])
                    cx.act(ssq[:, j:j + 1], ssq[:, j:j + 1], AF.Ln, (ssq_k, sm_k), (ssq_k,),
                           bias=epsb, scale=1.0 / 256)
                    cx.act(ssq[:, j:j + 1], ssq[:, j:j + 1], AF.Exp, (ssq_k,), (ssq_k,), scale=-0.5)
                    cx.stt(os_[:, qs, :], cb_[:], ssq[:, j:j + 1], gs[:], ALU.mult, ALU.mult,
                           (cbk, ssq_k, gs_k), (os_k,))
            if c == nmaps - 1:
                ci = qt // 2
                rds = (os_k,) if xo is None else (os_k, xo.tk(ci // 4, ci % 4))
                cx.dma("sp", o_d[qt * 512:(qt + 1) * 512, hl * VD:(hl + 1) * VD].rearrange("(qs p) v -> p qs v", p=128),
                       os_[:], os_k, rds, ())
                if xo is not None and hl == nheads - 1 and qt % 2 == 1:
                    xo.issue(cx, cck, (ci % 4,), gs=(ci // 4,))
                del qtiles[qt]

        LOOK = 1 if (diff and PAIR) else 3
        for i in range(min(LOOK, len(tasks))):
            stage_s(i)
        for i in range(len(tasks)):
            if i + LOOK < len(tasks):
                stage_s(i + LOOK)
            stage_av(i)
    cx.s.emit()


TT = 1024


def row_phase(nc, es, layer, x_d, o_d, ident_d, wo_d, gm_d, win_d, wout_d, ex, gsem=None, tag=""):
    cx = Ctx(nc, es, gsem, tag)
    rk = RowKit(cx, ident_d, nss=128, wslots=2, hnslots=2)
    h, _ = cx.sb("h", [128, 8, 2048], F32)
    hk = [Tk(f"h{i}") for i in range(8)]
    aT, aT_k = cx.sb("aT", [128, 16, TT], BF16)
    hid = Slots(cx, "hid", 2, [128, 4, TT], BF16)
    g, g_k = cx.sb("g", [128, 2048], F32)
    obs = Slots(cx, "ob", 2, [128, 2048], BF16)
    rr = Slots(cx, "rr", 2, [128, 512], BF16)
    ost = Slots(cx, "ost", 4, [128, 512], BF16)
    if layer == 0:
        wf, wf_k = cx.sb("wf", [128, 16, 16], BF16)
        cx.dma("pool", wf[:], ex["wf"].rearrange("(kc p) n -> p kc n", p=128), wf_k, (), (wf_k,))
        bfb, bfb_k = cx.sb("bfb", [128, 16], F32)
        cx.dma("sp", bfb[:], ex["bf"], bfb_k, (), (bfb_k,))
        zs = Slots(cx, "zs", 2, [128, 16], F32)
        one, one_k = cx.sb("one", [128, 1], F32)
        cx.memset(one[:], 1.0, (one_k,))
    else:
        fin = Slots(cx, "fin", 1, [128, 2048], F32)

    cck = Tk("cc")
    xis_ = ex.get("xis") or (None, None, None)
    for rt in range(TOK // TT):
        r0 = rt * TT
        for tb in range(8):
            cx.dma("sp", h[:, tb, :], x_d[r0 + tb * 128:r0 + (tb + 1) * 128, :], hk[tb], (), (hk[tb],))
        for tb in range(8):
            ob, ob_k = obs.next()
            rw = r0 + tb * 128
            cx.dma("sp", ob[:, :].rearrange("p (g v) -> p g v", g=4),
                   o_d[:, rw // 1024, rw % 1024:rw % 1024 + 128, :].rearrange("g p v -> p g v"), ob_k, (), (ob_k,))
            rk.transpose_in(ob, ob_k, aT, aT_k, tb * 128)
        for wb in range(4):
            wt, wk = rk.load_w(wo_d, wb * 512, 512)
            if layer == 0 and wb == 1 and rt > 0 and ex.get("xis") is not None:
                for xi in ex["xis"]:
                    xi.issue(cx, cck, (rt - 1,), defer=rk)

            def cb(tb, pa, pa_k, wb=wb):
                hs = h[:, tb, wb * 512:(wb + 1) * 512]
                cx.tt(hs, hs, pa[:], ALU.add, (pa_k, hk[tb]), (hk[tb],))
            rk.lin_tm(aT, aT_k, TT, wt, wk, 512, cb)
        cx.dma("sp", g[:], gm_d, g_k, (), (g_k,))
        for tb in range(8):
            rk.norm_T(h[:, tb, :], hk[tb], g, g_k, aT, aT_k, tb * 128)
        for hb in range(DFF // 512):
            wt, wk = rk.load_w(win_d, hb * 512, 512)
            hd, hd_k = hid.next()

            def cb(oc, tt, pa, pa_k, hd=hd, hd_k=hd_k):
                r_, rk_ = rr.next()
                cx.act(r_[:], pa[:], AF.Relu, (pa_k,), (rk_,))
                cx.tt(hd[:, oc, tt * 512:(tt + 1) * 512], r_[:], r_[:], ALU.mult, (rk_,), (hd_k,))
            rk.lin_fm(aT, aT_k, TT, wt, wk, 512, cb)
            wt, wk = rk.load_w(wout_d, 0, 2048, nk=4, k0=hb * 4)
            wv = wt
            for tb in range(8):
                for cg in range(4):
                    pa, pa_k = rk.pacc.next()
                    for kc in range(4):
                        cx.mm(pa[:], hd[:, kc, tb * 128:(tb + 1) * 128], wv[:, kc, cg * 512:(cg + 1) * 512],
                              kc == 0, kc == 3, (hd_k, wk.for_kc(kc)), (pa_k,))
                    hs = h[:, tb, cg * 512:(cg + 1) * 512]
                    cx.tt(hs, hs, pa[:], ALU.add, (pa_k, hk[tb]), (hk[tb],))
        if layer == 0:
            for tb in range(8):
                cx.dma("sp", ex["h_out"][r0 + tb * 128:r0 + (tb + 1) * 128, :], h[:, tb, :], hk[tb], (hk[tb],), ())
            cx.dma("sp", g[:], ex["kvg"], g_k, (), (g_k,))
            for tb in range(8):
                rk.norm_T(h[:, tb, :], hk[tb], g, g_k, aT, aT_k, tb * 128)
            for tb in range(8):
                pa, pa_k = rk.pacc.next()
                for kc in range(16):
                    cx.mm(pa[:, 0:16], aT[:, kc, tb * 128:(tb + 1) * 128], wf[:, kc, :], kc == 0, kc == 15,
                          (aT_k, wf_k), (pa_k,))
                z, zk = zs.next()
                cx.tt(z[:], pa[:, 0:16], bfb[:], ALU.add, (pa_k, bfb_k), (zk,))
                cx.act(z[:], z[:], AF.Exp, (zk,), (zk,), scale=-1.0)
                cx.act(z[:], z[:], AF.Ln, (zk, one_k), (zk,), bias=one[:, 0:1])
                cx.ts(z[:], z[:], -1.0, None, ALU.mult, None, (zk,), (zk,))
                for gg in range(4):
                    cx.dma("sp", ex["logf"][gg, r0 + tb * 128:r0 + (tb + 1) * 128, :], z[:, gg * 4:(gg + 1) * 4],
                           zk, (zk,), ())
            def early_kv(si, wb, rt=rt):
                if rt == 3 and xis_[0] is not None and si == 1 and wb == 1:
                    xis_[1].issue(cx, cck, (3,), defer=rk)

            def early_q(si, wb, rt=rt):
                if rt == 3 and xis_[0] is not None and wb == 1:
                    xis_[2].issue(cx, cck, (3,), defer=rk)
            proj_qkv(cx, rk, aT, aT_k, TT, r0, ost,
                     [("fm", ex["wk"], 0, ex["k2T"], xis_[1]), ("tm", ex["wv"], 0, ex["v2"], xis_[2])], early_kv)
            cx.dma("sp", g[:], ex["qg"], g_k, (), (g_k,))
            for tb in range(8):
                rk.norm_T(h[:, tb, :], hk[tb], g, g_k, aT, aT_k, tb * 128)
            proj_qkv(cx, rk, aT, aT_k, TT, r0, ost, [("fm", ex["wq"], 0, ex["q2T"], xis_[0])], early_q)
        else:
            cx.dma("sp", g[:], ex["fg"], g_k, (), (g_k,))
            for tb in range(8):
                f_, fk = fin.next()
                rs = rk.norm_stats(h[:, tb, :], hk[tb])
                cx.stt(f_[:], h[:, tb, :], rs, g[:], ALU.mult, ALU.mult, (hk[tb], rk.rs_k, g_k), (fk,))
                cx.dma("sp", ex["out"][r0 + tb * 128:r0 + (tb + 1) * 128, :], f_[:], fk, (fk,), ())
    rk.drain()
    cx.s.emit()


def _t5_bucket_np(rel):
    half, max_exact = 16, 8
    ret = np.where(rel > 0, half, 0)
    n = np.abs(rel)
    nf = np.maximum(n, 1).astype(np.float32)
    large = max_exact + (np.log(nf / np.float32(max_exact)) / np.float32(math.log(128 / max_exact))
                         * np.float32(half - max_exact)).astype(np.int32)
    large = np.minimum(large, half - 1)
    return ret + np.where(n < max_exact, n, large)


def _bc(v, n=128):
    v = np.asarray(v, np.float32).reshape(1, -1)
    return np.ascontiguousarray(np.broadcast_to(v, (n, v.shape[1])))


def _a2a(outs, key, b):
    return [np.ascontiguousarray(np.stack([outs[b * 4 + r][key][g] for r in range(4)])) for g in range(4)]


def _launch(build, in_maps):
    nc = bass.Bass("TRN2", target_bir_lowering=False)
    with ExitStack() as es:
        build(nc, es)
    res = run_bass_kernel_spmd(nc, in_maps, core_ids=list(range(NCORE)))
    return res.results


def _din(nc, name, shape, dt=F32):
    return nc.dram_tensor(name, list(shape), dt, kind="ExternalInput").ap()


def _dout(nc, name, shape, dt=F32):
    return nc.dram_tensor(name, list(shape), dt, kind="ExternalOutput").ap()


IDENT = np.eye(128, dtype=np.float32).astype(ml_dtypes.bfloat16)


def build_a(nc, es):
    phase_a(nc, es, _din(nc, "x", [TOK, D]), _din(nc, "g", [128, D]), _din(nc, "wqkv", [D, 3 * D]),
            _din(nc, "ident", [128, 128], BF16), _dout(nc, "qT", [16, 128, TOK], BF16),
            _dout(nc, "kT", [16, 128, TOK], BF16), _dout(nc, "v", [4, TOK, 512], BF16))


def build_attn(mode):
    def build(nc, es):
        aux = {}
        if mode == "diff":
            aux["bias"] = _din(nc, "bias", [128, 2, 640])
            aux["subg"] = _din(nc, "subg", [128, 256])
            aux["lam"] = _din(nc, "lam", [128, 4, 128])
        else:
            aux["trimask"] = _din(nc, "trimask", [128, 128], BF16)
            aux["cmats"] = _din(nc, "cmats", [128, 3, 128])
            aux["logf"] = _din(nc, "logf", [4, TOK, 4])
        attention(nc, es, mode, _din(nc, "qT", [4, 4, 128, TOK], BF16), _din(nc, "kT", [4, 4, 128, TOK], BF16),
                  _din(nc, "v", [4, TOK, 512], BF16), _dout(nc, "o", [S, 512], BF16), aux)
    return build


def build_row(layer):
    def build(nc, es):
        ex = {}
        if layer == 0:
            for nm in ("kvg", "qg"):
                ex[nm] = _din(nc, nm, [128, D])
            for nm in ("wk", "wv", "wq"):
                ex[nm] = _din(nc, nm, [D, D])
            ex["wf"] = _din(nc, "wf", [D, 16])
            ex["bf"] = _din(nc, "bf", [128, 16])
            ex["h_out"] = _dout(nc, "h_out", [TOK, D])
            ex["q2T"] = _dout(nc, "q2T", [16, 128, TOK], BF16)
            ex["k2T"] = _dout(nc, "k2T", [16, 128, TOK], BF16)
            ex["v2"] = _dout(nc, "v2", [4, TOK, 512], BF16)
            ex["logf"] = _dout(nc, "logf", [4, TOK, 4])
        else:
            ex["fg"] = _din(nc, "fg", [128, D])
            ex["out"] = _dout(nc, "out", [TOK, D])
        row_phase(nc, es, layer, _din(nc, "x", [TOK, D]), _din(nc, "o", [4, TOK, 512], BF16),
                  _din(nc, "ident", [128, 128], BF16), _din(nc, "wo", [D, D]), _din(nc, "gm", [128, D]),
                  _din(nc, "win", [D, DFF]), _din(nc, "wout", [DFF, D]), ex)
    return build


I32 = mybir.dt.int32


class SemPool:
    def __init__(self, stack):
        self.stack = stack
        self.free = []
        self.regs = []


GROUPS = [[0, 1, 2, 3], [4, 5, 6, 7]]


PAIR = False
FOXH = 2
CHK = 524288


def comm_phase(nc, es, gsem, tag, cid_d, items):
    cx = Ctx(nc, es, gsem, tag)
    cid, cid_k = cx.sb("cid", [1, 8], I32)
    cx.dma("sp", cid[:], cid_d, cid_k, (), (cid_k,))
    if not gsem.regs:
        gsem.regs = [gsem.stack.enter_context(nc.gpsimd.register(f"cr{i}")) for i in range(6)]
        for i in range(6):
            o = cx.s.op("pool", lambda e, i=i: e.reg_load(gsem.regs[i], cid[:1, i:i + 1]), (cid_k,), ())
            o.nosig = True
    regs = gsem.regs
    cck = Tk("cc")
    inner = [[8192, CHK // 8192], [1, 8192]]
    for n, it in enumerate(items):
        gk, dk = Tk(f"g{n}"), Tk(f"l{n}")
        if isinstance(it, XItem):
            it.tks = {}
            it.issue(cx, cck, (0, 1, 2, 3))
            gall, dst = it.gall, it.dst
            cx.s.op("pool", lambda e, dst=dst, gall=gall: e.dma_start(
                out=bass.AP(dst, 0, [[32768, 16 * CHK // 32768], [1, 32768]]),
                in_=bass.AP(gall, regs[0], [[32768, 16 * CHK // 32768], [1, 32768]])),
                tuple(it.tks.values()), (dk,), dma=dk)
            it.reset()
        else:
            _, src, gath, dst, ri, rstride, nel = it
            cx.s.op("pool", lambda e, src=src, gath=gath: e.collective_compute(
                "AllGather", ALU.bypass, replica_groups=GROUPS, ins=[src.ap().opt()], outs=[gath.ap().opt()]),
                (), (gk,), dma=cck, inc=1)
            cx.s.op("pool", lambda e, dst=dst, gath=gath, ri=ri, rstride=rstride, nel=nel: e.dma_start(
                out=bass.AP(dst, 0, [[nel, 4], [1, nel]]), in_=bass.AP(gath, regs[ri], [[rstride, 4], [1, nel]])),
                (gk,), (dk,), dma=dk)
    cx.s.emit()


def build_fused(nc, upto=9):
    di = lambda name, shape, dt=F32: nc.dram_tensor(name, list(shape), dt, kind="ExternalInput").ap()
    x_d = di("x", [TOK, D])
    ident_d = di("ident", [128, 128], BF16)
    cid_d = di("cid", [1, 8], I32)
    g_attn0, g_attn1, g_mlp0, g_mlp1, g_kv, g_fin = [di(n, [128, D]) for n in
                                                     ("g_attn0", "g_attn1", "g_mlp0", "g_mlp1", "g_kv", "g_fin")]
    wqkv = di("wqkv", [D, 3 * D])
    wo0, wo1, wk, wv, wq = [di(n, [D, D]) for n in ("wo0", "wo1", "wk", "wv", "wq")]
    win0, win1 = di("win0", [D, DFF]), di("win1", [D, DFF])
    wout0, wout1 = di("wout0", [DFF, D]), di("wout1", [DFF, D])
    wf, bf = di("wf", [D, 16]), di("bf", [128, 16])
    aux0 = {"bias": di("bias", [128, 2, 640]), "subg": di("subg", [128, 256]), "lam": di("lam", [128, 4, 128])}
    trimask, cmats = di("trimask", [128, 128], BF16), di("cmats", [128, 3, 128])
    out_d = nc.dram_tensor("out", [TOK, D], F32, kind="ExternalOutput").ap()
    dt_ = lambda name, shape, dt=BF16: nc.dram_tensor(name, list(shape), dt)
    qT, kT, v = dt_("i_qT", [8192, 1024]), dt_("i_kT", [8192, 1024]), dt_("i_v", [4 * TOK, 512])
    Gq, Gk, Gv = dt_("i_Gq", [8192, TOK]), dt_("i_Gk", [8192, TOK]), dt_("i_Gv", [16 * TOK, 512])
    Lq, Lk, Lv = dt_("i_Lq", [8192, 1024]), dt_("i_Lk", [8192, 1024]), dt_("i_Lv", [4 * TOK, 512])
    o, Lo = dt_("i_o", [S, 512]), dt_("i_Lo", [4 * TOK, 512])
    h1 = dt_("i_h1", [TOK, D], F32)
    logf, Glogf, Llogf = dt_("i_logf", [4 * TOK, 4], F32), dt_("i_Glogf", [16 * TOK, 4], F32), dt_("i_Llogf", [4 * TOK, 4], F32)
    u3 = lambda t: t.ap().rearrange("(g rt ul d) t -> g rt ul d t", g=4, rt=4, ul=4)
    r4 = lambda t: t.ap().rearrange("(rt r ul d) t -> rt r ul d t", rt=4, r=4, ul=4)
    xq, xk_, xv = XItem(qT, Gq, Lq), XItem(kT, Gk, Lk), XItem(v, Gv, Lv)
    xo = XItem(o, Gv, Lo)
    c4 = lambda t: t.ap().rearrange("(c r i) v -> r c i v", c=4, r=4)
    g3 = lambda t: t.ap().rearrange("(g t) v -> g t v", g=4)
    LG = ("small", logf, Glogf, Llogf, 1, 4 * TOK * 4, TOK * 4)
    with ExitStack() as gstack:
        gsem = SemPool(gstack)
        if upto > 0:
            with ExitStack() as es:
                phase_a(nc, es, x_d, g_attn0, wqkv, ident_d, u3(qT), u3(kT), g3(v), gsem, "A_", (xq, xk_, xv))
        if upto > 1:
            with ExitStack() as es:
                comm_phase(nc, es, gsem, "X1_", cid_d, [xq, xk_, xv])
        if upto > 2:
            with ExitStack() as es:
                attention(nc, es, "diff", r4(Lq), r4(Lk), c4(Lv), o.ap(), aux0, gsem, "B1_", xo)
        if upto > 3:
            with ExitStack() as es:
                comm_phase(nc, es, gsem, "X2_", cid_d, [xo])
        if upto > 4:
            with ExitStack() as es:
                ex = {"kvg": g_kv, "qg": g_attn1, "wk": wk, "wv": wv, "wq": wq, "wf": wf, "bf": bf,
                      "h_out": h1.ap(), "q2T": u3(qT), "k2T": u3(kT), "v2": g3(v), "logf": g3(logf), "xis": (xq, xk_, xv)}
                row_phase(nc, es, 0, x_d, c4(Lo), ident_d, wo0, g_mlp0, win0, wout0, ex, gsem, "B2_")
        if upto > 5:
            with ExitStack() as es:
                comm_phase(nc, es, gsem, "X3_", cid_d, [xq, xk_, xv, LG])
        if upto > 6:
            with ExitStack() as es:
                aux1 = {"trimask": trimask, "cmats": cmats, "logf": g3(Llogf)}
                attention(nc, es, "fox", r4(Lq), r4(Lk), c4(Lv), o.ap(), aux1, gsem, "C1_", xo)
        if upto > 7:
            with ExitStack() as es:
                comm_phase(nc, es, gsem, "X4_", cid_d, [xo])
        if upto > 8:
            with ExitStack() as es:
                row_phase(nc, es, 1, h1.ap(), c4(Lo), ident_d, wo1, g_mlp1, win1, wout1,
                          {"fg": g_fin, "out": out_d}, gsem, "C2_")
        if upto < 9:
            with ExitStack() as es:
                cx = Ctx(nc, es, gsem, "Z_")
                k = Tk("z")
                cx.dma("sp", out_d, x_d, k, (), (k,))
                cx.s.emit()


def kernel(x, rel_bias_table, attn_norm_g, mlp_norm_g, w_qkv_a, lam_q1, lam_k1, lam_q2, lam_k2,
           subln_g, w_o_a, kv_norm_g, w_k_b, w_v_b, w_f_b, b_f_b, w_q_b, w_o_b, w_mlp_in, w_mlp_out,
           final_norm_g, _upto=9):
    f32 = lambda a: np.ascontiguousarray(np.asarray(a, np.float32))
    x = f32(x)
    kk = np.arange(128)[:, None]
    jj = np.arange(640)[None, :]
    bias_all = f32(rel_bias_table)[_t5_bucket_np(kk - jj)]
    tri = (np.arange(128)[None, :] >= np.arange(128)[:, None]).astype(np.float32)
    cm = np.zeros((128, 3, 128), np.float32)
    cm[:, 0, :] = tri
    cm[127, 1, :] = 1.0
    cm[0, 2, :] = 1.0
    shared = {
        "ident": IDENT, "g_attn0": _bc(attn_norm_g[0]), "g_attn1": _bc(attn_norm_g[1]),
        "g_mlp0": _bc(mlp_norm_g[0]), "g_mlp1": _bc(mlp_norm_g[1]), "g_kv": _bc(kv_norm_g),
        "g_fin": _bc(final_norm_g), "wqkv": f32(w_qkv_a[0]), "wo0": f32(w_o_a[0]), "wo1": f32(w_o_b[0]),
        "wk": f32(w_k_b), "wv": f32(w_v_b), "wq": f32(w_q_b[0]), "win0": f32(w_mlp_in[0]),
        "win1": f32(w_mlp_in[1]), "wout0": f32(w_mlp_out[0]), "wout1": f32(w_mlp_out[1]),
        "wf": f32(w_f_b), "bf": _bc(b_f_b), "subg": _bc(subln_g[0]),
        "lam": np.ascontiguousarray(np.broadcast_to(
            np.stack([f32(lam_q1[0]), f32(lam_k1[0]), f32(lam_q2[0]), f32(lam_k2[0])])[None], (128, 4, 128))),
        "trimask": tri.astype(ml_dtypes.bfloat16), "cmats": cm,
    }
    in_maps = []
    for c in range(NCORE):
        b, g = c // 4, c % 4
        m = dict(shared)
        m["x"] = np.ascontiguousarray(x[b, g * TOK:(g + 1) * TOK])
        m["bias"] = np.ascontiguousarray(bias_all[:, :, 2 * g:2 * g + 2].transpose(0, 2, 1))
        cid = np.zeros((1, 8), np.int32)
        cid[0, 0] = g * 16 * CHK
        for r in range(4):
            cid[0, 2 + r] = g * 16 * CHK + r * CHK
        cid[0, 1] = g * TOK * 4
        m["cid"] = cid
        in_maps.append(m)
    nc = bass.Bass("TRN2", target_bir_lowering=False)
    build_fused(nc, _upto)
    res = run_bass_kernel_spmd(nc, in_maps, core_ids=list(range(NCORE))).results
    out = np.empty((NB, S, D), np.float32)
    for c in range(NCORE):
        out[c // 4, (c % 4) * TOK:(c % 4 + 1) * TOK] = res[c]["out"]
    return out
```

```python
import math
from contextlib import ExitStack

import numpy as np
import ml_dtypes
import concourse.bass as bass
import concourse.mybir as mybir
from concourse.bass_utils import run_bass_kernel_spmd

F32 = mybir.dt.float32
BF16 = mybir.dt.bfloat16
AF = mybir.ActivationFunctionType
ALU = mybir.AluOpType
AX = mybir.AxisListType

D = 2048
S = 16384
NB = 2
DFF = 8192
NCORE = 8
TOK = 4096
CH = 2048
NEG = -30000.0
SCALE = 128 ** -0.5
EPS = 1e-6
SUBEPS = 1e-5
LAMBDA_INIT = 0.8 - 0.6 * math.exp(-0.3 * 0)


class Tk:
    __slots__ = ("name", "w", "r", "semcnt")

    def __init__(self, name):
        self.name = name
        self.w = None
        self.r = []
        self.semcnt = 0


class Op:
    __slots__ = ("eng", "fn", "deps", "needed", "is_dma", "tk", "val", "key", "inc", "nosig")


ENGS = ("pe", "act", "dve", "pool", "sp")
BLK = {"pe": "tensor", "act": "scalar", "dve": "vector", "pool": "gpsimd", "sp": "sync"}


class Sched:
    def __init__(self, nc, gsem=None, tag=""):
        self.nc = nc
        self.gsem = gsem
        self.tag = tag
        self.ops = {e: [] for e in ENGS}
        self.all = []

    def op(self, eng, fn, reads=(), writes=(), dma=None, inc=16):
        o = Op()
        o.inc = inc
        o.nosig = False
        o.eng = eng
        o.fn = fn
        o.needed = False
        o.is_dma = dma is not None
        o.tk = dma
        o.val = 0
        o.key = None
        deps = []
        for t in reads:
            if t.w is not None:
                deps.append(t.w)
        for t in writes:
            if t.w is not None:
                deps.append(t.w)
            deps.extend(t.r)
        o.deps = deps
        for d in deps:
            d.needed = True
        for t in reads:
            t.r.append(o)
        for t in writes:
            t.w = o
            t.r = []
        self.ops[eng].append(o)
        self.all.append(o)
        return o

    def emit(self):
        nc = self.nc
        engcnt = {e: 0 for e in ENGS}
        keys = {}
        for o in self.all:
            if o.is_dma:
                o.tk.semcnt += o.inc
                o.val = o.tk.semcnt
                o.key = ("t", id(o.tk))
                keys[o.key] = "d_" + o.tk.name
            elif o.needed:
                engcnt[o.eng] += 1
                o.val = engcnt[o.eng]
                o.key = ("e", o.eng)
                keys[o.key] = "e_" + o.eng
        if self.gsem is not None:
            for e in ENGS:
                for o in reversed(self.ops[e]):
                    if not o.is_dma and not o.nosig:
                        if not o.needed:
                            o.needed = True
                            engcnt[e] += 1
                            o.val = engcnt[e]
                            o.key = ("e", e)
                            keys[o.key] = "e_" + e
                        break
        final = {}
        for o in self.all:
            if o.key is not None:
                final[o.key] = max(final.get(o.key, 0), o.val)
        with ExitStack() as es:
            sems, base = {}, {}
            for k, n in keys.items():
                if self.gsem is None:
                    sems[k], base[k] = es.enter_context(nc.semaphore(self.tag + n)), 0
                elif self.gsem.free:
                    sems[k], base[k] = self.gsem.free.pop()
                else:
                    sems[k], base[k] = self.gsem.stack.enter_context(nc.semaphore(self.tag + n)), 0
            block = es.enter_context(nc.Block())
            for en in ENGS:
                def body(eng, en=en):
                    waited = {}
                    for o in self.ops[en]:
                        for d in o.deps:
                            if en == "pe" and d.eng == "pe" and not d.is_dma:
                                continue
                            if waited.get(d.key, 0) >= d.val:
                                continue
                            eng.wait_ge(sems[d.key], base[d.key] + d.val)
                            waited[d.key] = d.val
                        ins = o.fn(eng)
                        if o.is_dma:
                            ins.then_inc(sems[o.key], o.inc)
                        elif o.needed:
                            ins.then_inc(sems[o.key], 1)
                    if en == "sp" or self.gsem is not None:
                        for k, v in final.items():
                            if waited.get(k, 0) < v:
                                eng.wait_ge(sems[k], base[k] + v)
                getattr(block, BLK[en])(body)
            if self.gsem is not None:
                for k in keys:
                    self.gsem.free.append((sems[k], base[k] + final[k]))


class Ctx:
    def __init__(self, nc, es, gsem=None, tag=""):
        self.nc = nc
        self.es = es
        self.tag = tag
        self.s = Sched(nc, gsem, tag)
        self.rr = 0

    def sb(self, name, shape, dt):
        t = self.es.enter_context(self.nc.sbuf_tensor("sb_" + self.tag + name, list(shape), dt))
        return t, Tk(name)

    def ps(self, name, shape, dt=F32):
        t = self.es.enter_context(self.nc.psum_tensor("ps_" + self.tag + name, list(shape), dt))
        return t, Tk(name)

    def dma(self, q, out, in_, tk, reads=(), writes=()):
        return self.s.op(q, lambda e: e.dma_start(out=out, in_=in_), reads, writes, dma=tk)

    def mm(self, out, lhsT, rhs, start, stop, reads, writes):
        return self.s.op("pe", lambda e: e.matmul(out, lhsT, rhs, start=start, stop=stop),
                         reads, writes)

    def tr(self, out, in_, ident, reads, writes):
        return self.s.op("pe", lambda e: e.transpose(out, in_, ident), reads, writes)

    def act(self, out, in_, func, reads, writes, bias=0.0, scale=1.0, accum=None):
        if accum is None:
            return self.s.op("act", lambda e: e.activation(out, in_, func, bias=bias, scale=scale),
                             reads, writes)
        return self.s.op("act", lambda e: e.activation(out, in_, func, bias=bias, scale=scale,
                                                       accum_out=accum), reads, writes)

    def copy(self, out, in_, reads, writes, eng=None):
        if eng is None:
            self.rr ^= 1
            eng = "act" if self.rr else "dve"
        if eng == "act":
            return self.s.op("act", lambda e: e.copy(out, in_), reads, writes)
        return self.s.op(eng, lambda e: e.tensor_copy(out, in_), reads, writes)

    def ts(self, out, in0, s1, s2, op0, op1, reads, writes, eng="dve"):
        if op1 is None:
            return self.s.op(eng, lambda e: e.tensor_scalar(out, in0, s1, s2, op0), reads, writes)
        return self.s.op(eng, lambda e: e.tensor_scalar(out, in0, s1, s2, op0, op1), reads, writes)

    def stt(self, out, in0, scalar, in1, op0, op1, reads, writes, eng="dve"):
        return self.s.op(eng, lambda e: e.scalar_tensor_tensor(out, in0, scalar, in1, op0, op1),
                         reads, writes)

    def tt(self, out, in0, in1, op, reads, writes, eng="dve"):
        return self.s.op(eng, lambda e: e.tensor_tensor(out, in0, in1, op), reads, writes)

    def memset(self, ap, val, writes, eng="dve"):
        return self.s.op(eng, lambda e: e.memset(ap, val), (), writes)


class Slots:
    def __init__(self, cx, name, n, shape, dt, psum=False):
        self.items = []
        for i in range(n):
            self.items.append(cx.ps(f"{name}{i}", shape, dt) if psum else cx.sb(f"{name}{i}", shape, dt))
        self.i = 0

    def next(self):
        it = self.items[self.i % len(self.items)]
        self.i += 1
        return it


class WKeys:
    def __init__(self, tks, st):
        self.tks, self.st = tks, st

    def for_kc(self, kc):
        return self.tks[kc // self.st]


class RowKit:
    def __init__(self, cx, ident_d, nss=64, wslots=3, hnslots=2):
        self.cx = cx
        self.ident, self.ident_k = cx.sb("ident", [128, 128], BF16)
        cx.dma("sp", self.ident[:], ident_d, self.ident_k, (), (self.ident_k,))
        self.w = Slots(cx, "w", wslots, [128, 8192], BF16)
        self.pacc = Slots(cx, "pacc", 6, [128, 512], F32, psum=True)
        self.ptr = Slots(cx, "ptr", 2, [128, 1024], BF16, psum=True)
        self.hn = Slots(cx, "hn", hnslots, [128, 2048], BF16)
        self.ss, self.ss_k = cx.sb("ss", [128, nss], F32)
        self.rs, self.rs_k = cx.sb("rs", [128, nss], F32)
        cx.memset(self.ss[:], 0.0, (self.ss_k,))
        self.epsb, self.eps_k = cx.sb("epsb", [128, 2], F32)
        cx.memset(self.epsb[:, 0:1], EPS, (self.eps_k,))
        cx.memset(self.epsb[:, 1:2], SUBEPS, (self.eps_k,))
        self.eps_main = self.epsb[:, 0:1]
        self.eps_sub = self.epsb[:, 1:2]
        self.nss = 0
        self.wsub = {}
        self.pending = []
        self.per_load = 1

    def drain(self, n=None):
        while self.pending and (n is None or n > 0):
            self.pending.pop(0)()
            if n is not None:
                n -= 1

    def norm_stats(self, x_ap, x_k, junk=None, junk_k=None):
        cx = self.cx
        j = self.nss
        self.nss += 1
        ss = self.ss[:, j:j + 1]
        rs = self.rs[:, j:j + 1]
        if junk is None:
            junk, junk_k = self.hn.next()
        cx.act(junk[:], x_ap, AF.Square, (x_k,), (junk_k, self.ss_k), accum=ss)
        cx.act(rs, ss, AF.Ln, (self.ss_k, self.eps_k), (self.rs_k,), bias=self.eps_main, scale=1.0 / D)
        cx.act(rs, rs, AF.Exp, (self.rs_k,), (self.rs_k,), scale=-0.5)
        return rs

    def norm_T(self, x_ap, x_k, g, g_k, dstT, dstT_k, tcol):
        cx = self.cx
        hn, hn_k = self.hn.next()
        rs = self.norm_stats(x_ap, x_k, hn, hn_k)
        cx.stt(hn[:], x_ap, rs, g[:], ALU.mult, ALU.mult, (x_k, self.rs_k, g_k), (hn_k,))
        self.transpose_in(hn, hn_k, dstT, dstT_k, tcol)

    def transpose_in(self, src, src_k, dstT, dstT_k, tcol, nchunk=16, c0=0):
        cx = self.cx
        for half in range(0, nchunk, 8):
            n = min(8, nchunk - half)
            pt, pt_k = self.ptr.next()
            for i in range(n):
                cx.tr(pt[:, i * 128:(i + 1) * 128], src[:, (half + i) * 128:(half + i + 1) * 128],
                      self.ident[:], (src_k, self.ident_k), (pt_k,))
            cx.copy(dstT[:, c0 + half:c0 + half + n, tcol:tcol + 128],
                    pt[:, 0:n * 128].rearrange("p (c t) -> p c t", t=128), (pt_k,), (dstT_k,))

    def load_w(self, w_d, c0, ncols, nk=16, k0=0):
        cx = self.cx
        wt, wk = self.w.next()
        if id(wk) not in self.wsub:
            self.wsub[id(wk)] = [Tk(wk.name + f"_{i}") for i in range(4)]
        subs = self.wsub[id(wk)]
        wv_s = wt[:, 0:nk * ncols].rearrange("p (k n) -> p k n", n=ncols)
        wv = w_d.rearrange("(kc p) n -> p kc n", p=128)
        st = max(1, nk // 4)
        for i, q in enumerate(range(0, nk, st)):
            cx.dma("pool", wv_s[:, q:q + st, :], wv[:, k0 + q:k0 + q + st, c0:c0 + ncols], subs[i], (), (subs[i],))
        self.drain(self.per_load)
        return wv_s, WKeys(subs, st)

    def lin_fm(self, srcT, srcT_k, ntok, wt, wk, ncols, cb, nk=16):
        cx = self.cx
        for oc in range(ncols // 128):
            for tt in range(ntok // 512):
                pa, pa_k = self.pacc.next()
                for kc in range(nk):
                    cx.mm(pa[:], wt[:, kc, oc * 128:(oc + 1) * 128], srcT[:, kc, tt * 512:(tt + 1) * 512],
                          kc == 0, kc == nk - 1, (wk.for_kc(kc), srcT_k), (pa_k,))
                cb(oc, tt, pa, pa_k)

    def lin_tm(self, srcT, srcT_k, ntok, wt, wk, ncols, cb, nk=16):
        cx = self.cx
        for tb in range(ntok // 128):
            pa, pa_k = self.pacc.next()
            for kc in range(nk):
                cx.mm(pa[:, 0:ncols], srcT[:, kc, tb * 128:(tb + 1) * 128], wt[:, kc, 0:ncols],
                      kc == 0, kc == nk - 1, (wk.for_kc(kc), srcT_k), (pa_k,))
            cb(tb, pa, pa_k)


class XItem:
    def __init__(self, src, gall, dst):
        self.src, self.gall, self.dst = src, gall, dst
        self.issued = set()
        self.tks = {}

    def reset(self):
        self.issued = set()
        self.tks = {}

    def tk(self, g, rt):
        if (g, rt) not in self.tks:
            self.tks[(g, rt)] = Tk(f"x{g}{rt}")
        return self.tks[(g, rt)]

    def issue(self, cx, cck, rts, gs=(0, 1, 2, 3), defer=None):
        for rt in rts:
            for g in gs:
                ci = 4 * g + rt
                if ci in self.issued:
                    continue
                self.issued.add(ci)
                src, gall = self.src, self.gall

                def rec(src=src, gall=gall, ci=ci, tk=self.tk(g, rt)):
                    cx.s.op("pool", lambda e: e.collective_compute(
                        "AllGather", ALU.bypass, replica_groups=GROUPS,
                        ins=[bass.AP(src, ci * CHK, [[8192, CHK // 8192], [1, 8192]])],
                        outs=[bass.AP(gall, ci * 4 * CHK, [[8192, 4 * CHK // 8192], [1, 8192]])]),
                        (), (tk,), dma=cck, inc=1)
                if defer is None:
                    rec()
                else:
                    defer.pending.append(rec)


def proj_qkv(cx, rk, hT, hT_k, ntok, t0, ost, specs, after_load=None):
    for si, (kind, w_d, col0, dst, xi) in enumerate(specs):
        for wb in range(4):
            wt, wk = rk.load_w(w_d, col0 + wb * 512, 512)
            if after_load is not None:
                after_load(si, wb)
            if kind == "fm":
                def cb(oc, tt, pa, pa_k, dst=dst, wb=wb, xi=xi):
                    o, ok = ost.next()
                    cx.copy(o[:], pa[:], (pa_k,), (ok,))
                    tok = t0 + tt * 512
                    rds = (ok,) if xi is None else (ok, xi.tk(wb, tok // 1024))
                    cx.dma("sp", dst[wb, tok // 1024, oc, :, tok % 1024:tok % 1024 + 512], o[:], ok, rds, ())
                rk.lin_fm(hT, hT_k, ntok, wt, wk, 512, cb)
            else:
                def cb(tb, pa, pa_k, dst=dst, wb=wb, xi=xi):
                    o, ok = ost.next()
                    cx.copy(o[:], pa[:], (pa_k,), (ok,))
                    tok = t0 + tb * 128
                    rds = (ok,) if xi is None else (ok, xi.tk(wb, tok // 1024))
                    cx.dma("sp", dst[wb, tok:tok + 128, :], o[:], ok, rds, ())
                rk.lin_tm(hT, hT_k, ntok, wt, wk, 512, cb)


def phase_a(nc, es, x_d, g_d, wqkv_d, ident_d, qT_d, kT_d, v_d, gsem=None, tag="", xis=(None, None, None)):
    cx = Ctx(nc, es, gsem, tag)
    rk = RowKit(cx, ident_d)
    rk.per_load = 2
    g, g_k = cx.sb("g", [128, 2048], F32)
    cx.dma("sp", g[:], g_d, g_k, (), (g_k,))
    xs = Slots(cx, "x", 2, [128, 2048], F32)
    hT, hT_k = cx.sb("hT", [128, 16, CH], BF16)
    ost = Slots(cx, "ost", 4, [128, 512], BF16)
    cck = Tk("cc")
    for half in range(2):
        t0 = half * CH
        if half == 1 and xis[0] is not None:
            for xi in xis:
                xi.issue(cx, cck, (0, 1), defer=rk)
        for tb in range(CH // 128):
            xt, xk = xs.next()
            cx.dma("sp", xt[:], x_d[t0 + tb * 128:t0 + (tb + 1) * 128, :], xk, (), (xk,))
            rk.norm_T(xt[:], xk, g, g_k, hT, hT_k, tb * 128)
        def early(si, wb, half=half):
            if half == 1 and xis[0] is not None and wb == 1 and si in (1, 2):
                xis[si - 1].issue(cx, cck, (2, 3), defer=rk)
        proj_qkv(cx, rk, hT, hT_k, CH, t0, ost,
                 [("fm", wqkv_d, 0, qT_d, xis[0]), ("fm", wqkv_d, 2048, kT_d, xis[1]),
                  ("tm", wqkv_d, 4096, v_d, xis[2])], early)
    rk.drain()
    cx.s.emit()


def attention(nc, es, mode, qT_d, kT_d, v_d, o_d, aux, gsem=None, tag="", xo=None):
    cx = Ctx(nc, es, gsem, tag)
    cck = Tk("cc")
    diff = mode == "diff"
    VD = 256 if diff else 128
    nheads = 2 if diff else 4
    nmaps = 2 if diff else 1
    NKB = S // 128
    KT = Slots(cx, "KT", 2, [128, S], BF16)
    Vt = Slots(cx, "Vt", 1 if diff else 2, [128, NKB, VD + 1], BF16)
    vkeys = {}
    for vt, vk in Vt.items:
        vkeys[id(vk)] = [Tk(vk.name + f"_{i}") for i in range(16)]
        cx.memset(vt[:, :, VD:VD + 1], 1.0, tuple(vkeys[id(vk)]), eng="pool")
    kkeys = {id(kk_): [Tk(kk_.name + f"_{i}") for i in range(16)] for _, kk_ in KT.items}
    QT = Slots(cx, "QT", 2, [128, nmaps, 512], BF16)
    if diff and PAIR:
        PT = Slots(cx, "PT", 3, [128, 1024], BF16)
        psS = Slots(cx, "psS", 2, [128, 1024], F32, psum=True)
    else:
        PT = Slots(cx, "PT", 5, [128, 512], BF16)
        psS = Slots(cx, "psS", 4, [128, 512], F32, psum=True)
    psO = Slots(cx, "psO", 4, [128, 512], F32, psum=True)
    ostg = Slots(cx, "ostg", 2, [128, 4, VD], BF16)
    sm, sm_k = cx.sb("sm", [128, 16], F32)
    rec = Slots(cx, "rec", 4, [128, 2], F32)
    nssq = 0
    if diff:
        bw, bw_k = cx.sb("bw", [128, 2, 640], F32)
        cx.dma("sp", bw[:], aux["bias"], bw_k, (), (bw_k,))
        cx.memset(bw[64:128, :, 0:64], NEG, (bw_k,))
        tmpS = Slots(cx, "tmpS", 2, [128, 512], F32)
        A0, A0_k = cx.sb("A0", [128, 4, 256], F32)
        comb = Slots(cx, "comb", 2, [128, 256], F32)
        junk, junk_k = cx.sb("junk", [128, 256], BF16)
        gs, gs_k = cx.sb("gs", [128, 256], F32)
        cx.dma("sp", gs[:], aux["subg"], gs_k, (), (gs_k,))
        cx.ts(gs[:], gs[:], 1.0 - LAMBDA_INIT, None, ALU.mult, None, (gs_k,), (gs_k,))
        lamb, lamb_k = cx.sb("lamb", [128, 4, 128], F32)
        cx.dma("sp", lamb[:], aux["lam"], lamb_k, (), (lamb_k,))
        lp, lp_k = cx.sb("lp", [128, 2, 128], F32)
        cx.tt(lp[:, 0, :], lamb[:, 0, :], lamb[:, 1, :], ALU.mult, (lamb_k,), (lp_k,))
        cx.tt(lp[:, 1, :], lamb[:, 2, :], lamb[:, 3, :], ALU.mult, (lamb_k,), (lp_k,))
        cx.s.op("dve", lambda e: e.tensor_reduce(sm[:, 0:2], lp[:], AX.X, ALU.add), (lp_k,), (sm_k,))
        cx.act(sm[:, 2:4], sm[:, 0:2], AF.Exp, (sm_k,), (sm_k,))
        cx.tt(sm[:, 4:5], sm[:, 3:4], sm[:, 2:3], ALU.subtract, (sm_k,), (sm_k,))
        cx.ts(sm[:, 5:6], sm[:, 4:5], -LAMBDA_INIT, None, ALU.add, None, (sm_k,), (sm_k,))
        nlam = sm[:, 5:6]
        cx.memset(sm[:, 6:7], SUBEPS, (sm_k,))
        epsb = sm[:, 6:7]
        ssq, ssq_k = cx.sb("ssq", [128, 2 * 32 * 4], F32)
        cx.memset(ssq[:], 0.0, (ssq_k,))
    else:
        tri, tri_k = cx.sb("tri", [128, 128], BF16)
        cx.dma("sp", tri[:], aux["trimask"], tri_k, (), (tri_k,))
        cm, cm_k = cx.sb("cm", [128, 3, 128], F32)
        cx.dma("sp", cm[:], aux["cmats"], cm_k, (), (cm_k,))
        X, X_k = cx.sb("X", [128, NKB, 4], F32)
        for r in range(4):
            cx.dma("sp", X[:, r * 32:(r + 1) * 32, :],
                   aux["logf"][r].rearrange("(kb p) h -> p kb h", p=128), X_k, (), (X_k,))
        pre, pre_k = cx.sb("pre", [128, NKB * 4], F32)
        sA, sA_k = cx.sb("sA", [128, NKB * 4], F32)
        sB, sB_k = cx.sb("sB", [128, NKB * 4], F32)
        cT, cT_k = cx.sb("cT", [128, NKB, 4], F32)
        crefb, crefb_k = cx.sb("crefb", [128, NKB, 4], F32)
        Xf = X[:].rearrange("p k h -> p (k h)")
        pc, pc_k = psS.next()
        cx.mm(pc[:], cm[:, 0, :], Xf, True, True, (cm_k, X_k), (pc_k,))
        cx.copy(pre[:], pc[:], (pc_k,), (pre_k,), eng="dve")
        pc, pc_k = psS.next()
        cx.mm(pc[:], cm[:, 1, :], pre[:], True, True, (cm_k, pre_k), (pc_k,))
        cx.copy(sA[:], pc[:], (pc_k,), (sA_k,), eng="dve")
        cx.tt(pre[:], pre[:], sA[:], ALU.subtract, (pre_k, sA_k), (pre_k,))
        a, ak, b, bk = sA, sA_k, sB, sB_k
        sft = 4
        while sft < NKB * 4:
            cx.copy(b[:, 0:sft], a[:, 0:sft], (ak,), (bk,), eng="dve")
            cx.tt(b[:, sft:], a[:, sft:], a[:, 0:NKB * 4 - sft], ALU.add, (ak,), (bk,))
            a, ak, b, bk = b, bk, a, ak
            sft *= 2
        cx.tt(cT[:].rearrange("p k h -> p (k h)"), pre[:], a[:], ALU.add, (pre_k, ak), (cT_k,))
        pc, pc_k = psS.next()
        cx.mm(pc[:], cm[:, 2, :], cT[:].rearrange("p k h -> p (k h)"), True, True, (cm_k, cT_k), (pc_k,))
        cx.copy(crefb[:].rearrange("p k h -> p (k h)"), pc[:], (pc_k,), (crefb_k,), eng="dve")
        bcol = Slots(cx, "bcol", 3, [128, 2, NKB], F32)

    for hl in range(nheads):
        vt, vk = Vt.next()
        vk = vkeys[id(vk)]
        for r in range(4):
            for cl in range(4):
                src = v_d[r, cl].rearrange("(kb p) v -> p kb v", p=128)
                cx.dma("sp", vt[:, r * 32 + cl * 8:r * 32 + cl * 8 + 8, 0:VD],
                       src[:, :, hl * VD:(hl + 1) * VD], vk[r * 4 + cl], (), (vk[r * 4 + cl],))
        kts = []
        for c in range(nmaps):
            u = hl * nmaps + c
            kt, kk = KT.next()
            kk = kkeys[id(kk)]
            for r in range(4):
                for rt in range(4):
                    cx.dma("pool", kt[:, r * 4096 + rt * 1024:r * 4096 + (rt + 1) * 1024], kT_d[rt, r, u],
                           kk[r * 4 + rt], (), (kk[r * 4 + rt],))
            kts.append((kt, kk))
        tasks = []
        for qt in range(S // 512):
            for c in range(nmaps):
                kb, nfar = 0, (max(0, 4 * qt - 1) if (diff and PAIR) else 0)
                while kb < 4 * qt + 4:
                    if kb + 1 < nfar:
                        tasks.append((qt, c, (kb, kb + 1)))
                        kb += 2
                    else:
                        tasks.append((qt, c, (kb,)))
                        kb += 1
        qtiles, ptiles, groups, ostage, bcols = {}, {}, {}, {}, {}

        def stage_s(i):
            nonlocal nssq
            qt, c, kbs = tasks[i]
            if qt not in qtiles:
                r, tq = qt // 8, (qt % 8) * 512
                q, qk = QT.next()
                for cc in range(nmaps):
                    cx.dma("sp", q[:, cc, :], qT_d[tq // 1024, r, hl * nmaps + cc, :, tq % 1024:tq % 1024 + 512],
                           qk, (), (qk,))
                qtiles[qt] = (q, qk)
                if not diff:
                    bc, bc_k = bcol.next()
                    nkb = 4 * qt + 4
                    for qh in range(FOXH):
                        refblk = 4 * qt + (2 if FOXH == 1 else 1 + 2 * qh)
                        cx.ts(bc[:, qh, 0:nkb], cT[:, 0:nkb, hl], -1.0, crefb[:, refblk, hl:hl + 1],
                              ALU.mult, ALU.add, (cT_k, crefb_k), (bc_k,))
                    bcols[qt] = (bc, bc_k)
            q, qk = qtiles[qt]
            kt, kk = kts[c]
            ps, ps_k = psS.next()
            p, pk = PT.next()
            for j, kb in enumerate(kbs):
                t = kb - 4 * qt
                ql = max(0, 128 * t)
                cx.mm(ps[:, j * 512 + ql:(j + 1) * 512], kt[:, kb * 128:(kb + 1) * 128], q[:, c, ql:512],
                      True, True, (kk[kb // 8], qk), (ps_k,))
            if diff:
                if t <= -2:
                    w_ = 512 * len(kbs)
                    cx.act(p[:, 0:w_], ps[:, 0:w_], AF.Exp, (ps_k, bw_k), (pk,), bias=bw[:, hl, 639:640], scale=SCALE)
                else:
                    tm, tmk = tmpS.next()
                    cx.stt(tm[:, ql:512], ps[:, ql:512], SCALE, bw[:, hl, ql - 128 * t:512 - 128 * t],
                           ALU.mult, ALU.add, (ps_k, bw_k), (tmk,))
                    cx.act(p[:, ql:512], tm[:, ql:512], AF.Exp, (tmk,), (pk,))
            else:
                bc, bc_k = bcols[qt]
                for qh in range(FOXH):
                    w_ = 512 // FOXH
                    lo, hi = max(ql, w_ * qh), w_ * (qh + 1)
                    if lo < hi:
                        cx.act(p[:, lo:hi], ps[:, lo:hi], AF.Exp, (ps_k, bc_k), (pk,),
                               bias=bc[:, qh, kb:kb + 1], scale=SCALE)
                if t >= 0:
                    cx.tt(p[:, ql:ql + 128], p[:, ql:ql + 128], tri[:], ALU.mult, (pk, tri_k), (pk,))
            ptiles[i] = (p, pk)

        def stage_av(i):
            nonlocal nssq
            qt, c, kbs = tasks[i]
            p, pk = ptiles.pop(i)
            if (qt, c) not in groups:
                groups[(qt, c)] = [psO.next() for _ in range(4)]
            Os = groups[(qt, c)]
            if qt not in ostage:
                ostage[qt] = ostg.next()
            os_, os_k = ostage[qt]
            for j, kb in enumerate(kbs):
                t = kb - 4 * qt
                for qs in range(max(t, 0), 4):
                    O, Ok = Os[qs]
                    cx.mm(O[:, 0:VD + 1], p[:, j * 512 + qs * 128:j * 512 + (qs + 1) * 128], vt[:, kb, :],
                          kb == 0, kb == 4 * qt + qs, (pk, vk[kb // 8]), (Ok,))
            if kb != 4 * qt + 3:
                return
            for qs in range(4):
                O, Ok = Os[qs]
                rc, rck = rec.next()
                cx.s.op("dve", lambda e, rc=rc, O=O: e.reciprocal(rc[:, 0:1], O[:, VD:VD + 1]), (Ok,), (rck,))
                if not diff:
                    cx.ts(os_[:, qs, :], O[:, 0:VD], rc[:, 0:1], None, ALU.mult, None, (Ok, rck), (os_k,))
                elif c == 0:
                    cx.ts(A0[:, qs, :], O[:, 0:VD], rc[:, 0:1], None, ALU.mult, None, (Ok, rck), (A0_k,))
                else:
                    cx.tt(rc[:, 1:2], rc[:, 0:1], nlam, ALU.mult, (rck, sm_k), (rck,))
                    cb_, cbk = comb.next()
                    cx.stt(cb_[:], O[:, 0:VD], rc[:, 1:2], A0[:, qs, :], ALU.mult, ALU.add,
                           (Ok, rck, A0_k), (cbk,))
                    j = nssq
                    nssq += 1
                    cx.act(junk[:], cb_[:], AF.Square, (cbk,), (junk_k, ssq_k), accum=ssq[:, j:j + 1])
                    cx.act(ssq[:, j:j + 1], ssq[:, j:j + 1], AF.Ln, (ssq_k, sm_k), (ssq_k,),
                           bias=epsb, scale=1.0 / 256)
                    cx.act(ssq[:, j:j + 1], ssq[:, j:j + 1], AF.Exp, (ssq_k,), (ssq_k,), scale=-0.5)
                    cx.stt(os_[:, qs, :], cb_[:], ssq[:, j:j + 1], gs[:], ALU.mult, ALU.mult,
                           (cbk, ssq_k, gs_k), (os_k,))
            if c == nmaps - 1:
                ci = qt // 2
                rds = (os_k,) if xo is None else (os_k, xo.tk(ci // 4, ci % 4))
                cx.dma("sp", o_d[qt * 512:(qt + 1) * 512, hl * VD:(hl + 1) * VD].rearrange("(qs p) v -> p qs v", p=128),
                       os_[:], os_k, rds, ())
                if xo is not None and hl == nheads - 1 and qt % 2 == 1:
                    xo.issue(cx, cck, (ci % 4,), gs=(ci // 4,))
                del qtiles[qt]

        LOOK = 1 if (diff and PAIR) else 3
        for i in range(min(LOOK, len(tasks))):
            stage_s(i)
        for i in range(len(tasks)):
            if i + LOOK < len(tasks):
                stage_s(i + LOOK)
            stage_av(i)
    cx.s.emit()


TT = 1024


def row_phase(nc, es, layer, x_d, o_d, ident_d, wo_d, gm_d, win_d, wout_d, ex, gsem=None, tag=""):
    cx = Ctx(nc, es, gsem, tag)
    rk = RowKit(cx, ident_d, nss=128, wslots=2, hnslots=2)
    h, _ = cx.sb("h", [128, 8, 2048], F32)
    hk = [Tk(f"h{i}") for i in range(8)]
    aT, aT_k = cx.sb("aT", [128, 16, TT], BF16)
    hid = Slots(cx, "hid", 2, [128, 4, TT], BF16)
    g, g_k = cx.sb("g", [128, 2048], F32)
    obs = Slots(cx, "ob", 2, [128, 2048], BF16)
    rr = Slots(cx, "rr", 2, [128, 512], BF16)
    ost = Slots(cx, "ost", 4, [128, 512], BF16)
    if layer == 0:
        wf, wf_k = cx.sb("wf", [128, 16, 16], BF16)
        cx.dma("pool", wf[:], ex["wf"].rearrange("(kc p) n -> p kc n", p=128), wf_k, (), (wf_k,))
        bfb, bfb_k = cx.sb("bfb", [128, 16], F32)
        cx.dma("sp", bfb[:], ex["bf"], bfb_k, (), (bfb_k,))
        zs = Slots(cx, "zs", 2, [128, 16], F32)
        one, one_k = cx.sb("one", [128, 1], F32)
        cx.memset(one[:], 1.0, (one_k,))
    else:
        fin = Slots(cx, "fin", 1, [128, 2048], F32)

    cck = Tk("cc")
    xis_ = ex.get("xis") or (None, None, None)
    for rt in range(TOK // TT):
        r0 = rt * TT
        for tb in range(8):
            cx.dma("sp", h[:, tb, :], x_d[r0 + tb * 128:r0 + (tb + 1) * 128, :], hk[tb], (), (hk[tb],))
        for tb in range(8):
            ob, ob_k = obs.next()
            rw = r0 + tb * 128
            cx.dma("sp", ob[:, :].rearrange("p (g v) -> p g v", g=4),
                   o_d[:, rw // 1024, rw % 1024:rw % 1024 + 128, :].rearrange("g p v -> p g v"), ob_k, (), (ob_k,))
            rk.transpose_in(ob, ob_k, aT, aT_k, tb * 128)
        for wb in range(4):
            wt, wk = rk.load_w(wo_d, wb * 512, 512)
            if layer == 0 and wb == 1 and rt > 0 and ex.get("xis") is not None:
                for xi in ex["xis"]:
                    xi.issue(cx, cck, (rt - 1,), defer=rk)

            def cb(tb, pa, pa_k, wb=wb):
                hs = h[:, tb, wb * 512:(wb + 1) * 512]
                cx.tt(hs, hs, pa[:], ALU.add, (pa_k, hk[tb]), (hk[tb],))
            rk.lin_tm(aT, aT_k, TT, wt, wk, 512, cb)
        cx.dma("sp", g[:], gm_d, g_k, (), (g_k,))
        for tb in range(8):
            rk.norm_T(h[:, tb, :], hk[tb], g, g_k, aT, aT_k, tb * 128)
        for hb in range(DFF // 512):
            wt, wk = rk.load_w(win_d, hb * 512, 512)
            hd, hd_k = hid.next()

            def cb(oc, tt, pa, pa_k, hd=hd, hd_k=hd_k):
                r_, rk_ = rr.next()
                cx.act(r_[:], pa[:], AF.Relu, (pa_k,), (rk_,))
                cx.tt(hd[:, oc, tt * 512:(tt + 1) * 512], r_[:], r_[:], ALU.mult, (rk_,), (hd_k,))
            rk.lin_fm(aT, aT_k, TT, wt, wk, 512, cb)
            wt, wk = rk.load_w(wout_d, 0, 2048, nk=4, k0=hb * 4)
            wv = wt
            for tb in range(8):
                for cg in range(4):
                    pa, pa_k = rk.pacc.next()
                    for kc in range(4):
                        cx.mm(pa[:], hd[:, kc, tb * 128:(tb + 1) * 128], wv[:, kc, cg * 512:(cg + 1) * 512],
                              kc == 0, kc == 3, (hd_k, wk.for_kc(kc)), (pa_k,))
                    hs = h[:, tb, cg * 512:(cg + 1) * 512]
                    cx.tt(hs, hs, pa[:], ALU.add, (pa_k, hk[tb]), (hk[tb],))
        if layer == 0:
            for tb in range(8):
                cx.dma("sp", ex["h_out"][r0 + tb * 128:r0 + (tb + 1) * 128, :], h[:, tb, :], hk[tb], (hk[tb],), ())
            cx.dma("sp", g[:], ex["kvg"], g_k, (), (g_k,))
            for tb in range(8):
                rk.norm_T(h[:, tb, :], hk[tb], g, g_k, aT, aT_k, tb * 128)
            for tb in range(8):
                pa, pa_k = rk.pacc.next()
                for kc in range(16):
                    cx.mm(pa[:, 0:16], aT[:, kc, tb * 128:(tb + 1) * 128], wf[:, kc, :], kc == 0, kc == 15,
                          (aT_k, wf_k), (pa_k,))
                z, zk = zs.next()
                cx.tt(z[:], pa[:, 0:16], bfb[:], ALU.add, (pa_k, bfb_k), (zk,))
                cx.act(z[:], z[:], AF.Exp, (zk,), (zk,), scale=-1.0)
                cx.act(z[:], z[:], AF.Ln, (zk, one_k), (zk,), bias=one[:, 0:1])
                cx.ts(z[:], z[:], -1.0, None, ALU.mult, None, (zk,), (zk,))
                for gg in range(4):
                    cx.dma("sp", ex["logf"][gg, r0 + tb * 128:r0 + (tb + 1) * 128, :], z[:, gg * 4:(gg + 1) * 4],
                           zk, (zk,), ())
            def early_kv(si, wb, rt=rt):
                if rt == 3 and xis_[0] is not None and si == 1 and wb == 1:
                    xis_[1].issue(cx, cck, (3,), defer=rk)

            def early_q(si, wb, rt=rt):
                if rt == 3 and xis_[0] is not None and wb == 1:
                    xis_[2].issue(cx, cck, (3,), defer=rk)
            proj_qkv(cx, rk, aT, aT_k, TT, r0, ost,
                     [("fm", ex["wk"], 0, ex["k2T"], xis_[1]), ("tm", ex["wv"], 0, ex["v2"], xis_[2])], early_kv)
            cx.dma("sp", g[:], ex["qg"], g_k, (), (g_k,))
            for tb in range(8):
                rk.norm_T(h[:, tb, :], hk[tb], g, g_k, aT, aT_k, tb * 128)
            proj_qkv(cx, rk, aT, aT_k, TT, r0, ost, [("fm", ex["wq"], 0, ex["q2T"], xis_[0])], early_q)
        else:
            cx.dma("sp", g[:], ex["fg"], g_k, (), (g_k,))
            for tb in range(8):
                f_, fk = fin.next()
                rs = rk.norm_stats(h[:, tb, :], hk[tb])
                cx.stt(f_[:], h[:, tb, :], rs, g[:], ALU.mult, ALU.mult, (hk[tb], rk.rs_k, g_k), (fk,))
                cx.dma("sp", ex["out"][r0 + tb * 128:r0 + (tb + 1) * 128, :], f_[:], fk, (fk,), ())
    rk.drain()
    cx.s.emit()


def _t5_bucket_np(rel):
    half, max_exact = 16, 8
    ret = np.where(rel > 0, half, 0)
    n = np.abs(rel)
    nf = np.maximum(n, 1).astype(np.float32)
    large = max_exact + (np.log(nf / np.float32(max_exact)) / np.float32(math.log(128 / max_exact))
                         * np.float32(half - max_exact)).astype(np.int32)
    large = np.minimum(large, half - 1)
    return ret + np.where(n < max_exact, n, large)


def _bc(v, n=128):
    v = np.asarray(v, np.float32).reshape(1, -1)
    return np.ascontiguousarray(np.broadcast_to(v, (n, v.shape[1])))


def _a2a(outs, key, b):
    return [np.ascontiguousarray(np.stack([outs[b * 4 + r][key][g] for r in range(4)])) for g in range(4)]


def _launch(build, in_maps):
    nc = bass.Bass("TRN2", target_bir_lowering=False)
    with ExitStack() as es:
        build(nc, es)
    res = run_bass_kernel_spmd(nc, in_maps, core_ids=list(range(NCORE)))
    return res.results


def _din(nc, name, shape, dt=F32):
    return nc.dram_tensor(name, list(shape), dt, kind="ExternalInput").ap()


def _dout(nc, name, shape, dt=F32):
    return nc.dram_tensor(name, list(shape), dt, kind="ExternalOutput").ap()


IDENT = np.eye(128, dtype=np.float32).astype(ml_dtypes.bfloat16)


def build_a(nc, es):
    phase_a(nc, es, _din(nc, "x", [TOK, D]), _din(nc, "g", [128, D]), _din(nc, "wqkv", [D, 3 * D]),
            _din(nc, "ident", [128, 128], BF16), _dout(nc, "qT", [16, 128, TOK], BF16),
            _dout(nc, "kT", [16, 128, TOK], BF16), _dout(nc, "v", [4, TOK, 512], BF16))


def build_attn(mode):
    def build(nc, es):
        aux = {}
        if mode == "diff":
            aux["bias"] = _din(nc, "bias", [128, 2, 640])
            aux["subg"] = _din(nc, "subg", [128, 256])
            aux["lam"] = _din(nc, "lam", [128, 4, 128])
        else:
            aux["trimask"] = _din(nc, "trimask", [128, 128], BF16)
            aux["cmats"] = _din(nc, "cmats", [128, 3, 128])
            aux["logf"] = _din(nc, "logf", [4, TOK, 4])
        attention(nc, es, mode, _din(nc, "qT", [4, 4, 128, TOK], BF16), _din(nc, "kT", [4, 4, 128, TOK], BF16),
                  _din(nc, "v", [4, TOK, 512], BF16), _dout(nc, "o", [S, 512], BF16), aux)
    return build


def build_row(layer):
    def build(nc, es):
        ex = {}
        if layer == 0:
            for nm in ("kvg", "qg"):
                ex[nm] = _din(nc, nm, [128, D])
            for nm in ("wk", "wv", "wq"):
                ex[nm] = _din(nc, nm, [D, D])
            ex["wf"] = _din(nc, "wf", [D, 16])
            ex["bf"] = _din(nc, "bf", [128, 16])
            ex["h_out"] = _dout(nc, "h_out", [TOK, D])
            ex["q2T"] = _dout(nc, "q2T", [16, 128, TOK], BF16)
            ex["k2T"] = _dout(nc, "k2T", [16, 128, TOK], BF16)
            ex["v2"] = _dout(nc, "v2", [4, TOK, 512], BF16)
            ex["logf"] = _dout(nc, "logf", [4, TOK, 4])
        else:
            ex["fg"] = _din(nc, "fg", [128, D])
            ex["out"] = _dout(nc, "out", [TOK, D])
        row_phase(nc, es, layer, _din(nc, "x", [TOK, D]), _din(nc, "o", [4, TOK, 512], BF16),
                  _din(nc, "ident", [128, 128], BF16), _din(nc, "wo", [D, D]), _din(nc, "gm", [128, D]),
                  _din(nc, "win", [D, DFF]), _din(nc, "wout", [DFF, D]), ex)
    return build


I32 = mybir.dt.int32


class SemPool:
    def __init__(self, stack):
        self.stack = stack
        self.free = []
        self.regs = []


GROUPS = [[0, 1, 2, 3], [4, 5, 6, 7]]


PAIR = False
FOXH = 2
CHK = 524288


def comm_phase(nc, es, gsem, tag, cid_d, items):
    cx = Ctx(nc, es, gsem, tag)
    cid, cid_k = cx.sb("cid", [1, 8], I32)
    cx.dma("sp", cid[:], cid_d, cid_k, (), (cid_k,))
    if not gsem.regs:
        gsem.regs = [gsem.stack.enter_context(nc.gpsimd.register(f"cr{i}")) for i in range(6)]
        for i in range(6):
            o = cx.s.op("pool", lambda e, i=i: e.reg_load(gsem.regs[i], cid[:1, i:i + 1]), (cid_k,), ())
            o.nosig = True
    regs = gsem.regs
    cck = Tk("cc")
    inner = [[8192, CHK // 8192], [1, 8192]]
    for n, it in enumerate(items):
        gk, dk = Tk(f"g{n}"), Tk(f"l{n}")
        if isinstance(it, XItem):
            it.tks = {}
            it.issue(cx, cck, (0, 1, 2, 3))
            gall, dst = it.gall, it.dst
            cx.s.op("pool", lambda e, dst=dst, gall=gall: e.dma_start(
                out=bass.AP(dst, 0, [[32768, 16 * CHK // 32768], [1, 32768]]),
                in_=bass.AP(gall, regs[0], [[32768, 16 * CHK // 32768], [1, 32768]])),
                tuple(it.tks.values()), (dk,), dma=dk)
            it.reset()
        else:
            _, src, gath, dst, ri, rstride, nel = it
            cx.s.op("pool", lambda e, src=src, gath=gath: e.collective_compute(
                "AllGather", ALU.bypass, replica_groups=GROUPS, ins=[src.ap().opt()], outs=[gath.ap().opt()]),
                (), (gk,), dma=cck, inc=1)
            cx.s.op("pool", lambda e, dst=dst, gath=gath, ri=ri, rstride=rstride, nel=nel: e.dma_start(
                out=bass.AP(dst, 0, [[nel, 4], [1, nel]]), in_=bass.AP(gath, regs[ri], [[rstride, 4], [1, nel]])),
                (gk,), (dk,), dma=dk)
    cx.s.emit()


def build_fused(nc, upto=9):
    di = lambda name, shape, dt=F32: nc.dram_tensor(name, list(shape), dt, kind="ExternalInput").ap()
    x_d = di("x", [TOK, D])
    ident_d = di("ident", [128, 128], BF16)
    cid_d = di("cid", [1, 8], I32)
    g_attn0, g_attn1, g_mlp0, g_mlp1, g_kv, g_fin = [di(n, [128, D]) for n in
                                                     ("g_attn0", "g_attn1", "g_mlp0", "g_mlp1", "g_kv", "g_fin")]
    wqkv = di("wqkv", [D, 3 * D])
    wo0, wo1, wk, wv, wq = [di(n, [D, D]) for n in ("wo0", "wo1", "wk", "wv", "wq")]
    win0, win1 = di("win0", [D, DFF]), di("win1", [D, DFF])
    wout0, wout1 = di("wout0", [DFF, D]), di("wout1", [DFF, D])
    wf, bf = di("wf", [D, 16]), di("bf", [128, 16])
    aux0 = {"bias": di("bias", [128, 2, 640]), "subg": di("subg", [128, 256]), "lam": di("lam", [128, 4, 128])}
    trimask, cmats = di("trimask", [128, 128], BF16), di("cmats", [128, 3, 128])
    out_d = nc.dram_tensor("out", [TOK, D], F32, kind="ExternalOutput").ap()
    dt_ = lambda name, shape, dt=BF16: nc.dram_tensor(name, list(shape), dt)
    qT, kT, v = dt_("i_qT", [8192, 1024]), dt_("i_kT", [8192, 1024]), dt_("i_v", [4 * TOK, 512])
    Gq, Gk, Gv = dt_("i_Gq", [8192, TOK]), dt_("i_Gk", [8192, TOK]), dt_("i_Gv", [16 * TOK, 512])
    Lq, Lk, Lv = dt_("i_Lq", [8192, 1024]), dt_("i_Lk", [8192, 1024]), dt_("i_Lv", [4 * TOK, 512])
    o, Lo = dt_("i_o", [S, 512]), dt_("i_Lo", [4 * TOK, 512])
    h1 = dt_("i_h1", [TOK, D], F32)
    logf, Glogf, Llogf = dt_("i_logf", [4 * TOK, 4], F32), dt_("i_Glogf", [16 * TOK, 4], F32), dt_("i_Llogf", [4 * TOK, 4], F32)
    u3 = lambda t: t.ap().rearrange("(g rt ul d) t -> g rt ul d t", g=4, rt=4, ul=4)
    r4 = lambda t: t.ap().rearrange("(rt r ul d) t -> rt r ul d t", rt=4, r=4, ul=4)
    xq, xk_, xv = XItem(qT, Gq, Lq), XItem(kT, Gk, Lk), XItem(v, Gv, Lv)
    xo = XItem(o, Gv, Lo)
    c4 = lambda t: t.ap().rearrange("(c r i) v -> r c i v", c=4, r=4)
    g3 = lambda t: t.ap().rearrange("(g t) v -> g t v", g=4)
    LG = ("small", logf, Glogf, Llogf, 1, 4 * TOK * 4, TOK * 4)
    with ExitStack() as gstack:
        gsem = SemPool(gstack)
        if upto > 0:
            with ExitStack() as es:
                phase_a(nc, es, x_d, g_attn0, wqkv, ident_d, u3(qT), u3(kT), g3(v), gsem, "A_", (xq, xk_, xv))
        if upto > 1:
            with ExitStack() as es:
                comm_phase(nc, es, gsem, "X1_", cid_d, [xq, xk_, xv])
        if upto > 2:
            with ExitStack() as es:
                attention(nc, es, "diff", r4(Lq), r4(Lk), c4(Lv), o.ap(), aux0, gsem, "B1_", xo)
        if upto > 3:
            with ExitStack() as es:
                comm_phase(nc, es, gsem, "X2_", cid_d, [xo])
        if upto > 4:
            with ExitStack() as es:
                ex = {"kvg": g_kv, "qg": g_attn1, "wk": wk, "wv": wv, "wq": wq, "wf": wf, "bf": bf,
                      "h_out": h1.ap(), "q2T": u3(qT), "k2T": u3(kT), "v2": g3(v), "logf": g3(logf), "xis": (xq, xk_, xv)}
                row_phase(nc, es, 0, x_d, c4(Lo), ident_d, wo0, g_mlp0, win0, wout0, ex, gsem, "B2_")
        if upto > 5:
            with ExitStack() as es:
                comm_phase(nc, es, gsem, "X3_", cid_d, [xq, xk_, xv, LG])
        if upto > 6:
            with ExitStack() as es:
                aux1 = {"trimask": trimask, "cmats": cmats, "logf": g3(Llogf)}
                attention(nc, es, "fox", r4(Lq), r4(Lk), c4(Lv), o.ap(), aux1, gsem, "C1_", xo)
        if upto > 7:
            with ExitStack() as es:
                comm_phase(nc, es, gsem, "X4_", cid_d, [xo])
        if upto > 8:
            with ExitStack() as es:
                row_phase(nc, es, 1, h1.ap(), c4(Lo), ident_d, wo1, g_mlp1, win1, wout1,
                          {"fg": g_fin, "out": out_d}, gsem, "C2_")
        if upto < 9:
            with ExitStack() as es:
                cx = Ctx(nc, es, gsem, "Z_")
                k = Tk("z")
                cx.dma("sp", out_d, x_d, k, (), (k,))
                cx.s.emit()


def kernel(x, rel_bias_table, attn_norm_g, mlp_norm_g, w_qkv_a, lam_q1, lam_k1, lam_q2, lam_k2,
           subln_g, w_o_a, kv_norm_g, w_k_b, w_v_b, w_f_b, b_f_b, w_q_b, w_o_b, w_mlp_in, w_mlp_out,
           final_norm_g, _upto=9):
    f32 = lambda a: np.ascontiguousarray(np.asarray(a, np.float32))
    x = f32(x)
    kk = np.arange(128)[:, None]
    jj = np.arange(640)[None, :]
    bias_all = f32(rel_bias_table)[_t5_bucket_np(kk - jj)]
    tri = (np.arange(128)[None, :] >= np.arange(128)[:, None]).astype(np.float32)
    cm = np.zeros((128, 3, 128), np.float32)
    cm[:, 0, :] = tri
    cm[127, 1, :] = 1.0
    cm[0, 2, :] = 1.0
    shared = {
        "ident": IDENT, "g_attn0": _bc(attn_norm_g[0]), "g_attn1": _bc(attn_norm_g[1]),
        "g_mlp0": _bc(mlp_norm_g[0]), "g_mlp1": _bc(mlp_norm_g[1]), "g_kv": _bc(kv_norm_g),
        "g_fin": _bc(final_norm_g), "wqkv": f32(w_qkv_a[0]), "wo0": f32(w_o_a[0]), "wo1": f32(w_o_b[0]),
        "wk": f32(w_k_b), "wv": f32(w_v_b), "wq": f32(w_q_b[0]), "win0": f32(w_mlp_in[0]),
        "win1": f32(w_mlp_in[1]), "wout0": f32(w_mlp_out[0]), "wout1": f32(w_mlp_out[1]),
        "wf": f32(w_f_b), "bf": _bc(b_f_b), "subg": _bc(subln_g[0]),
        "lam": np.ascontiguousarray(np.broadcast_to(
            np.stack([f32(lam_q1[0]), f32(lam_k1[0]), f32(lam_q2[0]), f32(lam_k2[0])])[None], (128, 4, 128))),
        "trimask": tri.astype(ml_dtypes.bfloat16), "cmats": cm,
    }
    in_maps = []
    for c in range(NCORE):
        b, g = c // 4, c % 4
        m = dict(shared)
        m["x"] = np.ascontiguousarray(x[b, g * TOK:(g + 1) * TOK])
        m["bias"] = np.ascontiguousarray(bias_all[:, :, 2 * g:2 * g + 2].transpose(0, 2, 1))
        cid = np.zeros((1, 8), np.int32)
        cid[0, 0] = g * 16 * CHK
        for r in range(4):
            cid[0, 2 + r] = g * 16 * CHK + r * CHK
        cid[0, 1] = g * TOK * 4
        m["cid"] = cid
        in_maps.append(m)
    nc = bass.Bass("TRN2", target_bir_lowering=False)
    build_fused(nc, _upto)
    res = run_bass_kernel_spmd(nc, in_maps, core_ids=list(range(NCORE))).results
    out = np.empty((NB, S, D), np.float32)
    for c in range(NCORE):
        out[c // 4, (c % 4) * TOK:(c % 4 + 1) * TOK] = res[c]["out"]
    return out
```
